# Optimizing a Trainium2 kernel written in Bass

```python
import jax, jax.numpy as jnp
from jax import lax
import numpy as np

D_MODEL = 2048
BATCH = 16
SEQ = 256
DEPTH = 2
DEC_BATCH = 4
DEC_SEQ = 1024
PAST_LEN = 512

GRID_W = 64
CONV_WIDTH = 31
CONV_CH = D_MODEL // 2
N_HEADS_B = 8
HEAD_DK = 128
HEAD_DV = 128
KEY_DIM = N_HEADS_B * HEAD_DK
VAL_DIM = N_HEADS_B * HEAD_DV
SHORT_CONV = 3
CHUNK = 64
EPS = 1e-6
IN_SPLITS = (CONV_CH, CONV_CH, CONV_CH, KEY_DIM, KEY_DIM, VAL_DIM, VAL_DIM,
             2 * N_HEADS_B, 2 * N_HEADS_B, D_MODEL, D_MODEL)
IN_WIDTH = 3 * CONV_CH + 2 * KEY_DIM + 2 * VAL_DIM + 4 * N_HEADS_B + 2 * D_MODEL

kernel_name = "hybrid_conformer_gdn_diffusion_step"


def _rmsnorm(x, g):
    xf = x.astype(jnp.float32)
    y = xf * lax.rsqrt(jnp.mean(xf * xf, axis=-1, keepdims=True) + EPS)
    return (y * g.astype(jnp.float32)).astype(x.dtype)


def _layernorm(x, g, b):
    xf = x.astype(jnp.float32)
    mu = jnp.mean(xf, axis=-1, keepdims=True)
    var = jnp.mean(jnp.square(xf - mu), axis=-1, keepdims=True)
    y = (xf - mu) * lax.rsqrt(var + EPS)
    return (y * g.astype(jnp.float32) + b.astype(jnp.float32)).astype(x.dtype)


def _l2norm(x):
    return x * lax.rsqrt(jnp.sum(x * x, axis=-1, keepdims=True) + EPS)


def _dwconv(x, w):
    k = w.shape[0]
    return lax.conv_general_dilated(
        x, w[:, None, :].astype(x.dtype), window_strides=(1,),
        padding=[(k // 2, k // 2)], dimension_numbers=('NWC', 'WIO', 'NWC'),
        feature_group_count=x.shape[-1])


def _axial_dwconv(u, w):
    b, n, ch = u.shape
    rows = n // GRID_W
    half = ch // 2
    grid = u.reshape(b, rows, GRID_W, ch)
    horiz = _dwconv(grid[..., :half].reshape(b * rows, GRID_W, half), w[:, :half])
    horiz = horiz.reshape(b, rows, GRID_W, half)
    vert = grid[..., half:].transpose(0, 2, 1, 3).reshape(b * GRID_W, rows, ch - half)
    vert = _dwconv(vert, w[:, half:]).reshape(b, GRID_W, rows, ch - half).transpose(0, 2, 1, 3)
    return jnp.concatenate([horiz, vert], axis=-1).reshape(b, n, ch)


def _chunk_gated_delta(q, k, v, g, beta, s0):
    b, n_tok, h, dk = q.shape
    dv = v.shape[-1]
    nc = n_tok // CHUNK

    def blk(t):
        t = jnp.moveaxis(t, 2, 1)
        return t.reshape(b, h, nc, CHUNK, *t.shape[3:])

    q, k, v, g, beta = blk(q), blk(k), blk(v), blk(g), blk(beta)
    g = jnp.cumsum(g, axis=-1)
    tri = jnp.tril(jnp.ones((CHUNK, CHUNK), dtype=bool))
    strict = jnp.tril(jnp.ones((CHUNK, CHUNK), dtype=bool), -1)
    decay = jnp.exp(jnp.where(tri, g[..., :, None] - g[..., None, :], -jnp.inf))
    kb = k * beta[..., None]
    a = jnp.einsum('bhnid,bhnjd->bhnij', kb, k) * decay
    a = jnp.where(strict, a, 0.0) + jnp.eye(CHUNK, dtype=a.dtype)
    u = lax.linalg.triangular_solve(a, v * beta[..., None], left_side=True, lower=True,
                                    unit_diagonal=True)
    w = lax.linalg.triangular_solve(a, kb * jnp.exp(g)[..., None], left_side=True,
                                    lower=True, unit_diagonal=True)
    qk = jnp.einsum('bhnid,bhnjd->bhnij', q, k) * decay
    q_dec = q * jnp.exp(g)[..., None]
    g_last = g[..., -1]
    k_dec = k * jnp.exp(g_last[..., None] - g)[..., None]

    def step(s, xs):
        u_c, w_c, qk_c, qd_c, kd_c, gl_c = xs
        v_new = u_c - jnp.einsum('bhcd,bhde->bhce', w_c, s)
        o_c = jnp.einsum('bhcd,bhde->bhce', qd_c, s) + jnp.einsum('bhij,bhje->bhie', qk_c, v_new)
        s = s * jnp.exp(gl_c)[..., None, None] + jnp.einsum('bhcd,bhce->bhde', kd_c, v_new)
        return s, o_c

    xs = tuple(jnp.moveaxis(t, 2, 0) for t in (u, w, qk, q_dec, k_dec, g_last))
    s_fin, o = lax.scan(step, s0, xs)
    o = o.transpose(1, 0, 3, 2, 4).reshape(b, n_tok, h, dv)
    return o, s_fin


def _layer(x, cvec, s0, latent, w_mod, b_mod, norm_g, w_in, conv_a_w, conv_a_b, ln_a_g,
           ln_a_b, w_pa, conv_qkv_w, a_log, dt_bias, head_norm_g, w_pb, w_o):
    b, n, _ = x.shape
    mod = (jax.nn.silu(cvec) @ w_mod + b_mod)[:, None, :]
    shift, scale, gate = jnp.split(mod, 3, axis=-1)
    hn = _rmsnorm(x, norm_g) * (1.0 + scale) + shift
    proj = hn @ w_in
    idx = np.cumsum(IN_SPLITS)[:-1].tolist()
    (a_val, a_glu, a_z, q, k, v, z_b, beta_l, alpha_l, gate_a, gate_b) = jnp.split(proj, idx, axis=-1)

    ua = a_val * jax.nn.sigmoid(a_glu)
    ua = _axial_dwconv(ua, conv_a_w) if latent else _dwconv(ua, conv_a_w)
    ua = jax.nn.silu(_layernorm(ua + conv_a_b, ln_a_g, ln_a_b))
    out_a = (ua * jax.nn.silu(a_z)) @ w_pa

    qkv = jax.nn.silu(_dwconv(jnp.concatenate([q, k, v], axis=-1), conv_qkv_w))
    qkv = qkv.astype(jnp.float32)
    qf, kf, vf = jnp.split(qkv, [KEY_DIM, 2 * KEY_DIM], axis=-1)
    qf = _l2norm(qf.reshape(b, n, N_HEADS_B, HEAD_DK)) * (HEAD_DK ** -0.5)
    kf = _l2norm(kf.reshape(b, n, N_HEADS_B, HEAD_DK))
    vf = vf.reshape(b, n, N_HEADS_B, HEAD_DV)
    beta = jax.nn.sigmoid(beta_l.astype(jnp.float32)).reshape(b, n, 2, N_HEADS_B)
    g = -jnp.exp(a_log.astype(jnp.float32)) * jax.nn.softplus(
        alpha_l.astype(jnp.float32).reshape(b, n, 2, N_HEADS_B) + dt_bias.astype(jnp.float32))
    o_f, s_f = _chunk_gated_delta(qf, kf, vf, g[:, :, 0], beta[:, :, 0], s0[:, 0])
    flip = lambda t: jnp.flip(t, axis=1)
    o_b, s_b = _chunk_gated_delta(flip(qf), flip(kf), flip(vf), flip(g[:, :, 1]),
                                  flip(beta[:, :, 1]), s0[:, 1])
    o = (o_f + flip(o_b)).astype(x.dtype)
    o = _rmsnorm(o, head_norm_g).reshape(b, n, VAL_DIM)
    out_b = (o * jax.nn.silu(z_b)) @ w_pb

    merged = jax.nn.sigmoid(gate_a) * out_a + jax.nn.sigmoid(gate_b) * out_b
    x = x + gate * (merged @ w_o)
    return x, jnp.stack([s_f, s_b], axis=1)


def setup_inputs(seed: int = 0) -> dict:
    key = jax.random.key(seed)
    ks = jax.random.split(key, 24)
    f32 = jnp.float32
    nrm = lambda k, shape, s: jax.random.normal(k, shape, f32) * s
    dt = jnp.exp(jax.random.uniform(ks[15], (DEPTH, 2, N_HEADS_B), f32,
                                    np.log(1e-3), np.log(1e-1)))
    return {
        "x_prompt": nrm(ks[0], (BATCH, SEQ, D_MODEL), 1.0),
        "x_sample": nrm(ks[1], (DEC_BATCH, DEC_SEQ, D_MODEL), 1.0),
        "state_delta": nrm(ks[2], (DEC_BATCH, DEPTH, 2, N_HEADS_B, HEAD_DK, HEAD_DV), 0.5),
        "c": nrm(ks[3], (DEC_BATCH, D_MODEL), 1.0),
        "c_ctx": nrm(ks[4], (D_MODEL,), 1.0),
        "w_mod": nrm(ks[5], (DEPTH, D_MODEL, 3 * D_MODEL), 0.5 * D_MODEL ** -0.5),
        "b_mod": nrm(ks[6], (DEPTH, 3 * D_MODEL), 0.01),
        "norm_g": 1.0 + nrm(ks[7], (DEPTH, D_MODEL), 0.01),
        "w_in": nrm(ks[8], (DEPTH, D_MODEL, IN_WIDTH), D_MODEL ** -0.5),
        "conv_a_w": nrm(ks[9], (DEPTH, CONV_WIDTH, CONV_CH), CONV_WIDTH ** -0.5),
        "conv_a_b": nrm(ks[10], (DEPTH, CONV_CH), 0.01),
        "ln_a_g": 1.0 + nrm(ks[11], (DEPTH, CONV_CH), 0.01),
        "ln_a_b": nrm(ks[12], (DEPTH, CONV_CH), 0.01),
        "w_pa": nrm(ks[13], (DEPTH, CONV_CH, D_MODEL), CONV_CH ** -0.5),
        "conv_qkv_w": nrm(ks[14], (DEPTH, SHORT_CONV, 2 * KEY_DIM + VAL_DIM), SHORT_CONV ** -0.5),
        "a_log": jnp.log(jax.random.uniform(ks[16], (DEPTH, 2, N_HEADS_B), f32, 1.0, 16.0)),
        "dt_bias": dt + jnp.log(-jnp.expm1(-dt)),
        "head_norm_g": 1.0 + nrm(ks[17], (DEPTH, HEAD_DV), 0.01),
        "w_pb": nrm(ks[18], (DEPTH, VAL_DIM, D_MODEL), VAL_DIM ** -0.5),
        "w_o": nrm(ks[19], (DEPTH, D_MODEL, D_MODEL), D_MODEL ** -0.5),
        "final_norm_g": 1.0 + nrm(ks[20], (D_MODEL,), 0.01),
    }


def reference(x_prompt, x_sample, state_delta, c, c_ctx, w_mod, b_mod, norm_g, w_in,
              conv_a_w, conv_a_b, ln_a_g, ln_a_b, w_pa, conv_qkv_w, a_log, dt_bias,
              head_norm_g, w_pb, w_o, final_norm_g):
    def layer_params(l):
        return (w_mod[l], b_mod[l], norm_g[l], w_in[l], conv_a_w[l], conv_a_b[l], ln_a_g[l],
                ln_a_b[l], w_pa[l], conv_qkv_w[l], a_log[l], dt_bias[l], head_norm_g[l],
                w_pb[l], w_o[l])

    zero_state = jnp.zeros((x_prompt.shape[0], 2, N_HEADS_B, HEAD_DK, HEAD_DV), jnp.float32)
    cvec_ctx = c_ctx[None, :]
    h = x_prompt
    ctx_states = []
    for l in range(DEPTH):
        h, st = _layer(h, cvec_ctx, zero_state, False, *layer_params(l))
        ctx_states.append(st)
    y_prompt = _rmsnorm(h, final_norm_g)
    state_delta_new = jnp.stack(ctx_states, axis=1).astype(x_prompt.dtype)

    hs = x_sample
    for l in range(DEPTH):
        hs, _ = _layer(hs, c, state_delta[:, l].astype(jnp.float32), True, *layer_params(l))
    y_sample = _rmsnorm(hs, final_norm_g)
    return (y_prompt, y_sample, state_delta_new)
```

```python
from contextlib import ExitStack
import numpy as np
import concourse.bass as bass
import concourse.mybir as mybir
from concourse.bass_utils import run_bass_kernel_spmd

F32 = mybir.dt.float32
BF16 = mybir.dt.bfloat16
AF = mybir.ActivationFunctionType
ALU = mybir.AluOpType

SAME_ENGINE_SYNC = False


class Sched:
    ENGS = ["tensor", "vector", "scalar", "gpsimd", "sync"]

    def __init__(self, nc):
        self.nc = nc
        self.stack = ExitStack()
        self.ops = {e: [] for e in self.ENGS}
        self.count = {}
        self.seen = {e: {} for e in self.ENGS}
        self.last_w = {}
        self.readers = {}
        self.sem_names = set(self.ENGS)
        self.final_group = set()
        self.out_sems = set()
        self.barrier_toks = []

    def barrier(self):
        self.barrier_toks = [(k, v) for k, v in self.count.items()]

    def sb(self, name, shape, dt):
        return self.stack.enter_context(self.nc.sbuf_tensor(name, shape, dt))

    def ps(self, name, shape, dt):
        return self.stack.enter_context(self.nc.psum_tensor(name, shape, dt))

    def _deps(self, eng, reads, writes, strict=False):
        toks = list(self.barrier_toks)
        for k in list(reads) + list(writes):
            t = self.last_w.get(k)
            if t is not None:
                toks.append(t)
        for k in writes:
            toks.extend(self.readers.get(k, []))
        waits = {}
        for (s, v) in toks:
            if s == eng and (eng in ("tensor", "sync") or not (SAME_ENGINE_SYNC or strict)):
                continue
            if self.seen[eng].get(s, 0) >= v:
                continue
            waits[s] = max(waits.get(s, 0), v)
        for s, v in waits.items():
            self.seen[eng][s] = v
        return sorted(waits.items())

    def _commit(self, tok, reads, writes):
        for k in writes:
            self.last_w[k] = tok
            self.readers[k] = []
        for k in reads:
            if k not in writes:
                self.readers.setdefault(k, []).append(tok)

    def op(self, eng, fn, reads=(), writes=(), strict=False):
        waits = self._deps(eng, reads, writes, strict)
        self.count[eng] = self.count.get(eng, 0) + 1
        tok = (eng, self.count[eng])
        self.ops[eng].append((waits, fn, (eng, 1)))
        self._commit(tok, reads, writes)

    def dma(self, eng, out, in_, reads=(), writes=(), semkey=None, out_final=False, group=None):
        if group is not None:
            semkey = "g_" + group
            self.final_group.add(semkey)
        if semkey is None:
            semkey = "d_" + str((list(writes) + list(reads))[0])
        self.sem_names.add(semkey)
        if out_final:
            self.out_sems.add(semkey)
        waits = self._deps(eng, reads, writes)
        self.count[semkey] = self.count.get(semkey, 0) + 16
        tok = (semkey, self.count[semkey])
        self.ops[eng].append((waits, lambda e: e.dma_start(out=out, in_=in_), (semkey, 16)))
        self._commit(tok, reads, writes)

    def finish(self):
        nc = self.nc
        sems = {}
        for i, s in enumerate(sorted(self.sem_names)):
            sems[s] = self.stack.enter_context(nc.semaphore("s%d" % i))
        final = dict(self.count)
        ops = self.ops
        fg = self.final_group
        out_sems = sorted(self.out_sems)

        def emit(eng_name):
            def body(e):
                for waits, fn, (isem, iv) in ops[eng_name]:
                    for s, v in waits:
                        if s in fg:
                            v = final[s]
                        e.wait_ge(sems[s], v)
                    ins = fn(e)
                    ins.then_inc(sems[isem], iv)
                if eng_name == "sync":
                    for s in out_sems:
                        e.wait_ge(sems[s], final[s])
            return body

        with nc.Block() as block:
            block.tensor(emit("tensor"))
            block.vector(emit("vector"))
            block.scalar(emit("scalar"))
            block.gpsimd(emit("gpsimd"))
            block.sync(emit("sync"))
        self.stack.close()


D = 2048
NT = 1024
DEPTH = 2
KT = 16
H = 8
C = 64
NCH = NT // C
IN_W = 11296
EPS = 1e-6
NLEV = 5
COL = dict(a_val=0, a_glu=1024, a_z=2048, q=3072, k=4096, v=5120, z_b=6144,
           beta=7168, alpha=7184, gate_a=7200, gate_b=9248)


def build_program(n_layers=DEPTH, debug=False):
    nc = bass.Bass("TRN2", target_bir_lowering=False)
    S = Sched(nc)

    def din(name, shape):
        return nc.dram_tensor(name, shape, F32, kind="ExternalInput").ap()

    xT_d = din("xT", [D, NT])
    cv_d = din("cv", [128, KT])
    s0_d = din("s0", [DEPTH, 2, 128, H * 128])
    flg_d = din("flg", [128, 4])
    cst_d = din("cst", [128, 6 * 128])
    w_mod_d = din("w_mod", [DEPTH, D, 3 * D])
    b_mod_d = din("b_mod", [DEPTH, 128, 48])
    ng_d = din("norm_g", [DEPTH, 128, KT])
    w_in_d = din("w_in", [DEPTH, D, IN_W])
    caw_d = din("caw", [DEPTH, 128, 8 * 31])
    cab_d = din("cab", [DEPTH, 128, 8])
    lng_d = din("lng", [DEPTH, 128, 8])
    lnb_d = din("lnb", [DEPTH, 128, 8])
    w_pa_d = din("w_pa", [DEPTH, 1024, D])
    cqw_d = din("cqw", [DEPTH, 128, 24 * 3])
    alog_d = din("alog", [DEPTH, 128, 8])
    dtb_d = din("dtb", [DEPTH, 128, 8])
    hng_d = din("hng", [DEPTH, 128, 1])
    w_pb_d = din("w_pb", [DEPTH, 1024, D])
    w_o_d = din("w_o", [DEPTH, D, D])
    fng_d = din("fng", [128, KT])
    yT_d = nc.dram_tensor("yT", [D, NT], F32, kind="ExternalOutput").ap()
    st_d = nc.dram_tensor("st", [DEPTH, 4, 2, H, 128, 128], F32, kind="ExternalOutput").ap()

    xT = S.sb("xTs", [128, KT * NT], F32)
    xTv = xT[:].rearrange("p (k t) -> p k t", k=KT)
    big = S.sb("big", [128, 24 * NT], BF16)
    bigv = big[:].rearrange("p (s t) -> p s t", s=24)
    of = S.sb("of", [128, 8 * NT], BF16)
    ofv = of[:].rearrange("p (s t) -> p s t", s=8)
    R1 = S.sb("R1", [128, 8192], F32)
    hnT = R1[:].bitcast(BF16).rearrange("p (k t) -> p k t", k=KT)
    R3 = S.sb("R3", [128, 3080], F32)
    wbs = [S.sb("wb%d" % i, [128, 16 * 256], BF16) for i in range(2)]
    wmf = R3[:, 0:1024]
    cst = S.sb("cst_s", [128, 6 * 128], F32)
    identb = S.sb("identb", [128, 128], BF16)
    onesb = S.sb("onesb", [128, 128], BF16)
    onesf = S.sb("onesf", [128, 128], F32)
    negones = S.sb("negones", [128, 64], F32)
    negtri = S.sb("negtri", [128, 64], F32)
    flg = S.sb("flg_s", [128, 4], F32)
    epsc = S.sb("epsc", [128, 3], F32)
    cvs = S.sb("cvs", [128, KT], F32)
    fng = S.sb("fng_s", [128, KT], F32)
    mods = [S.sb("mod%d" % i, [128, 48], F32) for i in range(DEPTH)]
    modAs = [S.sb("modA%d" % i, [128, KT], F32) for i in range(DEPTH)]
    bmods = [S.sb("bmod%d" % i, [128, 48], F32) for i in range(DEPTH)]
    ngs = [S.sb("ngs%d" % i, [128, KT], F32) for i in range(DEPTH)]
    scb = S.sb("scb", [128, KT], BF16)
    par = {}
    for nm, w in (("caw", 248), ("cab", 8), ("lng", 8), ("lnb", 8),
                  ("cqw", 72), ("hng", 1)):
        par[nm] = S.sb("p_" + nm, [128, w], F32)
    cawP = S.sb("cawP", [128, 248], F32)
    cawS = S.sb("cawS", [128, 248], F32)
    cawNS = S.sb("cawNS", [128, 248], F32)
    cqwL = S.sb("cqwL", [128, 24], F32)
    cqwR = S.sb("cqwR", [128, 24], F32)
    alog = S.sb("alog_s", [128, 8], F32)
    dtb = S.sb("dtb_s", [128, 8], F32)
    negA = S.sb("negA", [128, 8], F32)
    gTM = S.sb("gTM", [128, NCH * 8], F32)
    bTM = S.sb("bTM", [128, NCH * 8], F32)
    wg = S.sb("wg", [128, KT * 32], BF16)
    St = R3[:, 1032:2056]
    tmpA = R3[:, 1032:2056]
    tmpB = S.sb("tmpB", [128, 512], F32)
    rs = R3[:, 2056:3080]
    sqs = [S.sb("sq%d" % i, [128, 512], BF16) for i in range(2)]
    sgb = R3[:, 0:1024].bitcast(BF16)
    pre = R3[:, 0:NT + 2]

    P = [S.ps("P%d" % i, [128, 512], F32) for i in range(7)]
    PT = S.ps("PT", [128, 1024], BF16)

    def T(fn, r, w): S.op("tensor", fn, r, w)
    def V(fn, r, w): S.op("vector", fn, r, w)
    def A(fn, r, w): S.op("scalar", fn, r, w)
    def G(fn, r, w): S.op("gpsimd", fn, r, w)

    S.dma("sync", cst[:], cst_d, writes=["cst"], group="par0")
    S.dma("sync", flg[:], flg_d, writes=["flg"], group="par0")
    S.dma("sync", cvs[:], cv_d, writes=["cvs"], group="par0")
    S.dma("sync", fng[:], fng_d, writes=["fng"], group="par0")
    S.dma("gpsimd", identb[:], cst_d[:, 0:128], writes=["identb"], group="par0c")
    for kt in range(KT):
        S.dma("sync", xTv[:, kt, :], xT_d[kt * 128:(kt + 1) * 128, :], writes=["x%d" % kt], group="xin")
    for i in range(DEPTH):
        S.dma("sync", bmods[i][:], b_mod_d[i], writes=["bmod%d" % i], group="par0")
        S.dma("sync", ngs[i][:], ng_d[i], writes=["ngs%d" % i], group="par0")
    A(lambda e: e.activation(scb[:], cvs[:], AF.Silu), ["cvs"], ["scb"])
    V(lambda e: e.memset(onesb[:], 1.0), [], ["onesb"])
    V(lambda e: e.memset(onesf[:], 1.0), [], ["onesf"])
    V(lambda e: e.memset(negones[:], -1.0), [], ["negones"])
    V(lambda e: e.memset(epsc[:, 0:1], EPS), [], ["epsc"])
    V(lambda e: e.memset(epsc[:, 1:2], 128.0 * EPS), ["epsc"], ["epsc"])
    V(lambda e: e.memset(epsc[:, 2:3], 1.0), ["epsc"], ["epsc"])
    tri2 = cst[:, 128:192]
    maskS2 = cst[:, 192:256]
    ident2 = cst[:, 256:320]
    V(lambda e: e.tensor_scalar(negtri[:], tri2, -1.0, None, ALU.mult), ["cst"], ["negtri"])

    def bc(ap, shape, axis):
        return ap.unsqueeze(axis).to_broadcast(shape)

    wctr = [0]

    def load_w(src3, ktn, ncols):
        i = wctr[0] % 2
        wctr[0] += 1
        key = "wb%d" % i
        dst = wbs[i][:, 0:ktn * ncols].rearrange("p (k c) -> p k c", k=ktn)
        S.dma("gpsimd", dst, src3, writes=[key])
        return dst, key

    pctr = [0]

    def proj(w2d, c0, ncols_total, ktn, rhs_fn, rhs_keys_fn, consumer, gcols=256, after_group=None):
        wv = w2d.rearrange("(k p) c -> p k c", p=128)
        for g0 in range(0, ncols_total, gcols):
            gc_ = min(gcols, ncols_total - g0)
            wt, wkey = load_w(wv[:, :, c0 + g0:c0 + g0 + gc_], ktn, gc_)
            for f0 in range(0, gc_, 128):
                fw = min(128, gc_ - f0)
                ft = (g0 + f0) // 128
                for half in range(2):
                    pi = pctr[0] % 4
                    pctr[0] += 1
                    pk = "P%d" % pi
                    for kt in range(ktn):
                        T(lambda e, kt=kt, pi=pi, f0=f0, fw=fw, half=half, wt=wt: e.matmul(
                            P[pi][0:fw, :], wt[:, kt, f0:f0 + fw], rhs_fn(kt, half),
                            start=(kt == 0), stop=(kt == ktn - 1)),
                          [wkey] + rhs_keys_fn(kt), [pk])
                    consumer(ft, half, P[pi][0:fw, :], pk)
            if after_group is not None:
                after_group(g0 // gcols)

    def hs(half):
        return slice(half * 512, (half + 1) * 512)


    PTf = PT[:].bitcast(F32)

    def mod_group(l, g):
        wv = w_mod_d[l].rearrange("(k p) c -> p k c", p=128)
        wt, wkey = load_w(wv[:, :, g * 256:(g + 1) * 256], KT, 256)
        for f in range(2):
            n = g * 2 + f
            for kt in range(KT):
                T(lambda e, kt=kt, n=n, f=f, wt=wt: e.matmul(PTf[:, n:n + 1], wt[:, kt, f * 128:(f + 1) * 128], scb[:, kt:kt + 1],
                                                            start=(kt == 0), stop=(kt == KT - 1)), [wkey, "scb"], ["PT"])

    def mod_group0(g, l=0, bi=None):
        wv = w_mod_d[l].rearrange("(k p) c -> p k c", p=128)
        if bi is None:
            bi = g % 6
        wt = big[:, bi * 4096:(bi + 1) * 4096].rearrange("p (k c) -> p k c", k=KT)
        bkeys = ["big%d" % (4 * bi + j) for j in range(4)]
        S.dma("gpsimd", wt, wv[:, :, g * 256:(g + 1) * 256], writes=bkeys, semkey="d_bigw%d" % bi)
        for f in range(2):
            n = g * 2 + f
            for kt in range(KT):
                T(lambda e, kt=kt, n=n, f=f, wt=wt: e.matmul(PTf[:, n:n + 1], wt[:, kt, f * 128:(f + 1) * 128], scb[:, kt:kt + 1],
                                                            start=(kt == 0), stop=(kt == KT - 1)), bkeys + ["scb"], ["PT"])

    def mod_finish(l, part):
        if part == 0:
            V(lambda e: e.tensor_tensor(mods[l][:, 0:32], PTf[:, 0:32], bmods[l][:, 0:32], ALU.add), ["PT", "bmod%d" % l], ["mod%d" % l])
            S.op("vector", lambda e: e.scalar_tensor_tensor(modAs[l][:], mods[l][:, 16:32], 1.0, ngs[l][:], ALU.add, ALU.mult),
                 ["mod%d" % l, "ngs%d" % l], ["modA%d" % l], strict=True)
        else:
            V(lambda e: e.tensor_tensor(mods[l][:, 32:48], PTf[:, 32:48], bmods[l][:, 32:48], ALU.add), ["PT", "bmod%d" % l], ["modg%d" % l])

    def f32v(R, off, n):
        return R[:, off:off + n]

    def b16v(R, off, n):
        return R[:, off:off + n // 2].bitcast(BF16)

    def h3(ap, h=H):
        return ap.rearrange("p (h t) -> p h t", h=h)

    W0 = wbs[0][:].bitcast(F32)
    W1 = wbs[1][:].bitcast(F32)
    gB = f32v(R1, 0, 512); gTri = f32v(R1, 512, 512); Em = f32v(R1, 1024, 512); ETm = f32v(R1, 1536, 512)
    Pm = f32v(R1, 2048, 512); EGs = [f32v(R1, 2560, 512), f32v(R1, 3072, 512)]
    u_ = f32v(R1, 3584, 1024); tmin = f32v(R1, 3584, 512); osum = f32v(R1, 4608, 512); rstd = f32v(R1, 5120, 512)
    MT0 = b16v(R1, 5632, 512)
    AA = [b16v(R1, 5888, 512), b16v(R1, 6144, 512)]
    AT = [b16v(R1, 6400, 512), b16v(R1, 6656, 512)]
    Pb = b16v(R1, 6912, 512); Plo = b16v(R1, 7168, 512); Rb = b16v(R1, 7424, 512); PbT = b16v(R1, 7680, 512)
    TTb = b16v(R1, 7936, 512)
    TTbg = b16v(W0, 0, 512); qkT = b16v(W0, 256, 512)
    wTs = [b16v(W0, 512, 512), b16v(W0, 768, 512)]
    qdTs = [b16v(W0, 1024, 512), b16v(W0, 1280, 512)]
    sqo = b16v(W0, 1536, 512)
    gcsL = [f32v(W0, 1792 + 64 * i, 8) for i in range(2)]; egcL = [f32v(W0, 1800 + 64 * i, 8) for i in range(2)]
    ejL = [f32v(W0, 1808 + 64 * i, 8) for i in range(2)]; bgL = [f32v(W0, 1816 + 64 * i, 8) for i in range(2)]
    edecL = [f32v(W0, 1824 + 64 * i, 16) for i in range(2)]
    nbm = gB
    kTM = b16v(W1, 0, 1024); vTM = b16v(W1, 512, 1024); vn = b16v(W1, 1024, 1024); vns = b16v(W1, 1536, 1024)
    Sts = [R3[:, 0:1024], R3[:, 1032:2056]]
    Sbs = [R3[:, 2056:2568].bitcast(BF16), R3[:, 2568:3080].bitcast(BF16)]
    HR = [slice(0, 64), slice(64, 128)]

    def deltanet(l):
        for f in range(16):
            for half in range(2):
                sq = sqs[half]
                pb = 5 + half
                V(lambda e, f=f, half=half, sq=sq: e.tensor_tensor(sq[:], bigv[:, f, hs(half)], bigv[:, f, hs(half)], ALU.mult),
                  ["big%d" % f], ["sq%d" % half])
                T(lambda e, sq=sq, pb=pb: e.matmul(P[pb][:], onesb[:], sq[:], start=True, stop=True),
                  ["sq%d" % half, "onesb"], ["P%d" % pb])
                if f < 8:
                    A(lambda e, half=half, pb=pb: e.activation(rs[:, hs(half)], P[pb][:], AF.Ln, bias=epsc[:, 1:2], scale=128.0),
                      ["P%d" % pb, "epsc"], ["rs"])
                else:
                    A(lambda e, half=half, pb=pb: e.activation(rs[:, hs(half)], P[pb][:], AF.Ln, bias=epsc[:, 0:1], scale=1.0),
                      ["P%d" % pb, "epsc"], ["rs"])
                A(lambda e, half=half: e.activation(rs[:, hs(half)], rs[:, hs(half)], AF.Exp, scale=-0.5), ["rs"], ["rs"])
            V(lambda e, f=f: e.tensor_tensor(bigv[:, f, :], bigv[:, f, :], rs, ALU.mult), ["big%d" % f, "rs"], ["big%d" % f])

        gTMv = gTM[:].rearrange("p (c g) -> p c g", g=8)
        bTMv = bTM[:].rearrange("p (c g) -> p c g", g=8)
        qk_keys = ["big%d" % i for i in range(16)]
        k_keys = ["big%d" % i for i in range(8, 16)]
        v_keys = ["big%d" % i for i in range(16, 24)]
        q_keys = ["big%d" % i for i in range(0, 8)]
        for d_ in range(2):
            S.dma("sync", Sts[d_], s0_d[l, d_], writes=["St%d" % d_])
            A(lambda e, d_=d_: e.activation(Sbs[d_], Sts[d_], AF.Copy), ["St%d" % d_, "rs"], ["Sb%d" % d_, "rs"])

        def dn_gates(s_):
            pr = s_ % 2
            gcs, egc, ej, bg, edec = gcsL[pr], egcL[pr], ejL[pr], bgL[pr], edecL[pr]
            kp = "_%d" % pr
            g8 = gTMv[:, s_, :]
            b8 = bTMv[:, s_, :]
            gTri3 = h3(gTri)
            G(lambda e: e.tensor_tensor(gTri3, bc(g8, [128, 8, 64], 2), bc(tri2, [128, 8, 64], 1), ALU.mult), ["gTM", "cst"], ["gTri"])
            for hf in range(2):
                r = HR[hf]
                tpd = (64 * hf, 64 * hf)
                T(lambda e, r=r, tpd=tpd: e.matmul(P[3][r, 0:8], tri2[r, :], g8[r, :], start=True, stop=True, tile_position=tpd),
                  ["cst", "gTM"], ["P3"])
                T(lambda e, r=r, hf=hf: e.matmul(P[3][:, 8 + 8 * hf:16 + 8 * hf], onesf[r, :], g8[r, :], start=True, stop=True,
                                                 tile_position=(64 * hf, 0)), ["onesf", "gTM"], ["P3"])
                T(lambda e, r=r, tpd=tpd: e.matmul(P[3][r, 24:32], onesf[r, 0:64], g8[r, :], start=True, stop=True, tile_position=tpd),
                  ["onesf", "gTM"], ["P3"])
            for hf in range(2):
                r = HR[hf]
                pg = 2 if hf == 0 else 6
                T(lambda e, r=r, hf=hf, pg=pg: e.matmul(P[pg][:, :], onesf[r, :], gTri[r, :], start=True, stop=True,
                                                        tile_position=(64 * hf, 0)), ["onesf", "gTri"], ["P%d" % pg])
            G(lambda e: e.tensor_tensor(h3(nbm), bc(maskS2, [128, 8, 64], 1), bc(b8, [128, 8, 64], 2), ALU.mult),
              ["cst", "bTM"], ["nbm"])
            A(lambda e: e.activation(gcs, P[3][:, 0:8], AF.Copy), ["P3"], ["gcs" + kp])
            for hf in range(2):
                r = HR[hf]
                pg = 2 if hf == 0 else 6
                V(lambda e, r=r, pg=pg: e.tensor_tensor(h3(tmin)[r], bc(gcs[r, :], [64, 8, 64], 2), h3(P[pg][r, :]), ALU.subtract),
                  ["gcs" + kp, "P%d" % pg], ["tmin"])
                V(lambda e, r=r, pg=pg: e.tensor_tensor(h3(osum)[r], h3(P[pg][r, :]), bc(gcs[r, :], [64, 8, 64], 2), ALU.subtract),
                  ["gcs" + kp, "P%d" % pg], ["osum"])
            G(lambda e: e.tensor_scalar(tmin, tmin, 0.0, None, ALU.min), ["tmin"], ["tmin"])
            A(lambda e: e.activation(Em, tmin, AF.Exp), ["tmin"], ["Em"])
            G(lambda e: e.tensor_scalar(osum, osum, 0.0, None, ALU.min), ["osum"], ["osum"])
            A(lambda e: e.activation(ETm, osum, AF.Exp), ["osum"], ["ETm"])
            A(lambda e: e.activation(EGs[0], P[2][:, :], AF.Exp), ["P2"], ["EG0"])
            A(lambda e: e.activation(EGs[1], P[6][:, :], AF.Exp), ["P6"], ["EG1"])
            A(lambda e: e.activation(egc, P[3][:, 0:8], AF.Exp), ["P3"], ["egc" + kp])
            A(lambda e: e.activation(edec, P[3][:, 8:24], AF.Exp), ["P3"], ["edec" + kp])
            if s_ % 4 == 0 and s_ > 0:
                V(lambda e: e.tensor_scalar(edec, edec, flg[:, 1:2], None, ALU.mult), ["edec" + kp, "flg"], ["edec" + kp])
            V(lambda e: e.tensor_tensor(ej, P[3][:, 24:32], gcs, ALU.subtract), ["P3", "gcs" + kp], ["ej" + kp])
            A(lambda e: e.activation(ej, ej, AF.Exp), ["ej" + kp], ["ej" + kp])
            V(lambda e: e.tensor_tensor(bg, b8, egc, ALU.mult), ["bTM", "egc" + kp], ["bg" + kp])

        def dn_step(s_):
            toks = [slice(s_ * C, (s_ + 1) * C), slice((NCH - 1 - s_) * C, (NCH - s_) * C)]
            g8 = gTMv[:, s_, :]
            b8 = bTMv[:, s_, :]
            pr = s_ % 2
            gcs, egc, ej, bg, edec = gcsL[pr], egcL[pr], ejL[pr], bgL[pr], edecL[pr]
            kp = "_%d" % pr
            Em3, ETm3, Pm3 = h3(Em), h3(ETm), h3(Pm)
            for hf in range(2):
                r = HR[hf]; tok = toks[hf]
                for h in range(H):
                    T(lambda e, h=h, r=r, tok=tok, hf=hf: e.matmul(P[4][r, h * 64:(h + 1) * 64], bigv[:, 8 + h, tok], bigv[:, 8 + h, tok],
                                                                  start=True, stop=True, tile_position=(0, 64 * hf)), k_keys, ["P4"])
            V(lambda e: e.scalar_tensor_tensor(Em, Em, 1.0, nbm, ALU.min, ALU.mult), ["Em", "nbm"], ["Em"])
            V(lambda e: e.scalar_tensor_tensor(MT0, P[4][:, :], -1.0, Em, ALU.mult, ALU.mult), ["P4", "Em"], ["MT0"])

            def mm_hh(out_bank, lhs, rhs, rkeys, wkey, rhs2=None):
                for h in range(H):
                    for hf in range(2):
                        r = HR[hf]
                        T(lambda e, h=h, r=r, hf=hf: e.matmul(out_bank[r, h * 64:(h + 1) * 64], h3(lhs)[r, h, :], h3(rhs)[r, h, :],
                                                             start=True, stop=(rhs2 is None), tile_position=(64 * hf, 64 * hf)), rkeys, [wkey])
                        if rhs2 is not None:
                            T(lambda e, h=h, r=r, hf=hf: e.matmul(out_bank[r, h * 64:(h + 1) * 64], h3(lhs)[r, h, :], h3(rhs2)[r, h, :],
                                                                 start=False, stop=True, tile_position=(64 * hf, 64 * hf)), rkeys, [wkey])

            def tr_hh(src, skey):
                for h in range(H):
                    for hf in range(2):
                        r = HR[hf]
                        T(lambda e, h=h, r=r, hf=hf: e.transpose(PT[r, h * 64:(h + 1) * 64], h3(src)[r, h, :], identb[r, r],
                                                                tile_position=(64 * hf, 64 * hf)), [skey, "identb"], ["PT"])

            tr_hh(MT0, "MT0")
            A(lambda e: e.activation(AA[0], PT[:, 0:512], AF.Copy), ["PT"], ["AA0"])
            V(lambda e: e.tensor_tensor(h3(Pb), h3(AA[0]), bc(ident2, [128, 8, 64], 1), ALU.add), ["AA0", "cst"], ["Pb"])
            def kv_tr(base, keys, dst, dkey, eng):
                for hf in range(2):
                    r = HR[hf]; tok = toks[hf]
                    for h in range(H):
                        T(lambda e, h=h, r=r, tok=tok, hf=hf: e.transpose(PT[r, h * 128:(h + 1) * 128], bigv[:, base + h, tok],
                                                                         identb[:, :], tile_position=(0, 64 * hf)),
                          keys + ["identb"], ["PT"])

            def kv_cp(dst, dkey, eng):
                if eng == "scalar":
                    A(lambda e: e.activation(dst, PT[:, :], AF.Copy), ["PT"], [dkey])
                else:
                    V(lambda e: e.tensor_copy(dst, PT[:, :]), ["PT"], [dkey])

            def qk_all():
                for hf in range(2):
                    r = HR[hf]; tok = toks[hf]
                    for h in range(H):
                        T(lambda e, h=h, r=r, tok=tok, hf=hf: e.matmul(P[2][r, h * 64:(h + 1) * 64], bigv[:, 8 + h, tok], bigv[:, h, tok],
                                                                      start=True, stop=True, tile_position=(0, 64 * hf)), qk_keys, ["P2"])
                V(lambda e: e.scalar_tensor_tensor(ETm3, ETm3, 1.0, bc(tri2, [128, 8, 64], 1), ALU.min, ALU.mult), ["ETm", "cst"], ["ETm"])
                V(lambda e: e.tensor_tensor(qkT, P[2][:, :], ETm, ALU.mult), ["P2", "ETm"], ["qkT"])
                G(lambda e: e.tensor_tensor(h3(qdTs[0]), bigv[:, 0:8, toks[0]], h3(EGs[0]), ALU.mult), q_keys + ["EG0"], ["qdT0"])
                G(lambda e: e.tensor_tensor(h3(qdTs[1]), bigv[:, 0:8, toks[1]], h3(EGs[1]), ALU.mult), q_keys + ["EG1"], ["qdT1"])

            def prod(atn, atnk):
                mm_hh(P[4], atn, Pb, [atnk, "Pb"], "P4")
                V(lambda e: e.tensor_tensor(Pb, Pb, P[4][:, :], ALU.add), ["Pb", "P4"], ["Pb"])

            cur = 0
            pending = None
            for k in range(2, NLEV + 1):
                nxt = 1 - cur
                atc, atk = (MT0, "MT0") if k == 2 else (AT[cur], "AT%d" % cur)
                mm_hh(P[5], AA[cur], atc, ["AA%d" % cur, atk], "P5")
                if k < NLEV:
                    mm_hh(P[6], atc, AA[cur], ["AA%d" % cur, atk], "P6")
                if k == 2:
                    kv_tr(8, k_keys, kTM, "kTM", "scalar")
                elif k == 3:
                    kv_tr(16, v_keys, vTM, "vTM", "vector")
                A(lambda e, nxt=nxt: e.activation(AT[nxt], P[5][:, :], AF.Copy), ["P5"], ["AT%d" % nxt])
                if k < NLEV:
                    V(lambda e, nxt=nxt: e.tensor_copy(AA[nxt], P[6][:, :]), ["P6"], ["AA%d" % nxt])
                if k == 2:
                    kv_cp(kTM, "kTM", "scalar")
                elif k == 3:
                    kv_cp(vTM, "vTM", "vector")
                if pending is not None:
                    prod(*pending)
                if k == 4:
                    qk_all()
                if k == NLEV and s_ + 1 < NCH:
                    dn_gates(s_ + 1)
                pending = (AT[nxt], "AT%d" % nxt)
                cur = nxt
            prod(*pending)
            mm_hh(P[4], MT0, Pb, ["MT0", "Pb"], "P4")
            V(lambda e: e.tensor_tensor(tmin, P[4][:, :], Pb, ALU.subtract), ["P4", "Pb"], ["tmin"])
            V(lambda e: e.tensor_tensor(h3(Rb), h3(tmin), bc(ident2, [128, 8, 64], 1), ALU.add), ["tmin", "cst"], ["Rb"])
            tr_hh(Pb, "Pb")
            A(lambda e: e.activation(PbT, PT[:, 0:512], AF.Copy), ["PT"], ["PbT"])
            mm_hh(P[5], PbT, Rb, ["PbT", "Rb"], "P5")
            V(lambda e: e.tensor_tensor(Pm, Pb, P[5][:, :], ALU.add), ["Pb", "P5"], ["Pm"])
            V(lambda e: e.tensor_tensor(h3(TTb), Pm3, bc(b8, [128, 8, 64], 2), ALU.mult), ["Pm", "bTM"], ["TTb"])
            V(lambda e: e.tensor_tensor(h3(TTbg), Pm3, bc(bg, [128, 8, 64], 2), ALU.mult), ["Pm", "bg" + kp], ["TTbg"])
            for h in range(H):
                for hf in range(2):
                    r = HR[hf]
                    pp = 5 if h < 4 else 6
                    T(lambda e, h=h, pp=pp, r=r, hf=hf: e.matmul(P[pp][r, (h % 4) * 128:(h % 4 + 1) * 128], h3(TTb)[r, h, :],
                                                                h3(vTM)[r, h, :], start=True, stop=True,
                                                                tile_position=(64 * hf, 64 * hf)), ["TTb", "vTM"], ["P%d" % pp])
            A(lambda e: e.activation(u_[:, 0:512], P[5][:, :], AF.Copy), ["P5", "tmin"], ["u0", "tmin"])
            V(lambda e: e.tensor_copy(u_[:, 512:1024], P[6][:, :]), ["P6"], ["u1"])
            for h in range(H):
                for hf in range(2):
                    r = HR[hf]
                    T(lambda e, h=h, r=r, hf=hf: e.matmul(P[hf][:, h * 64:(h + 1) * 64], h3(kTM)[r, h, :], h3(TTbg)[r, h, :],
                                                         start=True, stop=True, tile_position=(64 * hf, 0)),
                      ["kTM", "TTbg"], ["P%d" % hf])
            A(lambda e: e.activation(wTs[0], P[0][:, :], AF.Copy), ["P0"], ["wT0"])
            V(lambda e: e.tensor_copy(wTs[1], P[1][:, :]), ["P1"], ["wT1"])
            for hf in range(2):
                r = HR[hf]
                for h in range(H):
                    pp = 5 if h < 4 else 6
                    T(lambda e, h=h, pp=pp, r=r, hf=hf: e.matmul(P[pp][r, (h % 4) * 128:(h % 4 + 1) * 128], h3(wTs[hf])[:, h, :],
                                                                Sbs[hf][:, h * 128:(h + 1) * 128], start=True, stop=True,
                                                                tile_position=(0, 64 * hf)), ["wT%d" % hf, "Sb%d" % hf], ["P%d" % pp])
            V(lambda e: e.tensor_tensor(vn[:, 0:512], u_[:, 0:512], P[5][:, :], ALU.subtract), ["u0", "P5"], ["vn0"])
            V(lambda e: e.tensor_tensor(vn[:, 512:1024], u_[:, 512:1024], P[6][:, :], ALU.subtract), ["u1", "P6"], ["vn1"])
            V(lambda e: e.tensor_tensor(h3(vns), h3(vn), bc(ej, [128, 8, 128], 2), ALU.mult), ["vn0", "vn1", "ej" + kp], ["vns"])
            for hf in range(2):
                r = HR[hf]
                po = 2 + hf
                for h in range(H):
                    T(lambda e, h=h, hf=hf, po=po: e.matmul(P[po][:, h * 64:(h + 1) * 64], Sbs[hf][:, h * 128:(h + 1) * 128],
                                                           h3(qdTs[hf])[:, h, :], start=True, stop=False),
                      ["Sb%d" % hf, "qdT%d" % hf], ["P%d" % po])
                    T(lambda e, h=h, hf=hf, po=po, r=r: e.matmul(P[po][:, h * 64:(h + 1) * 64], h3(vn)[r, h, :], h3(qkT)[r, h, :],
                                                                start=False, stop=True, tile_position=(64 * hf, 0)),
                      ["vn0", "vn1", "qkT"], ["P%d" % po])
            for hf in range(2):
                po = 2 + hf; tok = toks[hf]
                if s_ < NCH // 2:
                    if hf == 0:
                        A(lambda e, po=po, tok=tok: e.activation(ofv[:, :, tok], h3(P[po][:, :]), AF.Copy), ["P%d" % po], ["of"])
                    else:
                        V(lambda e, po=po, tok=tok: e.tensor_copy(ofv[:, :, tok], h3(P[po][:, :])), ["P%d" % po], ["of"])
                else:
                    V(lambda e, po=po, tok=tok: e.tensor_tensor(h3(osum), h3(P[po][:, :]), ofv[:, :, tok], ALU.add),
                      ["P%d" % po, "of"], ["osum"])
                    A(lambda e: e.activation(sqo, osum, AF.Square), ["osum"], ["sqo"])
                    T(lambda e: e.matmul(P[4][:, :], onesb[:], sqo, start=True, stop=True), ["sqo", "onesb"], ["P4"])
                    A(lambda e: e.activation(rstd, P[4][:, :], AF.Ln, bias=epsc[:, 0:1], scale=1.0 / 128), ["P4", "epsc"], ["rstd"])
                    A(lambda e: e.activation(rstd, rstd, AF.Exp, scale=-0.5), ["rstd"], ["rstd"])
                    V(lambda e, tok=tok: e.scalar_tensor_tensor(ofv[:, :, tok], h3(osum), par["hng"][:, 0:1], h3(rstd),
                                                                ALU.mult, ALU.mult), ["osum", "rstd", "p_hng", "of"], ["of"])
            for hf in range(2):
                r = HR[hf]
                banks = (5, 6) if hf == 0 else (0, 1)
                for h in range(H):
                    pp = banks[0] if h < 4 else banks[1]
                    T(lambda e, h=h, pp=pp, r=r, hf=hf: e.matmul(P[pp][:, (h % 4) * 128:(h % 4 + 1) * 128], h3(kTM)[r, h, :],
                                                                h3(vns)[r, h, :], start=True, stop=True, tile_position=(64 * hf, 0)),
                      ["kTM", "vns"], ["P%d" % pp])
                St = Sts[hf]; sk = "St%d" % hf
                V(lambda e, St=St, hf=hf: e.tensor_tensor(h3(St), h3(St), bc(edec[:, 8 * hf:8 * hf + 8], [128, 8, 128], 2), ALU.mult),
                  [sk, "edec" + kp], [sk])
                V(lambda e, St=St, b0=banks[0]: e.tensor_tensor(St[:, 0:512], St[:, 0:512], P[b0][:, :], ALU.add), [sk, "P%d" % banks[0]], [sk])
                V(lambda e, St=St, b1=banks[1]: e.tensor_tensor(St[:, 512:1024], St[:, 512:1024], P[b1][:, :], ALU.add), [sk, "P%d" % banks[1]], [sk])
                if s_ % 4 == 3:
                    seq = (s_ // 4) if hf == 0 else ((NCH - 1 - s_) // 4)
                    S.dma("sync", st_d[l, seq, hf].rearrange("h k v -> k h v"), h3(St), reads=[sk],
                          writes=["st_out"], semkey="d_Sout%d" % hf, out_final=True)
                    V(lambda e, St=St, hf=hf: e.tensor_scalar(Sbs[hf], St, flg[:, 1:2], None, ALU.mult), [sk, "flg"], ["Sb%d" % hf])
                else:
                    A(lambda e, St=St, hf=hf: e.activation(Sbs[hf], St, AF.Copy), [sk], ["Sb%d" % hf])

        dn_gates(0)
        for s_ in range(NCH):
            dn_step(s_)

    accs = [big[:, 0:2048].bitcast(F32), big[:, 2048:4096].bitcast(F32)]
    acc_keys = [["big0", "big1"], ["big2", "big3"]]
    mean = big[:, 4096:6144].bitcast(F32); mean_keys = ["big4", "big5"]
    lrs = big[:, 6144:8192].bitcast(F32); lrs_keys = ["big6", "big7"]
    cttmp = big[:, 8192:10240].bitcast(F32); cttmp_keys = ["big8", "big9"]

    NDG = 6
    dgs = [R3[:, 64 * i:64 * (i + 1)].bitcast(BF16) for i in range(NDG)]
    dgctr = [0]

    def conv_tile(l, ct):
        acc = accs[ct % 2]; ak = acc_keys[ct % 2]
        ua = bigv[:, 16 + ct, :]; uk = "big%d" % (16 + ct)
        caw = par["caw"]
        banks = [0, 1] if ct % 2 == 0 else [2, 0]
        taps = []
        wmain, wk = (caw, "p_caw") if ct < 4 else (cawP, "cawP")
        order = [0] + [d for d in range(-15, 16) if d != 0]
        for d in order:
            j = ct * 31 + 15 + d
            if d == 0:
                taps.append((caw, "p_caw", j, "seg", d))
            else:
                taps.append((wmain, wk, j, "seg", d))
        if ct >= 4:
            for d in range(-15, 16):
                if d != 0:
                    taps.append((cawS, "cawS", ct * 31 + 15 + d, "vert", d))
        specs = {0: [], 1: []}
        for ti, (wt_, wk_, j, kind, d) in enumerate(taps):
            for hf in range(2):
                base = 512 * hf
                if kind == "seg":
                    e_ = abs(d)
                    if d >= 0:
                        o = (0, 256 - d); i_ = (d, 256)
                    else:
                        o = (e_, 256); i_ = (0, 256 - e_)
                    specs[hf].append((ti, "seg", o, i_))
                else:
                    if d > 0:
                        lo, hi = base, min(base + 512, NT - 64 * d); sh = 64 * d
                    else:
                        lo, hi = max(base, -64 * d), base + 512; sh = 64 * d
                    if hi > lo:
                        specs[hf].append((ti, "vert", (lo - base, hi - base), (lo + sh, hi + sh)))
        last = {hf: specs[hf][-1][0] for hf in range(2)}
        by_tap = {}
        for hf in range(2):
            for sp in specs[hf]:
                by_tap.setdefault(sp[0], []).append((hf, sp))
        for ti, (wt_, wk_, j, kind, d) in enumerate(taps):
            di = dgctr[0] % NDG
            dgctr[0] += 1
            dg = dgs[di]; dk = "dg%d" % di
            A(lambda e, dg=dg, wt_=wt_, j=j: e.activation(dg, identb[:, :], AF.Identity, bias=0.0, scale=wt_[:, j:j + 1]),
              ["identb", wk_], [dk])
            for hf, sp in by_tap.get(ti, []):
                pb = banks[hf]
                pv = P[pb][:, :].rearrange("p (s t) -> p s t", s=2)
                xv = ua[:, hs(hf)].rearrange("p (s t) -> p s t", s=2)
                first = (ti == 0)
                lastf = (ti == last[hf]) and (sp is [x for x in specs[hf] if x[0] == ti][-1])
                if sp[1] == "seg":
                    (o0, o1), (i0_, i1_) = sp[2], sp[3]
                    T(lambda e, dg=dg, pv=pv, xv=xv, o0=o0, o1=o1, i0_=i0_, i1_=i1_, first=first, lastf=lastf: e.matmul(
                        pv[:, :, o0:o1], dg, xv[:, :, i0_:i1_], start=first, stop=lastf), [dk, uk], ["P%d" % pb])
                else:
                    (o0, o1), (i0_, i1_) = sp[2], sp[3]
                    T(lambda e, dg=dg, pb=pb, o0=o0, o1=o1, i0_=i0_, i1_=i1_, lastf=lastf: e.matmul(
                        P[pb][:, o0:o1], dg, ua[:, i0_:i1_], start=False, stop=lastf), [dk, uk], ["P%d" % pb])
        for hf in range(2):
            pb = banks[hf]
            A(lambda e, pb=pb, hf=hf: e.activation(acc[:, hs(hf)], P[pb][:, :], AF.Identity, bias=par["cab"][:, ct:ct + 1], scale=1.0),
              ["P%d" % pb, "p_cab"], ak)
        if ct < 4:
            a4 = acc.rearrange("p (b s t) -> p b s t", b=4, s=4); x4 = ua.rearrange("p (b s t) -> p b s t", b=4, s=4)

            def tap(o_ap, i_ap, w_ap, wkey):
                V(lambda e: e.scalar_tensor_tensor(o_ap, i_ap, w_ap, o_ap, ALU.mult, ALU.add), [uk, wkey] + ak, ak)

            for d in list(range(-15, 0)) + list(range(1, 16)):
                j = ct * 31 + 15 + d
                e_ = abs(d)
                for sg in range(3):
                    if d > 0:
                        tap(a4[:, :, sg, 64 - d:64], x4[:, :, sg + 1, 0:d], cawNS[:, j:j + 1], "cawNS")
                    else:
                        tap(a4[:, :, sg + 1, 0:e_], x4[:, :, sg, 64 - e_:64], cawNS[:, j:j + 1], "cawNS")
        A(lambda e: e.activation(ua, acc, AF.Copy), ak, [uk])
        for half in range(2):
            A(lambda e, half=half: e.activation(sqs[half][:], acc[:, hs(half)], AF.Square), ak, ["sq%d" % half])
            T(lambda e, half=half: e.matmul(P[4 + half][:], onesb[:], ua[:, hs(half)], start=(ct == 0), stop=(ct == 7)),
              [uk, "onesb"], ["P%d" % (4 + half)])
            pq = 6 if half == 0 else 3
            T(lambda e, half=half, pq=pq: e.matmul(P[pq][:], onesb[:], sqs[half][:], start=(ct == 0), stop=(ct == 7)),
              ["sq%d" % half, "onesb"], ["P%d" % pq])

    def rest_of_layer(l, compute_stats_rs, hn_rhs, hn_keys, mod):
        def zb_cons(ft, half, ps, pk):
            A(lambda e: e.activation(sqs[half][:], ps, AF.Silu), [pk], ["sq%d" % half])
            V(lambda e: e.tensor_tensor(ofv[:, ft, hs(half)], ofv[:, ft, hs(half)], sqs[half][:], ALU.mult),
              ["of", "sq%d" % half], ["of"])
        proj(w_in_d[l], COL["z_b"], 1024, KT, hn_rhs, hn_keys, zb_cons)

        def glu_cons(ft, half, ps, pk):
            A(lambda e: e.activation(bigv[:, 16 + ft, hs(half)], ps, AF.Sigmoid), [pk], ["big%d" % (16 + ft)])
        proj(w_in_d[l], COL["a_glu"], 1024, KT, hn_rhs, hn_keys, glu_cons)

        def val_cons(ft, half, ps, pk):
            V(lambda e: e.tensor_tensor(bigv[:, 16 + ft, hs(half)], bigv[:, 16 + ft, hs(half)], ps, ALU.mult),
              [pk, "big%d" % (16 + ft)], ["big%d" % (16 + ft)])
        proj(w_in_d[l], COL["a_val"], 1024, KT, hn_rhs, hn_keys, val_cons)

        for ct in range(8):
            conv_tile(l, ct)
            if l + 1 < n_layers:
                for gg in range(3 * ct, 3 * ct + 3):
                    if gg % 5 < 2:
                        mod_group(l + 1, gg)
                    else:
                        mod_group0(gg, l + 1, bi=gg % 5 - 1)
        if l + 1 < n_layers:
            mod_finish(l + 1, 0)
            mod_finish(l + 1, 1)
        for half in range(2):
            pq = 6 if half == 0 else 3
            A(lambda e, half=half: e.activation(mean[:, hs(half)], P[4 + half][:], AF.Identity, bias=0.0, scale=1.0 / 1024),
              ["P%d" % (4 + half)], mean_keys)
            A(lambda e, half=half: e.activation(tmpB[:], P[4 + half][:], AF.Square, scale=1.0 / 1024),
              ["P%d" % (4 + half)], ["tmpB"])
            V(lambda e, pq=pq: e.scalar_tensor_tensor(tmpB[:], P[pq][:], 1.0 / 1024, tmpB[:], ALU.mult, ALU.subtract),
              ["P%d" % pq, "tmpB"], ["tmpB"])
            A(lambda e, half=half: e.activation(lrs[:, hs(half)], tmpB[:], AF.Ln, bias=epsc[:, 0:1], scale=1.0),
              ["tmpB", "epsc"], lrs_keys)
            A(lambda e, half=half: e.activation(lrs[:, hs(half)], lrs[:, hs(half)], AF.Exp, scale=-0.5), lrs_keys, lrs_keys)
        for ct in range(8):
            uk = "big%d" % (16 + ct)
            V(lambda e, ct=ct: e.tensor_tensor(cttmp, bigv[:, 16 + ct, :], mean, ALU.subtract), [uk] + mean_keys, cttmp_keys)
            V(lambda e: e.tensor_tensor(cttmp, cttmp, lrs, ALU.mult), cttmp_keys + lrs_keys, cttmp_keys)
            A(lambda e, ct=ct: e.activation(bigv[:, 16 + ct, :], cttmp, AF.Silu, bias=par["lnb"][:, ct:ct + 1],
                                            scale=par["lng"][:, ct:ct + 1]), cttmp_keys + ["p_lnb", "p_lng"], [uk])

        def az_cons(ft, half, ps, pk):
            A(lambda e: e.activation(sqs[half][:], ps, AF.Silu), [pk], ["sq%d" % half])
            V(lambda e: e.tensor_tensor(bigv[:, 16 + ft, hs(half)], bigv[:, 16 + ft, hs(half)], sqs[half][:], ALU.mult),
              ["big%d" % (16 + ft), "sq%d" % half], ["big%d" % (16 + ft)])
        proj(w_in_d[l], COL["a_z"], 1024, KT, hn_rhs, hn_keys, az_cons)

        sgbv = sgb.rearrange("p (f t) -> p f t", f=2)
        pa_rhs = lambda kt, half: bigv[:, 16 + kt, hs(half)]
        pa_keys = lambda kt: ["big%d" % (16 + kt)]
        pb_rhs = lambda kt, half: ofv[:, kt, hs(half)]
        pb_keys = lambda kt: ["of"]
        for g in range(8):
            def ga_cons(ft, half, ps, pk, g=g):
                A(lambda e: e.activation(bigv[:, 2 * g + ft, hs(half)], ps, AF.Sigmoid), [pk], ["big%d" % (2 * g + ft)])
            proj(w_in_d[l], COL["gate_a"] + g * 256, 256, KT, hn_rhs, hn_keys, ga_cons)

            def oa_cons(ft, half, ps, pk, g=g):
                V(lambda e: e.tensor_tensor(bigv[:, 2 * g + ft, hs(half)], bigv[:, 2 * g + ft, hs(half)], ps, ALU.mult),
                  [pk, "big%d" % (2 * g + ft)], ["big%d" % (2 * g + ft)])
            proj(w_pa_d[l], g * 256, 256, 8, pa_rhs, pa_keys, oa_cons)

            def gb_cons(ft, half, ps, pk, g=g):
                A(lambda e: e.activation(sgbv[:, ft, hs(half)], ps, AF.Sigmoid), [pk], ["R3a"])
            proj(w_in_d[l], COL["gate_b"] + g * 256, 256, KT, hn_rhs, hn_keys, gb_cons)

            def ob_cons(ft, half, ps, pk, g=g):
                V(lambda e: e.tensor_tensor(tmpB[:], sgbv[:, ft, hs(half)], ps, ALU.mult), [pk, "R3a"], ["tmpB"])
                V(lambda e: e.tensor_tensor(bigv[:, 2 * g + ft, hs(half)], bigv[:, 2 * g + ft, hs(half)], tmpB[:], ALU.add),
                  ["tmpB", "big%d" % (2 * g + ft)], ["big%d" % (2 * g + ft)])
            proj(w_pb_d[l], g * 256, 256, 8, pb_rhs, pb_keys, ob_cons)

        def wo_cons(ft, half, ps, pk):
            V(lambda e: e.scalar_tensor_tensor(xTv[:, ft, hs(half)], ps, mod[:, 32 + ft:33 + ft], xTv[:, ft, hs(half)],
                                               ALU.mult, ALU.add), [pk, "modg%d" % l, "x%d" % ft], ["x%d" % ft])
        proj(w_o_d[l], 0, D, KT, lambda kt, half: bigv[:, kt, hs(half)], lambda kt: ["big%d" % kt], wo_cons)

    def final_norm(compute_stats_rs):
        compute_stats_rs(1.0 / D, epsc[:, 0:1])
        bufs = [(tmpA, "tmpA"), (R3[:, 0:NT], "R3a")]
        for kt in range(KT):
            buf, bk = bufs[kt % 2]
            V(lambda e, kt=kt, buf=buf: e.scalar_tensor_tensor(buf, xTv[:, kt, :], fng[:, kt:kt + 1], rs, ALU.mult, ALU.mult),
              ["x%d" % kt, "fng", "rs"], [bk])
            S.dma("sync", yT_d[kt * 128:(kt + 1) * 128, :], buf, reads=[bk], writes=["yT%d" % kt], semkey="d_y" + bk,
                  out_final=True)

    for l in range(n_layers):
        grp = "par%d" % (l + 1)
        for nm, d_ap in (("caw", caw_d), ("cab", cab_d), ("lng", lng_d),
                         ("lnb", lnb_d), ("cqw", cqw_d), ("hng", hng_d)):
            S.dma("sync", par[nm][:], d_ap[l], writes=["p_" + nm], group=grp)
        S.dma("sync", alog[:], alog_d[l], writes=["alog"], group=grp)
        S.dma("sync", dtb[:], dtb_d[l], writes=["dtb"], group=grp)
        S.dma("gpsimd", wg[:].rearrange("p (k c) -> p k c", k=KT),
              w_in_d[l].rearrange("(k p) c -> p k c", p=128)[:, :, COL["beta"]:COL["beta"] + 32],
              writes=["wg"], semkey="d_wg")
        caw = par["caw"]
        V(lambda e: e.tensor_scalar(cawP[:], caw[:], flg[:, 0:1], None, ALU.mult), ["p_caw", "flg"], ["cawP"])
        V(lambda e: e.tensor_scalar(cawS[:], caw[:], flg[:, 1:2], None, ALU.mult), ["p_caw", "flg"], ["cawS"])
        V(lambda e: e.tensor_scalar(cawNS[:], caw[:], flg[:, 3:4], None, ALU.mult), ["p_caw", "flg"], ["cawNS"])
        cq3 = par["cqw"][:].rearrange("p (f j) -> p f j", j=3)
        V(lambda e: e.tensor_scalar(cqwL[:], cq3[:, :, 0], flg[:, 2:3], None, ALU.mult), ["p_cqw", "flg"], ["cqwL"])
        V(lambda e: e.tensor_scalar(cqwR[:], cq3[:, :, 2], flg[:, 2:3], None, ALU.mult), ["p_cqw", "flg"], ["cqwR"])
        A(lambda e: e.activation(negA[:], alog[:], AF.Exp), ["alog"], ["negA"])
        V(lambda e: e.tensor_scalar(negA[:], negA[:], -1.0, None, ALU.mult), ["negA"], ["negA"])

        mod = mods[l]; modA = modAs[l]
        if l == 0:
            for g in range(24):
                mod_group0(g)
            mod_finish(0, 0)
            mod_finish(0, 1)

        def compute_stats_rs(scale, bias_ap):
            for half in range(2):
                for kt in range(KT):
                    sq = sqs[kt % 2]
                    A(lambda e, sq=sq, kt=kt, half=half: e.activation(sq[:], xTv[:, kt, hs(half)], AF.Square),
                      ["x%d" % kt], ["sq%d" % (kt % 2)])
                    T(lambda e, sq=sq, kt=kt: e.matmul(P[5][:], onesb[:], sq[:], start=(kt == 0), stop=(kt == KT - 1)),
                      ["sq%d" % (kt % 2), "onesb"], ["P5"])
                A(lambda e, half=half: e.activation(rs[:, hs(half)], P[5][:], AF.Ln, bias=bias_ap, scale=scale),
                  ["P5", "epsc"], ["rs"])
                A(lambda e, half=half: e.activation(rs[:, hs(half)], rs[:, hs(half)], AF.Exp, scale=-0.5), ["rs"], ["rs"])

        def compute_hn(mod=mod, modA=modA, l=l):
            compute_stats_rs(1.0 / D, epsc[:, 0:1])
            for kt in range(KT):
                V(lambda e, kt=kt: e.scalar_tensor_tensor(tmpA, xTv[:, kt, :], modA[:, kt:kt + 1], rs,
                                                          ALU.mult, ALU.mult),
                  ["x%d" % kt, "modA%d" % l, "rs"], ["tmpA"])
                A(lambda e, kt=kt: e.activation(hnT[:, kt, :], tmpA, AF.Identity, bias=mod[:, kt:kt + 1], scale=1.0),
                  ["tmpA", "mod%d" % l], ["hn%d" % kt])

        hn_rhs = lambda kt, half: hnT[:, kt, hs(half)]
        hn_keys = lambda kt: ["hn%d" % kt]
        compute_hn()

        def qkv_consumer(base_ft):
            def cons(ft, half, ps, pk):
                f = base_ft + ft
                A(lambda e: e.activation(pre[:, 1 + half * 512:1 + (half + 1) * 512], ps, AF.Copy), [pk], ["R3a"])
                if half == 1:
                    cw = par["cqw"]
                    V(lambda e: e.tensor_scalar(tmpA, pre[:, 1:NT + 1], cw[:, 3 * f + 1:3 * f + 2], None, ALU.mult),
                      ["R3a", "p_cqw"], ["tmpA"])
                    V(lambda e: e.scalar_tensor_tensor(tmpA, pre[:, 0:NT], cw[:, 3 * f:3 * f + 1], tmpA,
                                                       ALU.mult, ALU.add), ["R3a", "p_cqw", "tmpA"], ["tmpA"])
                    V(lambda e: e.scalar_tensor_tensor(tmpA, pre[:, 2:NT + 2], cw[:, 3 * f + 2:3 * f + 3], tmpA,
                                                       ALU.mult, ALU.add), ["R3a", "p_cqw", "tmpA"], ["tmpA"])
                    tv = tmpA.rearrange("p (s t) -> p s t", s=4)
                    pv = pre[:, 1:NT + 1].rearrange("p (s t) -> p s t", s=4)
                    V(lambda e: e.scalar_tensor_tensor(tv[:, 1:4, 0:1], pv[:, 0:3, 255:256], cqwL[:, f:f + 1],
                                                       tv[:, 1:4, 0:1], ALU.mult, ALU.add),
                      ["R3a", "cqwL", "tmpA"], ["tmpA"])
                    V(lambda e: e.scalar_tensor_tensor(tv[:, 0:3, 255:256], pv[:, 1:4, 0:1], cqwR[:, f:f + 1],
                                                       tv[:, 0:3, 255:256], ALU.mult, ALU.add),
                      ["R3a", "cqwR", "tmpA"], ["tmpA"])
                    A(lambda e: e.activation(bigv[:, f, :], tmpA, AF.Silu), ["tmpA"], ["big%d" % f])
            return cons

        V(lambda e: e.memset(pre[:, 0:1], 0.0), [], ["R3a"])
        V(lambda e: e.memset(pre[:, NT + 1:NT + 2], 0.0), ["R3a"], ["R3a"])
        proj(w_in_d[l], COL["q"], 3072, KT, hn_rhs, hn_keys, qkv_consumer(0))

        wg5 = wg[:].rearrange("p (k a d e) -> p k a d e", k=KT, a=2, d=2, e=8)
        for s_ in range(NCH):
            for hf in range(2):
                c = s_ if hf == 0 else NCH - 1 - s_
                for kt in range(KT):
                    T(lambda e, c=c, kt=kt, hf=hf, s_=s_: e.matmul(
                        P[4][hf * 64:hf * 64 + 64, s_ * 16:(s_ + 1) * 16].rearrange("p (a e) -> p a e", a=2),
                        hnT[:, kt, c * C:(c + 1) * C], wg5[:, kt, :, hf, :], start=(kt == 0), stop=(kt == KT - 1),
                        tile_position=(0, 64 * hf)), ["hn%d" % kt, "wg"], ["P4"])
        p4v = P[4][:, 0:NCH * 16].rearrange("p (s a e) -> p s a e", s=NCH, a=2)
        gTMv = gTM[:].rearrange("p (c g) -> p c g", g=8)
        bTMv = bTM[:].rearrange("p (c g) -> p c g", g=8)
        A(lambda e: e.activation(bTMv, p4v[:, :, 0, :], AF.Sigmoid), ["P4"], ["bTM"])
        V(lambda e: e.tensor_tensor(gTMv, p4v[:, :, 1, :], bc(dtb[:], [128, NCH, 8], 1), ALU.add), ["P4", "dtb"], ["gTM"])
        A(lambda e: e.activation(gTM[:], gTM[:], AF.Exp), ["gTM"], ["gTM"])
        S.op("scalar", lambda e: e.activation(gTM[:], gTM[:], AF.Ln, bias=epsc[:, 2:3], scale=1.0), ["gTM", "epsc"], ["gTM"], strict=True)
        V(lambda e: e.tensor_tensor(gTMv, gTMv, bc(negA[:], [128, NCH, 8], 1), ALU.mult), ["gTM", "negA"], ["gTM"])

        S.barrier()
        deltanet(l)
        S.barrier()
        compute_hn()
        rest_of_layer(l, compute_stats_rs, hn_rhs, hn_keys, mod)
    final_norm(compute_stats_rs)
    S.finish()
    return nc


_PROG = {}
_DEBUG = False
_LAST = {}


def _consts():
    c = np.zeros((128, 6 * 128), np.float32)
    c[:, 0:128] = np.eye(128, dtype=np.float32)
    r = np.arange(64)[:, None]; q = np.arange(64)[None, :]
    c[0:64, 128:192] = (r <= q); c[64:128, 128:192] = (r >= q)
    c[0:64, 192:256] = (r > q); c[64:128, 192:256] = (r < q)
    c[0:64, 256:320] = np.eye(64); c[64:128, 256:320] = np.eye(64)
    return c


def _fm(v, nt):
    return np.ascontiguousarray(np.asarray(v, np.float32).reshape(nt, 128).T)


def kernel(x_prompt, x_sample, state_delta, c, c_ctx, w_mod, b_mod, norm_g, w_in, conv_a_w, conv_a_b,
           ln_a_g, ln_a_b, w_pa, conv_qkv_w, a_log, dt_bias, head_norm_g, w_pb, w_o, final_norm_g):
    f32 = np.float32
    if "nc" not in _PROG:
        _PROG["nc"] = build_program(debug=_DEBUG)
    nc = _PROG["nc"]
    x_prompt = np.asarray(x_prompt, f32); x_sample = np.asarray(x_sample, f32)
    state_delta = np.asarray(state_delta, f32)
    shared = {
        "cst": _consts(),
        "w_mod": np.ascontiguousarray(np.asarray(w_mod, f32)),
        "b_mod": np.stack([_fm(b_mod[l], 48) for l in range(DEPTH)]),
        "norm_g": np.stack([_fm(norm_g[l], KT) for l in range(DEPTH)]),
        "w_in": np.ascontiguousarray(np.asarray(w_in, f32)),
        "caw": np.stack([np.ascontiguousarray(np.asarray(conv_a_w[l], f32).T.reshape(8, 128, 31).transpose(1, 0, 2)).reshape(128, 248)
                         for l in range(DEPTH)]),
        "cab": np.stack([_fm(conv_a_b[l], 8) for l in range(DEPTH)]),
        "lng": np.stack([_fm(ln_a_g[l], 8) for l in range(DEPTH)]),
        "lnb": np.stack([_fm(ln_a_b[l], 8) for l in range(DEPTH)]),
        "w_pa": np.ascontiguousarray(np.asarray(w_pa, f32)),
        "cqw": np.stack([np.ascontiguousarray(np.asarray(conv_qkv_w[l], f32).T.reshape(24, 128, 3).transpose(1, 0, 2)).reshape(128, 72)
                         for l in range(DEPTH)]),
        "alog": np.stack([np.repeat(np.asarray(a_log[l], f32).reshape(2, 8), 64, axis=0) for l in range(DEPTH)]),
        "dtb": np.stack([np.repeat(np.asarray(dt_bias[l], f32).reshape(2, 8), 64, axis=0) for l in range(DEPTH)]),
        "hng": np.asarray(head_norm_g, f32).reshape(DEPTH, 128, 1).copy(),
        "w_pb": np.ascontiguousarray(np.asarray(w_pb, f32)),
        "w_o": np.ascontiguousarray(np.asarray(w_o, f32)),
        "fng": _fm(final_norm_g, KT),
    }
    in_maps = []
    for core in range(8):
        m = dict(shared)
        if core < 4:
            xt = x_prompt[4 * core:4 * core + 4].reshape(NT, D)
            m["cv"] = _fm(c_ctx, KT)
            m["s0"] = np.zeros((DEPTH, 2, 128, H * 128), f32)
            m["flg"] = np.ascontiguousarray(np.broadcast_to(np.array([1, 0, -1, 0], f32), (128, 4)))
        else:
            b = core - 4
            xt = x_sample[b]
            m["cv"] = _fm(np.asarray(c, f32)[b], KT)
            m["s0"] = np.ascontiguousarray(state_delta[b].transpose(0, 1, 3, 2, 4)).reshape(DEPTH, 2, 128, H * 128)
            m["flg"] = np.ascontiguousarray(np.broadcast_to(np.array([0, 1, 0, -1], f32), (128, 4)))
        m["xT"] = np.ascontiguousarray(xt.T)
        in_maps.append(m)
    res = run_bass_kernel_spmd(nc, in_maps, core_ids=list(range(8)))
    r = res.results
    _LAST['r'] = r
    y_prompt = np.stack([r[i]["yT"].T.reshape(4, 256, D) for i in range(4)]).reshape(16, 256, D)
    y_sample = np.stack([r[4 + b]["yT"].T for b in range(4)])
    st = np.concatenate([r[i]["st"].transpose(1, 0, 2, 3, 4, 5) for i in range(4)], axis=0)
    return (np.ascontiguousarray(y_prompt, dtype=f32), np.ascontiguousarray(y_sample, dtype=f32),
            np.ascontiguousarray(st, dtype=f32))
```

```python
from contextlib import ExitStack
import numpy as np
import concourse.bass as bass
import concourse.mybir as mybir
from concourse.bass_utils import run_bass_kernel_spmd

F32 = mybir.dt.float32
BF16 = mybir.dt.bfloat16
AF = mybir.ActivationFunctionType
ALU = mybir.AluOpType

SAME_ENGINE_SYNC = False


class Sched:
    ENGS = ["tensor", "vector", "scalar", "gpsimd", "sync"]

    def __init__(self, nc):
        self.nc = nc
        self.stack = ExitStack()
        self.ops = {e: [] for e in self.ENGS}
        self.count = {}
        self.seen = {e: {} for e in self.ENGS}
        self.last_w = {}
        self.readers = {}
        self.sem_names = set(self.ENGS)
        self.final_group = set()
        self.out_sems = set()
        self.barrier_toks = []

    def barrier(self):
        self.barrier_toks = [(k, v) for k, v in self.count.items()]

    def sb(self, name, shape, dt):
        return self.stack.enter_context(self.nc.sbuf_tensor(name, shape, dt))

    def ps(self, name, shape, dt):
        return self.stack.enter_context(self.nc.psum_tensor(name, shape, dt))

    def _deps(self, eng, reads, writes, strict=False):
        toks = list(self.barrier_toks)
        for k in list(reads) + list(writes):
            t = self.last_w.get(k)
            if t is not None:
                toks.append(t)
        for k in writes:
            toks.extend(self.readers.get(k, []))
        waits = {}
        for (s, v) in toks:
            if s == eng and (eng in ("tensor", "sync") or not (SAME_ENGINE_SYNC or strict)):
                continue
            if self.seen[eng].get(s, 0) >= v:
                continue
            waits[s] = max(waits.get(s, 0), v)
        for s, v in waits.items():
            self.seen[eng][s] = v
        return sorted(waits.items())

    def _commit(self, tok, reads, writes):
        for k in writes:
            self.last_w[k] = tok
            self.readers[k] = []
        for k in reads:
            if k not in writes:
                self.readers.setdefault(k, []).append(tok)

    def op(self, eng, fn, reads=(), writes=(), strict=False):
        waits = self._deps(eng, reads, writes, strict)
        self.count[eng] = self.count.get(eng, 0) + 1
        tok = (eng, self.count[eng])
        self.ops[eng].append((waits, fn, (eng, 1)))
        self._commit(tok, reads, writes)

    def dma(self, eng, out, in_, reads=(), writes=(), semkey=None, out_final=False, group=None):
        if group is not None:
            semkey = "g_" + group
            self.final_group.add(semkey)
        if semkey is None:
            semkey = "d_" + str((list(writes) + list(reads))[0])
        self.sem_names.add(semkey)
        if out_final:
            self.out_sems.add(semkey)
        waits = self._deps(eng, reads, writes)
        self.count[semkey] = self.count.get(semkey, 0) + 16
        tok = (semkey, self.count[semkey])
        self.ops[eng].append((waits, lambda e: e.dma_start(out=out, in_=in_), (semkey, 16)))
        self._commit(tok, reads, writes)

    def finish(self):
        nc = self.nc
        sems = {}
        for i, s in enumerate(sorted(self.sem_names)):
            sems[s] = self.stack.enter_context(nc.semaphore("s%d" % i))
        final = dict(self.count)
        ops = self.ops
        fg = self.final_group
        out_sems = sorted(self.out_sems)

        def emit(eng_name):
            def body(e):
                for waits, fn, (isem, iv) in ops[eng_name]:
                    for s, v in waits:
                        if s in fg:
                            v = final[s]
                        e.wait_ge(sems[s], v)
                    ins = fn(e)
                    ins.then_inc(sems[isem], iv)
                if eng_name == "sync":
                    for s in out_sems:
                        e.wait_ge(sems[s], final[s])
            return body

        with nc.Block() as block:
            block.tensor(emit("tensor"))
            block.vector(emit("vector"))
            block.scalar(emit("scalar"))
            block.gpsimd(emit("gpsimd"))
            block.sync(emit("sync"))
        self.stack.close()


D = 2048
NT = 1024
DEPTH = 2
KT = 16
H = 8
C = 64
NCH = NT // C
IN_W = 11296
EPS = 1e-6
NLEV = 5
COL = dict(a_val=0, a_glu=1024, a_z=2048, q=3072, k=4096, v=5120, z_b=6144,
           beta=7168, alpha=7184, gate_a=7200, gate_b=9248)


def build_program(n_layers=DEPTH, debug=False):
    nc = bass.Bass("TRN2", target_bir_lowering=False)
    S = Sched(nc)

    def din(name, shape):
        return nc.dram_tensor(name, shape, F32, kind="ExternalInput").ap()

    xT_d = din("xT", [D, NT])
    cv_d = din("cv", [128, KT])
    s0_d = din("s0", [DEPTH, 2, 128, H * 128])
    flg_d = din("flg", [128, 4])
    cst_d = din("cst", [128, 6 * 128])
    w_mod_d = din("w_mod", [DEPTH, D, 3 * D])
    b_mod_d = din("b_mod", [DEPTH, 128, 48])
    ng_d = din("norm_g", [DEPTH, 128, KT])
    w_in_d = din("w_in", [DEPTH, D, IN_W])
    caw_d = din("caw", [DEPTH, 128, 8 * 31])
    cab_d = din("cab", [DEPTH, 128, 8])
    lng_d = din("lng", [DEPTH, 128, 8])
    lnb_d = din("lnb", [DEPTH, 128, 8])
    w_pa_d = din("w_pa", [DEPTH, 1024, D])
    cqw_d = din("cqw", [DEPTH, 128, 24 * 3])
    alog_d = din("alog", [DEPTH, 128, 8])
    dtb_d = din("dtb", [DEPTH, 128, 8])
    hng_d = din("hng", [DEPTH, 128, 1])
    w_pb_d = din("w_pb", [DEPTH, 1024, D])
    w_o_d = din("w_o", [DEPTH, D, D])
    fng_d = din("fng", [128, KT])
    yT_d = nc.dram_tensor("yT", [D, NT], F32, kind="ExternalOutput").ap()
    st_d = nc.dram_tensor("st", [DEPTH, 4, 2, H, 128, 128], F32, kind="ExternalOutput").ap()

    xT = S.sb("xTs", [128, KT * NT], F32)
    xTv = xT[:].rearrange("p (k t) -> p k t", k=KT)
    big = S.sb("big", [128, 24 * NT], BF16)
    bigv = big[:].rearrange("p (s t) -> p s t", s=24)
    of = S.sb("of", [128, 8 * NT], BF16)
    ofv = of[:].rearrange("p (s t) -> p s t", s=8)
    R1 = S.sb("R1", [128, 8192], F32)
    hnT = R1[:].bitcast(BF16).rearrange("p (k t) -> p k t", k=KT)
    R3 = S.sb("R3", [128, 3080], F32)
    wbs = [S.sb("wb%d" % i, [128, 16 * 256], BF16) for i in range(2)]
    wmf = R3[:, 0:1024]
    cst = S.sb("cst_s", [128, 6 * 128], F32)
    identb = S.sb("identb", [128, 128], BF16)
    onesb = S.sb("onesb", [128, 128], BF16)
    onesf = S.sb("onesf", [128, 128], F32)
    negones = S.sb("negones", [128, 64], F32)
    negtri = S.sb("negtri", [128, 64], F32)
    flg = S.sb("flg_s", [128, 4], F32)
    epsc = S.sb("epsc", [128, 3], F32)
    cvs = S.sb("cvs", [128, KT], F32)
    fng = S.sb("fng_s", [128, KT], F32)
    mods = [S.sb("mod%d" % i, [128, 48], F32) for i in range(DEPTH)]
    modAs = [S.sb("modA%d" % i, [128, KT], F32) for i in range(DEPTH)]
    bmods = [S.sb("bmod%d" % i, [128, 48], F32) for i in range(DEPTH)]
    ngs = [S.sb("ngs%d" % i, [128, KT], F32) for i in range(DEPTH)]
    scb = S.sb("scb", [128, KT], BF16)
    par = {}
    for nm, w in (("caw", 248), ("cab", 8), ("lng", 8), ("lnb", 8),
                  ("cqw", 72), ("hng", 1)):
        par[nm] = S.sb("p_" + nm, [128, w], F32)
    cawP = S.sb("cawP", [128, 248], F32)
    cawS = S.sb("cawS", [128, 248], F32)
    cawNS = S.sb("cawNS", [128, 248], F32)
    cqwL = S.sb("cqwL", [128, 24], F32)
    cqwR = S.sb("cqwR", [128, 24], F32)
    alog = S.sb("alog_s", [128, 8], F32)
    dtb = S.sb("dtb_s", [128, 8], F32)
    negA = S.sb("negA", [128, 8], F32)
    gTM = S.sb("gTM", [128, NCH * 8], F32)
    bTM = S.sb("bTM", [128, NCH * 8], F32)
    wg = S.sb("wg", [128, KT * 32], BF16)
    St = R3[:, 1032:2056]
    tmpA = R3[:, 1032:2056]
    tmpB = S.sb("tmpB", [128, 512], F32)
    rs = R3[:, 2056:3080]
    sqs = [S.sb("sq%d" % i, [128, 512], BF16) for i in range(2)]
    sgb = R3[:, 0:1024].bitcast(BF16)
    pre = R3[:, 0:NT + 2]

    P = [S.ps("P%d" % i, [128, 512], F32) for i in range(7)]
    PT = S.ps("PT", [128, 1024], BF16)

    def T(fn, r, w): S.op("tensor", fn, r, w)
    def V(fn, r, w): S.op("vector", fn, r, w)
    def A(fn, r, w): S.op("scalar", fn, r, w)
    def G(fn, r, w): S.op("gpsimd", fn, r, w)

    S.dma("sync", cst[:], cst_d, writes=["cst"], group="par0")
    S.dma("sync", flg[:], flg_d, writes=["flg"], group="par0")
    S.dma("sync", cvs[:], cv_d, writes=["cvs"], group="par0")
    S.dma("sync", fng[:], fng_d, writes=["fng"], group="par0")
    S.dma("gpsimd", identb[:], cst_d[:, 0:128], writes=["identb"], group="par0c")
    for kt in range(KT):
        S.dma("sync", xTv[:, kt, :], xT_d[kt * 128:(kt + 1) * 128, :], writes=["x%d" % kt], group="xin")
    for i in range(DEPTH):
        S.dma("sync", bmods[i][:], b_mod_d[i], writes=["bmod%d" % i], group="par0")
        S.dma("sync", ngs[i][:], ng_d[i], writes=["ngs%d" % i], group="par0")
    A(lambda e: e.activation(scb[:], cvs[:], AF.Silu), ["cvs"], ["scb"])
    V(lambda e: e.memset(onesb[:], 1.0), [], ["onesb"])
    V(lambda e: e.memset(onesf[:], 1.0), [], ["onesf"])
    V(lambda e: e.memset(negones[:], -1.0), [], ["negones"])
    V(lambda e: e.memset(epsc[:, 0:1], EPS), [], ["epsc"])
    V(lambda e: e.memset(epsc[:, 1:2], 128.0 * EPS), ["epsc"], ["epsc"])
    V(lambda e: e.memset(epsc[:, 2:3], 1.0), ["epsc"], ["epsc"])
    tri2 = cst[:, 128:192]
    maskS2 = cst[:, 192:256]
    ident2 = cst[:, 256:320]
    V(lambda e: e.tensor_scalar(negtri[:], tri2, -1.0, None, ALU.mult), ["cst"], ["negtri"])

    def bc(ap, shape, axis):
        return ap.unsqueeze(axis).to_broadcast(shape)

    wctr = [0]

    def load_w(src3, ktn, ncols):
        i = wctr[0] % 2
        wctr[0] += 1
        key = "wb%d" % i
        dst = wbs[i][:, 0:ktn * ncols].rearrange("p (k c) -> p k c", k=ktn)
        S.dma("gpsimd", dst, src3, writes=[key])
        return dst, key

    pctr = [0]

    def proj(w2d, c0, ncols_total, ktn, rhs_fn, rhs_keys_fn, consumer, gcols=256, after_group=None):
        wv = w2d.rearrange("(k p) c -> p k c", p=128)
        for g0 in range(0, ncols_total, gcols):
            gc_ = min(gcols, ncols_total - g0)
            wt, wkey = load_w(wv[:, :, c0 + g0:c0 + g0 + gc_], ktn, gc_)
            for f0 in range(0, gc_, 128):
                fw = min(128, gc_ - f0)
                ft = (g0 + f0) // 128
                for half in range(2):
                    pi = pctr[0] % 4
                    pctr[0] += 1
                    pk = "P%d" % pi
                    for kt in range(ktn):
                        T(lambda e, kt=kt, pi=pi, f0=f0, fw=fw, half=half, wt=wt: e.matmul(
                            P[pi][0:fw, :], wt[:, kt, f0:f0 + fw], rhs_fn(kt, half),
                            start=(kt == 0), stop=(kt == ktn - 1)),
                          [wkey] + rhs_keys_fn(kt), [pk])
                    consumer(ft, half, P[pi][0:fw, :], pk)
            if after_group is not None:
                after_group(g0 // gcols)

    def hs(half):
        return slice(half * 512, (half + 1) * 512)


    PTf = PT[:].bitcast(F32)

    def mod_group(l, g):
        wv = w_mod_d[l].rearrange("(k p) c -> p k c", p=128)
        wt, wkey = load_w(wv[:, :, g * 256:(g + 1) * 256], KT, 256)
        for f in range(2):
            n = g * 2 + f
            for kt in range(KT):
                T(lambda e, kt=kt, n=n, f=f, wt=wt: e.matmul(PTf[:, n:n + 1], wt[:, kt, f * 128:(f + 1) * 128], scb[:, kt:kt + 1],
                                                            start=(kt == 0), stop=(kt == KT - 1)), [wkey, "scb"], ["PT"])

    def mod_group0(g, l=0, bi=None):
        wv = w_mod_d[l].rearrange("(k p) c -> p k c", p=128)
        if bi is None:
            bi = g % 6
        wt = big[:, bi * 4096:(bi + 1) * 4096].rearrange("p (k c) -> p k c", k=KT)
        bkeys = ["big%d" % (4 * bi + j) for j in range(4)]
        S.dma("gpsimd", wt, wv[:, :, g * 256:(g + 1) * 256], writes=bkeys, semkey="d_bigw%d" % bi)
        for f in range(2):
            n = g * 2 + f
            for kt in range(KT):
                T(lambda e, kt=kt, n=n, f=f, wt=wt: e.matmul(PTf[:, n:n + 1], wt[:, kt, f * 128:(f + 1) * 128], scb[:, kt:kt + 1],
                                                            start=(kt == 0), stop=(kt == KT - 1)), bkeys + ["scb"], ["PT"])

    def mod_finish(l, part):
        if part == 0:
            V(lambda e: e.tensor_tensor(mods[l][:, 0:32], PTf[:, 0:32], bmods[l][:, 0:32], ALU.add), ["PT", "bmod%d" % l], ["mod%d" % l])
            S.op("vector", lambda e: e.scalar_tensor_tensor(modAs[l][:], mods[l][:, 16:32], 1.0, ngs[l][:], ALU.add, ALU.mult),
                 ["mod%d" % l, "ngs%d" % l], ["modA%d" % l], strict=True)
        else:
            V(lambda e: e.tensor_tensor(mods[l][:, 32:48], PTf[:, 32:48], bmods[l][:, 32:48], ALU.add), ["PT", "bmod%d" % l], ["modg%d" % l])

    def f32v(R, off, n):
        return R[:, off:off + n]

    def b16v(R, off, n):
        return R[:, off:off + n // 2].bitcast(BF16)

    def h3(ap, h=H):
        return ap.rearrange("p (h t) -> p h t", h=h)

    W0 = wbs[0][:].bitcast(F32)
    W1 = wbs[1][:].bitcast(F32)
    gB = f32v(R1, 0, 512); gTri = f32v(R1, 512, 512); Em = f32v(R1, 1024, 512); ETm = f32v(R1, 1536, 512)
    Pm = f32v(R1, 2048, 512); EGs = [f32v(R1, 2560, 512), f32v(R1, 3072, 512)]
    u_ = f32v(R1, 3584, 1024); tmin = f32v(R1, 3584, 512); osum = f32v(R1, 4608, 512); rstd = f32v(R1, 5120, 512)
    MT0 = b16v(R1, 5632, 512)
    AA = [b16v(R1, 5888, 512), b16v(R1, 6144, 512)]
    AT = [b16v(R1, 6400, 512), b16v(R1, 6656, 512)]
    Pb = b16v(R1, 6912, 512); Plo = b16v(R1, 7168, 512); Rb = b16v(R1, 7424, 512); PbT = b16v(R1, 7680, 512)
    TTb = b16v(R1, 7936, 512)
    TTbg = b16v(W0, 0, 512); qkT = b16v(W0, 256, 512)
    wTs = [b16v(W0, 512, 512), b16v(W0, 768, 512)]
    qdTs = [b16v(W0, 1024, 512), b16v(W0, 1280, 512)]
    sqo = b16v(W0, 1536, 512)
    gcsL = [f32v(W0, 1792 + 64 * i, 8) for i in range(2)]; egcL = [f32v(W0, 1800 + 64 * i, 8) for i in range(2)]
    ejL = [f32v(W0, 1808 + 64 * i, 8) for i in range(2)]; bgL = [f32v(W0, 1816 + 64 * i, 8) for i in range(2)]
    edecL = [f32v(W0, 1824 + 64 * i, 16) for i in range(2)]
    nbm = gB
    kTM = b16v(W1, 0, 1024); vTM = b16v(W1, 512, 1024); vn = b16v(W1, 1024, 1024); vns = b16v(W1, 1536, 1024)
    Sts = [R3[:, 0:1024], R3[:, 1032:2056]]
    Sbs = [R3[:, 2056:2568].bitcast(BF16), R3[:, 2568:3080].bitcast(BF16)]
    HR = [slice(0, 64), slice(64, 128)]

    def deltanet(l):
        for f in range(16):
            for half in range(2):
                sq = sqs[half]
                pb = 5 + half
                V(lambda e, f=f, half=half, sq=sq: e.tensor_tensor(sq[:], bigv[:, f, hs(half)], bigv[:, f, hs(half)], ALU.mult),
                  ["big%d" % f], ["sq%d" % half])
                T(lambda e, sq=sq, pb=pb: e.matmul(P[pb][:], onesb[:], sq[:], start=True, stop=True),
                  ["sq%d" % half, "onesb"], ["P%d" % pb])
                if f < 8:
                    A(lambda e, half=half, pb=pb: e.activation(rs[:, hs(half)], P[pb][:], AF.Ln, bias=epsc[:, 1:2], scale=128.0),
                      ["P%d" % pb, "epsc"], ["rs"])
                else:
                    A(lambda e, half=half, pb=pb: e.activation(rs[:, hs(half)], P[pb][:], AF.Ln, bias=epsc[:, 0:1], scale=1.0),
                      ["P%d" % pb, "epsc"], ["rs"])
                A(lambda e, half=half: e.activation(rs[:, hs(half)], rs[:, hs(half)], AF.Exp, scale=-0.5), ["rs"], ["rs"])
            V(lambda e, f=f: e.tensor_tensor(bigv[:, f, :], bigv[:, f, :], rs, ALU.mult), ["big%d" % f, "rs"], ["big%d" % f])

        gTMv = gTM[:].rearrange("p (c g) -> p c g", g=8)
        bTMv = bTM[:].rearrange("p (c g) -> p c g", g=8)
        qk_keys = ["big%d" % i for i in range(16)]
        k_keys = ["big%d" % i for i in range(8, 16)]
        v_keys = ["big%d" % i for i in range(16, 24)]
        q_keys = ["big%d" % i for i in range(0, 8)]
        for d_ in range(2):
            S.dma("sync", Sts[d_], s0_d[l, d_], writes=["St%d" % d_])
            A(lambda e, d_=d_: e.activation(Sbs[d_], Sts[d_], AF.Copy), ["St%d" % d_, "rs"], ["Sb%d" % d_, "rs"])

        def dn_gates(s_):
            pr = s_ % 2
            gcs, egc, ej, bg, edec = gcsL[pr], egcL[pr], ejL[pr], bgL[pr], edecL[pr]
            kp = "_%d" % pr
            g8 = gTMv[:, s_, :]
            b8 = bTMv[:, s_, :]
            gTri3 = h3(gTri)
            V(lambda e: e.tensor_tensor(gTri3, bc(g8, [128, 8, 64], 2), bc(tri2, [128, 8, 64], 1), ALU.mult), ["gTM", "cst"], ["gTri"])
            for hf in range(2):
                r = HR[hf]
                tpd = (64 * hf, 64 * hf)
                T(lambda e, r=r, tpd=tpd: e.matmul(P[3][r, 0:8], tri2[r, :], g8[r, :], start=True, stop=True, tile_position=tpd),
                  ["cst", "gTM"], ["P3"])
                T(lambda e, r=r, hf=hf: e.matmul(P[3][:, 8 + 8 * hf:16 + 8 * hf], onesf[r, :], g8[r, :], start=True, stop=True,
                                                 tile_position=(64 * hf, 0)), ["onesf", "gTM"], ["P3"])
                T(lambda e, r=r, tpd=tpd: e.matmul(P[3][r, 24:32], onesf[r, 0:64], g8[r, :], start=True, stop=True, tile_position=tpd),
                  ["onesf", "gTM"], ["P3"])
            for hf in range(2):
                r = HR[hf]
                pg = 2 if hf == 0 else 6
                T(lambda e, r=r, hf=hf, pg=pg: e.matmul(P[pg][:, :], onesf[r, :], gTri[r, :], start=True, stop=True,
                                                        tile_position=(64 * hf, 0)), ["onesf", "gTri"], ["P%d" % pg])
            G(lambda e: e.tensor_tensor(h3(nbm), bc(maskS2, [128, 8, 64], 1), bc(b8, [128, 8, 64], 2), ALU.mult),
              ["cst", "bTM"], ["nbm"])
            A(lambda e: e.activation(gcs, P[3][:, 0:8], AF.Copy), ["P3"], ["gcs" + kp])
            for hf in range(2):
                r = HR[hf]
                pg = 2 if hf == 0 else 6
                V(lambda e, r=r, pg=pg: e.tensor_tensor(h3(tmin)[r], bc(gcs[r, :], [64, 8, 64], 2), h3(P[pg][r, :]), ALU.subtract),
                  ["gcs" + kp, "P%d" % pg], ["tmin"])
                V(lambda e, r=r, pg=pg: e.tensor_tensor(h3(osum)[r], h3(P[pg][r, :]), bc(gcs[r, :], [64, 8, 64], 2), ALU.subtract),
                  ["gcs" + kp, "P%d" % pg], ["osum"])
            V(lambda e: e.tensor_scalar(tmin, tmin, 0.0, None, ALU.min), ["tmin"], ["tmin"])
            A(lambda e: e.activation(Em, tmin, AF.Exp), ["tmin"], ["Em"])
            V(lambda e: e.tensor_scalar(osum, osum, 0.0, None, ALU.min), ["osum"], ["osum"])
            A(lambda e: e.activation(ETm, osum, AF.Exp), ["osum"], ["ETm"])
            A(lambda e: e.activation(EGs[0], P[2][:, :], AF.Exp), ["P2"], ["EG0"])
            A(lambda e: e.activation(EGs[1], P[6][:, :], AF.Exp), ["P6"], ["EG1"])
            A(lambda e: e.activation(egc, P[3][:, 0:8], AF.Exp), ["P3"], ["egc" + kp])
            A(lambda e: e.activation(edec, P[3][:, 8:24], AF.Exp), ["P3"], ["edec" + kp])
            if s_ % 4 == 0 and s_ > 0:
                V(lambda e: e.tensor_scalar(edec, edec, flg[:, 1:2], None, ALU.mult), ["edec" + kp, "flg"], ["edec" + kp])
            V(lambda e: e.tensor_tensor(ej, P[3][:, 24:32], gcs, ALU.subtract), ["P3", "gcs" + kp], ["ej" + kp])
            A(lambda e: e.activation(ej, ej, AF.Exp), ["ej" + kp], ["ej" + kp])
            V(lambda e: e.tensor_tensor(bg, b8, egc, ALU.mult), ["bTM", "egc" + kp], ["bg" + kp])

        def dn_step(s_):
            toks = [slice(s_ * C, (s_ + 1) * C), slice((NCH - 1 - s_) * C, (NCH - s_) * C)]
            g8 = gTMv[:, s_, :]
            b8 = bTMv[:, s_, :]
            pr = s_ % 2
            gcs, egc, ej, bg, edec = gcsL[pr], egcL[pr], ejL[pr], bgL[pr], edecL[pr]
            kp = "_%d" % pr
            Em3, ETm3, Pm3 = h3(Em), h3(ETm), h3(Pm)
            for hf in range(2):
                r = HR[hf]; tok = toks[hf]
                for h in range(H):
                    T(lambda e, h=h, r=r, tok=tok, hf=hf: e.matmul(P[4][r, h * 64:(h + 1) * 64], bigv[:, 8 + h, tok], bigv[:, 8 + h, tok],
                                                                  start=True, stop=True, tile_position=(0, 64 * hf)), k_keys, ["P4"])
            V(lambda e: e.scalar_tensor_tensor(Em, Em, 1.0, nbm, ALU.min, ALU.mult), ["Em", "nbm"], ["Em"])
            V(lambda e: e.scalar_tensor_tensor(MT0, P[4][:, :], -1.0, Em, ALU.mult, ALU.mult), ["P4", "Em"], ["MT0"])

            def mm_hh(out_bank, lhs, rhs, rkeys, wkey, rhs2=None):
                for h in range(H):
                    for hf in range(2):
                        r = HR[hf]
                        T(lambda e, h=h, r=r, hf=hf: e.matmul(out_bank[r, h * 64:(h + 1) * 64], h3(lhs)[r, h, :], h3(rhs)[r, h, :],
                                                             start=True, stop=(rhs2 is None), tile_position=(64 * hf, 64 * hf)), rkeys, [wkey])
                        if rhs2 is not None:
                            T(lambda e, h=h, r=r, hf=hf: e.matmul(out_bank[r, h * 64:(h + 1) * 64], h3(lhs)[r, h, :], h3(rhs2)[r, h, :],
                                                                 start=False, stop=True, tile_position=(64 * hf, 64 * hf)), rkeys, [wkey])

            def tr_hh(src, skey):
                for h in range(H):
                    for hf in range(2):
                        r = HR[hf]
                        T(lambda e, h=h, r=r, hf=hf: e.transpose(PT[r, h * 64:(h + 1) * 64], h3(src)[r, h, :], identb[r, r],
                                                                tile_position=(64 * hf, 64 * hf)), [skey, "identb"], ["PT"])

            tr_hh(MT0, "MT0")
            A(lambda e: e.activation(AA[0], PT[:, 0:512], AF.Copy), ["PT"], ["AA0"])
            V(lambda e: e.tensor_tensor(h3(Pb), h3(AA[0]), bc(ident2, [128, 8, 64], 1), ALU.add), ["AA0", "cst"], ["Pb"])
            def kv_tr(base, keys, dst, dkey, eng):
                for hf in range(2):
                    r = HR[hf]; tok = toks[hf]
                    for h in range(H):
                        T(lambda e, h=h, r=r, tok=tok, hf=hf: e.transpose(PT[r, h * 128:(h + 1) * 128], bigv[:, base + h, tok],
                                                                         identb[:, :], tile_position=(0, 64 * hf)),
                          keys + ["identb"], ["PT"])

            def kv_cp(dst, dkey, eng):
                if eng == "scalar":
                    A(lambda e: e.activation(dst, PT[:, :], AF.Copy), ["PT"], [dkey])
                else:
                    V(lambda e: e.tensor_copy(dst, PT[:, :]), ["PT"], [dkey])

            def qk_all():
                for hf in range(2):
                    r = HR[hf]; tok = toks[hf]
                    for h in range(H):
                        T(lambda e, h=h, r=r, tok=tok, hf=hf: e.matmul(P[2][r, h * 64:(h + 1) * 64], bigv[:, 8 + h, tok], bigv[:, h, tok],
                                                                      start=True, stop=True, tile_position=(0, 64 * hf)), qk_keys, ["P2"])
                V(lambda e: e.scalar_tensor_tensor(ETm3, ETm3, 1.0, bc(tri2, [128, 8, 64], 1), ALU.min, ALU.mult), ["ETm", "cst"], ["ETm"])
                V(lambda e: e.tensor_tensor(qkT, P[2][:, :], ETm, ALU.mult), ["P2", "ETm"], ["qkT"])
                V(lambda e: e.tensor_tensor(h3(qdTs[0]), bigv[:, 0:8, toks[0]], h3(EGs[0]), ALU.mult), q_keys + ["EG0"], ["qdT0"])
                V(lambda e: e.tensor_tensor(h3(qdTs[1]), bigv[:, 0:8, toks[1]], h3(EGs[1]), ALU.mult), q_keys + ["EG1"], ["qdT1"])

            def prod(atn, atnk):
                mm_hh(P[4], atn, Pb, [atnk, "Pb"], "P4")
                V(lambda e: e.tensor_tensor(Pb, Pb, P[4][:, :], ALU.add), ["Pb", "P4"], ["Pb"])

            cur = 0
            pending = None
            for k in range(2, NLEV + 1):
                nxt = 1 - cur
                atc, atk = (MT0, "MT0") if k == 2 else (AT[cur], "AT%d" % cur)
                mm_hh(P[5], AA[cur], atc, ["AA%d" % cur, atk], "P5")
                if k < NLEV:
                    mm_hh(P[6], atc, AA[cur], ["AA%d" % cur, atk], "P6")
                if k == 2:
                    kv_tr(8, k_keys, kTM, "kTM", "scalar")
                elif k == 3:
                    kv_tr(16, v_keys, vTM, "vTM", "vector")
                A(lambda e, nxt=nxt: e.activation(AT[nxt], P[5][:, :], AF.Copy), ["P5"], ["AT%d" % nxt])
                if k < NLEV:
                    V(lambda e, nxt=nxt: e.tensor_copy(AA[nxt], P[6][:, :]), ["P6"], ["AA%d" % nxt])
                if k == 2:
                    kv_cp(kTM, "kTM", "scalar")
                elif k == 3:
                    kv_cp(vTM, "vTM", "vector")
                if pending is not None:
                    prod(*pending)
                if k == 4:
                    qk_all()
                if k == NLEV and s_ + 1 < NCH:
                    dn_gates(s_ + 1)
                pending = (AT[nxt], "AT%d" % nxt)
                cur = nxt
            prod(*pending)
            mm_hh(P[4], MT0, Pb, ["MT0", "Pb"], "P4")
            V(lambda e: e.tensor_tensor(tmin, P[4][:, :], Pb, ALU.subtract), ["P4", "Pb"], ["tmin"])
            V(lambda e: e.tensor_tensor(h3(Rb), h3(tmin), bc(ident2, [128, 8, 64], 1), ALU.add), ["tmin", "cst"], ["Rb"])
            tr_hh(Pb, "Pb")
            A(lambda e: e.activation(PbT, PT[:, 0:512], AF.Copy), ["PT"], ["PbT"])
            mm_hh(P[5], PbT, Rb, ["PbT", "Rb"], "P5")
            V(lambda e: e.tensor_tensor(Pm, Pb, P[5][:, :], ALU.add), ["Pb", "P5"], ["Pm"])
            V(lambda e: e.tensor_tensor(h3(TTb), Pm3, bc(b8, [128, 8, 64], 2), ALU.mult), ["Pm", "bTM"], ["TTb"])
            V(lambda e: e.tensor_tensor(h3(TTbg), Pm3, bc(bg, [128, 8, 64], 2), ALU.mult), ["Pm", "bg" + kp], ["TTbg"])
            for h in range(H):
                for hf in range(2):
                    r = HR[hf]
                    pp = 5 if h < 4 else 6
                    T(lambda e, h=h, pp=pp, r=r, hf=hf: e.matmul(P[pp][r, (h % 4) * 128:(h % 4 + 1) * 128], h3(TTb)[r, h, :],
                                                                h3(vTM)[r, h, :], start=True, stop=True,
                                                                tile_position=(64 * hf, 64 * hf)), ["TTb", "vTM"], ["P%d" % pp])
            A(lambda e: e.activation(u_[:, 0:512], P[5][:, :], AF.Copy), ["P5", "tmin"], ["u0", "tmin"])
            V(lambda e: e.tensor_copy(u_[:, 512:1024], P[6][:, :]), ["P6"], ["u1"])
            for h in range(H):
                for hf in range(2):
                    r = HR[hf]
                    T(lambda e, h=h, r=r, hf=hf: e.matmul(P[hf][:, h * 64:(h + 1) * 64], h3(kTM)[r, h, :], h3(TTbg)[r, h, :],
                                                         start=True, stop=True, tile_position=(64 * hf, 0)),
                      ["kTM", "TTbg"], ["P%d" % hf])
            A(lambda e: e.activation(wTs[0], P[0][:, :], AF.Copy), ["P0"], ["wT0"])
            V(lambda e: e.tensor_copy(wTs[1], P[1][:, :]), ["P1"], ["wT1"])
            for hf in range(2):
                r = HR[hf]
                for h in range(H):
                    pp = 5 if h < 4 else 6
                    T(lambda e, h=h, pp=pp, r=r, hf=hf: e.matmul(P[pp][r, (h % 4) * 128:(h % 4 + 1) * 128], h3(wTs[hf])[:, h, :],
                                                                Sbs[hf][:, h * 128:(h + 1) * 128], start=True, stop=True,
                                                                tile_position=(0, 64 * hf)), ["wT%d" % hf, "Sb%d" % hf], ["P%d" % pp])
            V(lambda e: e.tensor_tensor(vn[:, 0:512], u_[:, 0:512], P[5][:, :], ALU.subtract), ["u0", "P5"], ["vn0"])
            V(lambda e: e.tensor_tensor(vn[:, 512:1024], u_[:, 512:1024], P[6][:, :], ALU.subtract), ["u1", "P6"], ["vn1"])
            V(lambda e: e.tensor_tensor(h3(vns), h3(vn), bc(ej, [128, 8, 128], 2), ALU.mult), ["vn0", "vn1", "ej" + kp], ["vns"])
            for hf in range(2):
                r = HR[hf]
                po = 2 + hf
                for h in range(H):
                    T(lambda e, h=h, hf=hf, po=po: e.matmul(P[po][:, h * 64:(h + 1) * 64], Sbs[hf][:, h * 128:(h + 1) * 128],
                                                           h3(qdTs[hf])[:, h, :], start=True, stop=False),
                      ["Sb%d" % hf, "qdT%d" % hf], ["P%d" % po])
                    T(lambda e, h=h, hf=hf, po=po, r=r: e.matmul(P[po][:, h * 64:(h + 1) * 64], h3(vn)[r, h, :], h3(qkT)[r, h, :],
                                                                start=False, stop=True, tile_position=(64 * hf, 0)),
                      ["vn0", "vn1", "qkT"], ["P%d" % po])
            for hf in range(2):
                po = 2 + hf; tok = toks[hf]
                if s_ < NCH // 2:
                    if hf == 0:
                        A(lambda e, po=po, tok=tok: e.activation(ofv[:, :, tok], h3(P[po][:, :]), AF.Copy), ["P%d" % po], ["of"])
                    else:
                        V(lambda e, po=po, tok=tok: e.tensor_copy(ofv[:, :, tok], h3(P[po][:, :])), ["P%d" % po], ["of"])
                else:
                    V(lambda e, po=po, tok=tok: e.tensor_tensor(h3(osum), h3(P[po][:, :]), ofv[:, :, tok], ALU.add),
                      ["P%d" % po, "of"], ["osum"])
                    A(lambda e: e.activation(sqo, osum, AF.Square), ["osum"], ["sqo"])
                    T(lambda e: e.matmul(P[4][:, :], onesb[:], sqo, start=True, stop=True), ["sqo", "onesb"], ["P4"])
                    A(lambda e: e.activation(rstd, P[4][:, :], AF.Ln, bias=epsc[:, 0:1], scale=1.0 / 128), ["P4", "epsc"], ["rstd"])
                    A(lambda e: e.activation(rstd, rstd, AF.Exp, scale=-0.5), ["rstd"], ["rstd"])
                    V(lambda e, tok=tok: e.scalar_tensor_tensor(ofv[:, :, tok], h3(osum), par["hng"][:, 0:1], h3(rstd),
                                                                ALU.mult, ALU.mult), ["osum", "rstd", "p_hng", "of"], ["of"])
            for hf in range(2):
                r = HR[hf]
                banks = (5, 6) if hf == 0 else (0, 1)
                for h in range(H):
                    pp = banks[0] if h < 4 else banks[1]
                    T(lambda e, h=h, pp=pp, r=r, hf=hf: e.matmul(P[pp][:, (h % 4) * 128:(h % 4 + 1) * 128], h3(kTM)[r, h, :],
                                                                h3(vns)[r, h, :], start=True, stop=True, tile_position=(64 * hf, 0)),
                      ["kTM", "vns"], ["P%d" % pp])
                St = Sts[hf]; sk = "St%d" % hf
                V(lambda e, St=St, hf=hf: e.tensor_tensor(h3(St), h3(St), bc(edec[:, 8 * hf:8 * hf + 8], [128, 8, 128], 2), ALU.mult),
                  [sk, "edec" + kp], [sk])
                V(lambda e, St=St, b0=banks[0]: e.tensor_tensor(St[:, 0:512], St[:, 0:512], P[b0][:, :], ALU.add), [sk, "P%d" % banks[0]], [sk])
                V(lambda e, St=St, b1=banks[1]: e.tensor_tensor(St[:, 512:1024], St[:, 512:1024], P[b1][:, :], ALU.add), [sk, "P%d" % banks[1]], [sk])
                if s_ % 4 == 3:
                    seq = (s_ // 4) if hf == 0 else ((NCH - 1 - s_) // 4)
                    S.dma("sync", st_d[l, seq, hf].rearrange("h k v -> k h v"), h3(St), reads=[sk],
                          writes=["st_out"], semkey="d_Sout%d" % hf, out_final=True)
                    V(lambda e, St=St, hf=hf: e.tensor_scalar(Sbs[hf], St, flg[:, 1:2], None, ALU.mult), [sk, "flg"], ["Sb%d" % hf])
                else:
                    A(lambda e, St=St, hf=hf: e.activation(Sbs[hf], St, AF.Copy), [sk], ["Sb%d" % hf])

        dn_gates(0)
        for s_ in range(NCH):
            dn_step(s_)

    accs = [big[:, 0:2048].bitcast(F32), big[:, 2048:4096].bitcast(F32)]
    acc_keys = [["big0", "big1"], ["big2", "big3"]]
    mean = big[:, 4096:6144].bitcast(F32); mean_keys = ["big4", "big5"]
    lrs = big[:, 6144:8192].bitcast(F32); lrs_keys = ["big6", "big7"]
    cttmp = big[:, 8192:10240].bitcast(F32); cttmp_keys = ["big8", "big9"]

    NDG = 6
    dgs = [R3[:, 64 * i:64 * (i + 1)].bitcast(BF16) for i in range(NDG)]
    dgctr = [0]

    def conv_tile(l, ct):
        acc = accs[ct % 2]; ak = acc_keys[ct % 2]
        ua = bigv[:, 16 + ct, :]; uk = "big%d" % (16 + ct)
        caw = par["caw"]
        banks = [0, 1] if ct % 2 == 0 else [2, 0]
        taps = []
        wmain, wk = (caw, "p_caw") if ct < 4 else (cawP, "cawP")
        order = [0] + [d for d in range(-15, 16) if d != 0]
        for d in order:
            j = ct * 31 + 15 + d
            if d == 0:
                taps.append((caw, "p_caw", j, "seg", d))
            else:
                taps.append((wmain, wk, j, "seg", d))
        if ct >= 4:
            for d in range(-15, 16):
                if d != 0:
                    taps.append((cawS, "cawS", ct * 31 + 15 + d, "vert", d))
        specs = {0: [], 1: []}
        for ti, (wt_, wk_, j, kind, d) in enumerate(taps):
            for hf in range(2):
                base = 512 * hf
                if kind == "seg":
                    e_ = abs(d)
                    if d >= 0:
                        o = (0, 256 - d); i_ = (d, 256)
                    else:
                        o = (e_, 256); i_ = (0, 256 - e_)
                    specs[hf].append((ti, "seg", o, i_))
                else:
                    if d > 0:
                        lo, hi = base, min(base + 512, NT - 64 * d); sh = 64 * d
                    else:
                        lo, hi = max(base, -64 * d), base + 512; sh = 64 * d
                    if hi > lo:
                        specs[hf].append((ti, "vert", (lo - base, hi - base), (lo + sh, hi + sh)))
        last = {hf: specs[hf][-1][0] for hf in range(2)}
        by_tap = {}
        for hf in range(2):
            for sp in specs[hf]:
                by_tap.setdefault(sp[0], []).append((hf, sp))
        for ti, (wt_, wk_, j, kind, d) in enumerate(taps):
            di = dgctr[0] % NDG
            dgctr[0] += 1
            dg = dgs[di]; dk = "dg%d" % di
            A(lambda e, dg=dg, wt_=wt_, j=j: e.activation(dg, identb[:, :], AF.Identity, bias=0.0, scale=wt_[:, j:j + 1]),
              ["identb", wk_], [dk])
            for hf, sp in by_tap.get(ti, []):
                pb = banks[hf]
                pv = P[pb][:, :].rearrange("p (s t) -> p s t", s=2)
                xv = ua[:, hs(hf)].rearrange("p (s t) -> p s t", s=2)
                first = (ti == 0)
                lastf = (ti == last[hf]) and (sp is [x for x in specs[hf] if x[0] == ti][-1])
                if sp[1] == "seg":
                    (o0, o1), (i0_, i1_) = sp[2], sp[3]
                    T(lambda e, dg=dg, pv=pv, xv=xv, o0=o0, o1=o1, i0_=i0_, i1_=i1_, first=first, lastf=lastf: e.matmul(
                        pv[:, :, o0:o1], dg, xv[:, :, i0_:i1_], start=first, stop=lastf), [dk, uk], ["P%d" % pb])
                else:
                    (o0, o1), (i0_, i1_) = sp[2], sp[3]
                    T(lambda e, dg=dg, pb=pb, o0=o0, o1=o1, i0_=i0_, i1_=i1_, lastf=lastf: e.matmul(
                        P[pb][:, o0:o1], dg, ua[:, i0_:i1_], start=False, stop=lastf), [dk, uk], ["P%d" % pb])
        for hf in range(2):
            pb = banks[hf]
            A(lambda e, pb=pb, hf=hf: e.activation(acc[:, hs(hf)], P[pb][:, :], AF.Identity, bias=par["cab"][:, ct:ct + 1], scale=1.0),
              ["P%d" % pb, "p_cab"], ak)
        if ct < 4:
            a4 = acc.rearrange("p (b s t) -> p b s t", b=4, s=4); x4 = ua.rearrange("p (b s t) -> p b s t", b=4, s=4)

            def tap(o_ap, i_ap, w_ap, wkey):
                V(lambda e: e.scalar_tensor_tensor(o_ap, i_ap, w_ap, o_ap, ALU.mult, ALU.add), [uk, wkey] + ak, ak)

            for d in list(range(-15, 0)) + list(range(1, 16)):
                j = ct * 31 + 15 + d
                e_ = abs(d)
                for sg in range(3):
                    if d > 0:
                        tap(a4[:, :, sg, 64 - d:64], x4[:, :, sg + 1, 0:d], cawNS[:, j:j + 1], "cawNS")
                    else:
                        tap(a4[:, :, sg + 1, 0:e_], x4[:, :, sg, 64 - e_:64], cawNS[:, j:j + 1], "cawNS")
        A(lambda e: e.activation(ua, acc, AF.Copy), ak, [uk])
        for half in range(2):
            A(lambda e, half=half: e.activation(sqs[half][:], acc[:, hs(half)], AF.Square), ak, ["sq%d" % half])
            T(lambda e, half=half: e.matmul(P[4 + half][:], onesb[:], ua[:, hs(half)], start=(ct == 0), stop=(ct == 7)),
              [uk, "onesb"], ["P%d" % (4 + half)])
            pq = 6 if half == 0 else 3
            T(lambda e, half=half, pq=pq: e.matmul(P[pq][:], onesb[:], sqs[half][:], start=(ct == 0), stop=(ct == 7)),
              ["sq%d" % half, "onesb"], ["P%d" % pq])

    def rest_of_layer(l, compute_stats_rs, hn_rhs, hn_keys, mod):
        def zb_cons(ft, half, ps, pk):
            A(lambda e: e.activation(sqs[half][:], ps, AF.Silu), [pk], ["sq%d" % half])
            V(lambda e: e.tensor_tensor(ofv[:, ft, hs(half)], ofv[:, ft, hs(half)], sqs[half][:], ALU.mult),
              ["of", "sq%d" % half], ["of"])
        proj(w_in_d[l], COL["z_b"], 1024, KT, hn_rhs, hn_keys, zb_cons)

        def glu_cons(ft, half, ps, pk):
            A(lambda e: e.activation(bigv[:, 16 + ft, hs(half)], ps, AF.Sigmoid), [pk], ["big%d" % (16 + ft)])
        proj(w_in_d[l], COL["a_glu"], 1024, KT, hn_rhs, hn_keys, glu_cons)

        def val_cons(ft, half, ps, pk):
            V(lambda e: e.tensor_tensor(bigv[:, 16 + ft, hs(half)], bigv[:, 16 + ft, hs(half)], ps, ALU.mult),
              [pk, "big%d" % (16 + ft)], ["big%d" % (16 + ft)])
        proj(w_in_d[l], COL["a_val"], 1024, KT, hn_rhs, hn_keys, val_cons)

        for ct in range(8):
            conv_tile(l, ct)
            if l + 1 < n_layers:
                for gg in range(3 * ct, 3 * ct + 3):
                    if gg % 5 < 2:
                        mod_group(l + 1, gg)
                    else:
                        mod_group0(gg, l + 1, bi=gg % 5 - 1)
        if l + 1 < n_layers:
            mod_finish(l + 1, 0)
            mod_finish(l + 1, 1)
        for half in range(2):
            pq = 6 if half == 0 else 3
            A(lambda e, half=half: e.activation(mean[:, hs(half)], P[4 + half][:], AF.Identity, bias=0.0, scale=1.0 / 1024),
              ["P%d" % (4 + half)], mean_keys)
            A(lambda e, half=half: e.activation(tmpB[:], P[4 + half][:], AF.Square, scale=1.0 / 1024),
              ["P%d" % (4 + half)], ["tmpB"])
            V(lambda e, pq=pq: e.scalar_tensor_tensor(tmpB[:], P[pq][:], 1.0 / 1024, tmpB[:], ALU.mult, ALU.subtract),
              ["P%d" % pq, "tmpB"], ["tmpB"])
            A(lambda e, half=half: e.activation(lrs[:, hs(half)], tmpB[:], AF.Ln, bias=epsc[:, 0:1], scale=1.0),
              ["tmpB", "epsc"], lrs_keys)
            A(lambda e, half=half: e.activation(lrs[:, hs(half)], lrs[:, hs(half)], AF.Exp, scale=-0.5), lrs_keys, lrs_keys)
        for ct in range(8):
            uk = "big%d" % (16 + ct)
            V(lambda e, ct=ct: e.tensor_tensor(cttmp, bigv[:, 16 + ct, :], mean, ALU.subtract), [uk] + mean_keys, cttmp_keys)
            V(lambda e: e.tensor_tensor(cttmp, cttmp, lrs, ALU.mult), cttmp_keys + lrs_keys, cttmp_keys)
            A(lambda e, ct=ct: e.activation(bigv[:, 16 + ct, :], cttmp, AF.Silu, bias=par["lnb"][:, ct:ct + 1],
                                            scale=par["lng"][:, ct:ct + 1]), cttmp_keys + ["p_lnb", "p_lng"], [uk])

        def az_cons(ft, half, ps, pk):
            A(lambda e: e.activation(sqs[half][:], ps, AF.Silu), [pk], ["sq%d" % half])
            V(lambda e: e.tensor_tensor(bigv[:, 16 + ft, hs(half)], bigv[:, 16 + ft, hs(half)], sqs[half][:], ALU.mult),
              ["big%d" % (16 + ft), "sq%d" % half], ["big%d" % (16 + ft)])
        proj(w_in_d[l], COL["a_z"], 1024, KT, hn_rhs, hn_keys, az_cons)

        sgbv = sgb.rearrange("p (f t) -> p f t", f=2)
        pa_rhs = lambda kt, half: bigv[:, 16 + kt, hs(half)]
        pa_keys = lambda kt: ["big%d" % (16 + kt)]
        pb_rhs = lambda kt, half: ofv[:, kt, hs(half)]
        pb_keys = lambda kt: ["of"]
        for g in range(8):
            def ga_cons(ft, half, ps, pk, g=g):
                A(lambda e: e.activation(bigv[:, 2 * g + ft, hs(half)], ps, AF.Sigmoid), [pk], ["big%d" % (2 * g + ft)])
            proj(w_in_d[l], COL["gate_a"] + g * 256, 256, KT, hn_rhs, hn_keys, ga_cons)

            def oa_cons(ft, half, ps, pk, g=g):
                V(lambda e: e.tensor_tensor(bigv[:, 2 * g + ft, hs(half)], bigv[:, 2 * g + ft, hs(half)], ps, ALU.mult),
                  [pk, "big%d" % (2 * g + ft)], ["big%d" % (2 * g + ft)])
            proj(w_pa_d[l], g * 256, 256, 8, pa_rhs, pa_keys, oa_cons)

            def gb_cons(ft, half, ps, pk, g=g):
                A(lambda e: e.activation(sgbv[:, ft, hs(half)], ps, AF.Sigmoid), [pk], ["R3a"])
            proj(w_in_d[l], COL["gate_b"] + g * 256, 256, KT, hn_rhs, hn_keys, gb_cons)

            def ob_cons(ft, half, ps, pk, g=g):
                V(lambda e: e.tensor_tensor(tmpB[:], sgbv[:, ft, hs(half)], ps, ALU.mult), [pk, "R3a"], ["tmpB"])
                V(lambda e: e.tensor_tensor(bigv[:, 2 * g + ft, hs(half)], bigv[:, 2 * g + ft, hs(half)], tmpB[:], ALU.add),
                  ["tmpB", "big%d" % (2 * g + ft)], ["big%d" % (2 * g + ft)])
            proj(w_pb_d[l], g * 256, 256, 8, pb_rhs, pb_keys, ob_cons)

        def wo_cons(ft, half, ps, pk):
            V(lambda e: e.scalar_tensor_tensor(xTv[:, ft, hs(half)], ps, mod[:, 32 + ft:33 + ft], xTv[:, ft, hs(half)],
                                               ALU.mult, ALU.add), [pk, "modg%d" % l, "x%d" % ft], ["x%d" % ft])
        proj(w_o_d[l], 0, D, KT, lambda kt, half: bigv[:, kt, hs(half)], lambda kt: ["big%d" % kt], wo_cons)

    def final_norm(compute_stats_rs):
        compute_stats_rs(1.0 / D, epsc[:, 0:1])
        bufs = [(tmpA, "tmpA"), (R3[:, 0:NT], "R3a")]
        for kt in range(KT):
            buf, bk = bufs[kt % 2]
            V(lambda e, kt=kt, buf=buf: e.scalar_tensor_tensor(buf, xTv[:, kt, :], fng[:, kt:kt + 1], rs, ALU.mult, ALU.mult),
              ["x%d" % kt, "fng", "rs"], [bk])
            S.dma("sync", yT_d[kt * 128:(kt + 1) * 128, :], buf, reads=[bk], writes=["yT%d" % kt], semkey="d_y" + bk,
                  out_final=True)

    for l in range(n_layers):
        grp = "par%d" % (l + 1)
        for nm, d_ap in (("caw", caw_d), ("cab", cab_d), ("lng", lng_d),
                         ("lnb", lnb_d), ("cqw", cqw_d), ("hng", hng_d)):
            S.dma("sync", par[nm][:], d_ap[l], writes=["p_" + nm], group=grp)
        S.dma("sync", alog[:], alog_d[l], writes=["alog"], group=grp)
        S.dma("sync", dtb[:], dtb_d[l], writes=["dtb"], group=grp)
        S.dma("gpsimd", wg[:].rearrange("p (k c) -> p k c", k=KT),
              w_in_d[l].rearrange("(k p) c -> p k c", p=128)[:, :, COL["beta"]:COL["beta"] + 32],
              writes=["wg"], semkey="d_wg")
        caw = par["caw"]
        V(lambda e: e.tensor_scalar(cawP[:], caw[:], flg[:, 0:1], None, ALU.mult), ["p_caw", "flg"], ["cawP"])
        V(lambda e: e.tensor_scalar(cawS[:], caw[:], flg[:, 1:2], None, ALU.mult), ["p_caw", "flg"], ["cawS"])
        V(lambda e: e.tensor_scalar(cawNS[:], caw[:], flg[:, 3:4], None, ALU.mult), ["p_caw", "flg"], ["cawNS"])
        cq3 = par["cqw"][:].rearrange("p (f j) -> p f j", j=3)
        V(lambda e: e.tensor_scalar(cqwL[:], cq3[:, :, 0], flg[:, 2:3], None, ALU.mult), ["p_cqw", "flg"], ["cqwL"])
        V(lambda e: e.tensor_scalar(cqwR[:], cq3[:, :, 2], flg[:, 2:3], None, ALU.mult), ["p_cqw", "flg"], ["cqwR"])
        A(lambda e: e.activation(negA[:], alog[:], AF.Exp), ["alog"], ["negA"])
        V(lambda e: e.tensor_scalar(negA[:], negA[:], -1.0, None, ALU.mult), ["negA"], ["negA"])

        mod = mods[l]; modA = modAs[l]
        if l == 0:
            for g in range(24):
                mod_group0(g)
            mod_finish(0, 0)
            mod_finish(0, 1)

        def compute_stats_rs(scale, bias_ap):
            for half in range(2):
                for kt in range(KT):
                    sq = sqs[kt % 2]
                    A(lambda e, sq=sq, kt=kt, half=half: e.activation(sq[:], xTv[:, kt, hs(half)], AF.Square),
                      ["x%d" % kt], ["sq%d" % (kt % 2)])
                    T(lambda e, sq=sq, kt=kt: e.matmul(P[5][:], onesb[:], sq[:], start=(kt == 0), stop=(kt == KT - 1)),
                      ["sq%d" % (kt % 2), "onesb"], ["P5"])
                A(lambda e, half=half: e.activation(rs[:, hs(half)], P[5][:], AF.Ln, bias=bias_ap, scale=scale),
                  ["P5", "epsc"], ["rs"])
                A(lambda e, half=half: e.activation(rs[:, hs(half)], rs[:, hs(half)], AF.Exp, scale=-0.5), ["rs"], ["rs"])

        def compute_hn(mod=mod, modA=modA, l=l):
            compute_stats_rs(1.0 / D, epsc[:, 0:1])
            for kt in range(KT):
                V(lambda e, kt=kt: e.scalar_tensor_tensor(tmpA, xTv[:, kt, :], modA[:, kt:kt + 1], rs,
                                                          ALU.mult, ALU.mult),
                  ["x%d" % kt, "modA%d" % l, "rs"], ["tmpA"])
                A(lambda e, kt=kt: e.activation(hnT[:, kt, :], tmpA, AF.Identity, bias=mod[:, kt:kt + 1], scale=1.0),
                  ["tmpA", "mod%d" % l], ["hn%d" % kt])

        hn_rhs = lambda kt, half: hnT[:, kt, hs(half)]
        hn_keys = lambda kt: ["hn%d" % kt]
        compute_hn()

        def qkv_consumer(base_ft):
            def cons(ft, half, ps, pk):
                f = base_ft + ft
                A(lambda e: e.activation(pre[:, 1 + half * 512:1 + (half + 1) * 512], ps, AF.Copy), [pk], ["R3a"])
                if half == 1:
                    cw = par["cqw"]
                    V(lambda e: e.tensor_scalar(tmpA, pre[:, 1:NT + 1], cw[:, 3 * f + 1:3 * f + 2], None, ALU.mult),
                      ["R3a", "p_cqw"], ["tmpA"])
                    V(lambda e: e.scalar_tensor_tensor(tmpA, pre[:, 0:NT], cw[:, 3 * f:3 * f + 1], tmpA,
                                                       ALU.mult, ALU.add), ["R3a", "p_cqw", "tmpA"], ["tmpA"])
                    V(lambda e: e.scalar_tensor_tensor(tmpA, pre[:, 2:NT + 2], cw[:, 3 * f + 2:3 * f + 3], tmpA,
                                                       ALU.mult, ALU.add), ["R3a", "p_cqw", "tmpA"], ["tmpA"])
                    tv = tmpA.rearrange("p (s t) -> p s t", s=4)
                    pv = pre[:, 1:NT + 1].rearrange("p (s t) -> p s t", s=4)
                    V(lambda e: e.scalar_tensor_tensor(tv[:, 1:4, 0:1], pv[:, 0:3, 255:256], cqwL[:, f:f + 1],
                                                       tv[:, 1:4, 0:1], ALU.mult, ALU.add),
                      ["R3a", "cqwL", "tmpA"], ["tmpA"])
                    V(lambda e: e.scalar_tensor_tensor(tv[:, 0:3, 255:256], pv[:, 1:4, 0:1], cqwR[:, f:f + 1],
                                                       tv[:, 0:3, 255:256], ALU.mult, ALU.add),
                      ["R3a", "cqwR", "tmpA"], ["tmpA"])
                    A(lambda e: e.activation(bigv[:, f, :], tmpA, AF.Silu), ["tmpA"], ["big%d" % f])
            return cons

        V(lambda e: e.memset(pre[:, 0:1], 0.0), [], ["R3a"])
        V(lambda e: e.memset(pre[:, NT + 1:NT + 2], 0.0), ["R3a"], ["R3a"])
        proj(w_in_d[l], COL["q"], 3072, KT, hn_rhs, hn_keys, qkv_consumer(0))

        wg5 = wg[:].rearrange("p (k a d e) -> p k a d e", k=KT, a=2, d=2, e=8)
        for s_ in range(NCH):
            for hf in range(2):
                c = s_ if hf == 0 else NCH - 1 - s_
                for kt in range(KT):
                    T(lambda e, c=c, kt=kt, hf=hf, s_=s_: e.matmul(
                        P[4][hf * 64:hf * 64 + 64, s_ * 16:(s_ + 1) * 16].rearrange("p (a e) -> p a e", a=2),
                        hnT[:, kt, c * C:(c + 1) * C], wg5[:, kt, :, hf, :], start=(kt == 0), stop=(kt == KT - 1),
                        tile_position=(0, 64 * hf)), ["hn%d" % kt, "wg"], ["P4"])
        p4v = P[4][:, 0:NCH * 16].rearrange("p (s a e) -> p s a e", s=NCH, a=2)
        gTMv = gTM[:].rearrange("p (c g) -> p c g", g=8)
        bTMv = bTM[:].rearrange("p (c g) -> p c g", g=8)
        A(lambda e: e.activation(bTMv, p4v[:, :, 0, :], AF.Sigmoid), ["P4"], ["bTM"])
        V(lambda e: e.tensor_tensor(gTMv, p4v[:, :, 1, :], bc(dtb[:], [128, NCH, 8], 1), ALU.add), ["P4", "dtb"], ["gTM"])
        A(lambda e: e.activation(gTM[:], gTM[:], AF.Exp), ["gTM"], ["gTM"])
        S.op("scalar", lambda e: e.activation(gTM[:], gTM[:], AF.Ln, bias=epsc[:, 2:3], scale=1.0), ["gTM", "epsc"], ["gTM"], strict=True)
        V(lambda e: e.tensor_tensor(gTMv, gTMv, bc(negA[:], [128, NCH, 8], 1), ALU.mult), ["gTM", "negA"], ["gTM"])

        S.barrier()
        deltanet(l)
        S.barrier()
        compute_hn()
        rest_of_layer(l, compute_stats_rs, hn_rhs, hn_keys, mod)
    final_norm(compute_stats_rs)
    S.finish()
    return nc


_PROG = {}
_DEBUG = False
_LAST = {}


def _consts():
    c = np.zeros((128, 6 * 128), np.float32)
    c[:, 0:128] = np.eye(128, dtype=np.float32)
    r = np.arange(64)[:, None]; q = np.arange(64)[None, :]
    c[0:64, 128:192] = (r <= q); c[64:128, 128:192] = (r >= q)
    c[0:64, 192:256] = (r > q); c[64:128, 192:256] = (r < q)
    c[0:64, 256:320] = np.eye(64); c[64:128, 256:320] = np.eye(64)
    return c


def _fm(v, nt):
    return np.ascontiguousarray(np.asarray(v, np.float32).reshape(nt, 128).T)


def kernel(x_prompt, x_sample, state_delta, c, c_ctx, w_mod, b_mod, norm_g, w_in, conv_a_w, conv_a_b,
           ln_a_g, ln_a_b, w_pa, conv_qkv_w, a_log, dt_bias, head_norm_g, w_pb, w_o, final_norm_g):
    f32 = np.float32
    if "nc" not in _PROG:
        _PROG["nc"] = build_program(debug=_DEBUG)
    nc = _PROG["nc"]
    x_prompt = np.asarray(x_prompt, f32); x_sample = np.asarray(x_sample, f32)
    state_delta = np.asarray(state_delta, f32)
    shared = {
        "cst": _consts(),
        "w_mod": np.ascontiguousarray(np.asarray(w_mod, f32)),
        "b_mod": np.stack([_fm(b_mod[l], 48) for l in range(DEPTH)]),
        "norm_g": np.stack([_fm(norm_g[l], KT) for l in range(DEPTH)]),
        "w_in": np.ascontiguousarray(np.asarray(w_in, f32)),
        "caw": np.stack([np.ascontiguousarray(np.asarray(conv_a_w[l], f32).T.reshape(8, 128, 31).transpose(1, 0, 2)).reshape(128, 248)
                         for l in range(DEPTH)]),
        "cab": np.stack([_fm(conv_a_b[l], 8) for l in range(DEPTH)]),
        "lng": np.stack([_fm(ln_a_g[l], 8) for l in range(DEPTH)]),
        "lnb": np.stack([_fm(ln_a_b[l], 8) for l in range(DEPTH)]),
        "w_pa": np.ascontiguousarray(np.asarray(w_pa, f32)),
        "cqw": np.stack([np.ascontiguousarray(np.asarray(conv_qkv_w[l], f32).T.reshape(24, 128, 3).transpose(1, 0, 2)).reshape(128, 72)
                         for l in range(DEPTH)]),
        "alog": np.stack([np.repeat(np.asarray(a_log[l], f32).reshape(2, 8), 64, axis=0) for l in range(DEPTH)]),
        "dtb": np.stack([np.repeat(np.asarray(dt_bias[l], f32).reshape(2, 8), 64, axis=0) for l in range(DEPTH)]),
        "hng": np.asarray(head_norm_g, f32).reshape(DEPTH, 128, 1).copy(),
        "w_pb": np.ascontiguousarray(np.asarray(w_pb, f32)),
        "w_o": np.ascontiguousarray(np.asarray(w_o, f32)),
        "fng": _fm(final_norm_g, KT),
    }
    in_maps = []
    for core in range(8):
        m = dict(shared)
        if core < 4:
            xt = x_prompt[4 * core:4 * core + 4].reshape(NT, D)
            m["cv"] = _fm(c_ctx, KT)
            m["s0"] = np.zeros((DEPTH, 2, 128, H * 128), f32)
            m["flg"] = np.ascontiguousarray(np.broadcast_to(np.array([1, 0, -1, 0], f32), (128, 4)))
        else:
            b = core - 4
            xt = x_sample[b]
            m["cv"] = _fm(np.asarray(c, f32)[b], KT)
            m["s0"] = np.ascontiguousarray(state_delta[b].transpose(0, 1, 3, 2, 4)).reshape(DEPTH, 2, 128, H * 128)
            m["flg"] = np.ascontiguousarray(np.broadcast_to(np.array([0, 1, 0, -1], f32), (128, 4)))
        m["xT"] = np.ascontiguousarray(xt.T)
        in_maps.append(m)
    res = run_bass_kernel_spmd(nc, in_maps, core_ids=list(range(8)))
    r = res.results
    _LAST['r'] = r
    y_prompt = np.stack([r[i]["yT"].T.reshape(4, 256, D) for i in range(4)]).reshape(16, 256, D)
    y_sample = np.stack([r[4 + b]["yT"].T for b in range(4)])
    st = np.concatenate([r[i]["st"].transpose(1, 0, 2, 3, 4, 5) for i in range(4)], axis=0)
    return (np.ascontiguousarray(y_prompt, dtype=f32), np.ascontiguousarray(y_sample, dtype=f32),
            np.ascontiguousarray(st, dtype=f32))
```

```python
from contextlib import ExitStack
import numpy as np
import concourse.bass as bass
import concourse.mybir as mybir
from concourse.bass_utils import run_bass_kernel_spmd

F32 = mybir.dt.float32
BF16 = mybir.dt.bfloat16
AF = mybir.ActivationFunctionType
ALU = mybir.AluOpType

SAME_ENGINE_SYNC = False


class Sched:
    ENGS = ["tensor", "vector", "scalar", "gpsimd", "sync"]

    def __init__(self, nc):
        self.nc = nc
        self.stack = ExitStack()
        self.ops = {e: [] for e in self.ENGS}
        self.count = {}
        self.seen = {e: {} for e in self.ENGS}
        self.last_w = {}
        self.readers = {}
        self.sem_names = set(self.ENGS)
        self.final_group = set()
        self.out_sems = set()
        self.barrier_toks = []

    def barrier(self):
        self.barrier_toks = [(k, v) for k, v in self.count.items()]

    def sb(self, name, shape, dt):
        return self.stack.enter_context(self.nc.sbuf_tensor(name, shape, dt))

    def ps(self, name, shape, dt):
        return self.stack.enter_context(self.nc.psum_tensor(name, shape, dt))

    def _deps(self, eng, reads, writes, strict=False):
        toks = list(self.barrier_toks)
        for k in list(reads) + list(writes):
            t = self.last_w.get(k)
            if t is not None:
                toks.append(t)
        for k in writes:
            toks.extend(self.readers.get(k, []))
        waits = {}
        for (s, v) in toks:
            if s == eng and (eng in ("tensor", "sync") or not (SAME_ENGINE_SYNC or strict)):
                continue
            if self.seen[eng].get(s, 0) >= v:
                continue
            waits[s] = max(waits.get(s, 0), v)
        for s, v in waits.items():
            self.seen[eng][s] = v
        return sorted(waits.items())

    def _commit(self, tok, reads, writes):
        for k in writes:
            self.last_w[k] = tok
            self.readers[k] = []
        for k in reads:
            if k not in writes:
                self.readers.setdefault(k, []).append(tok)

    def op(self, eng, fn, reads=(), writes=(), strict=False):
        waits = self._deps(eng, reads, writes, strict)
        self.count[eng] = self.count.get(eng, 0) + 1
        tok = (eng, self.count[eng])
        self.ops[eng].append((waits, fn, (eng, 1)))
        self._commit(tok, reads, writes)

    def dma(self, eng, out, in_, reads=(), writes=(), semkey=None, out_final=False, group=None):
        if group is not None:
            semkey = "g_" + group
            self.final_group.add(semkey)
        if semkey is None:
            semkey = "d_" + str((list(writes) + list(reads))[0])
        self.sem_names.add(semkey)
        if out_final:
            self.out_sems.add(semkey)
        waits = self._deps(eng, reads, writes)
        self.count[semkey] = self.count.get(semkey, 0) + 16
        tok = (semkey, self.count[semkey])
        self.ops[eng].append((waits, lambda e: e.dma_start(out=out, in_=in_), (semkey, 16)))
        self._commit(tok, reads, writes)

    def finish(self):
        nc = self.nc
        sems = {}
        for i, s in enumerate(sorted(self.sem_names)):
            sems[s] = self.stack.enter_context(nc.semaphore("s%d" % i))
        final = dict(self.count)
        ops = self.ops
        fg = self.final_group
        out_sems = sorted(self.out_sems)

        def emit(eng_name):
            def body(e):
                for waits, fn, (isem, iv) in ops[eng_name]:
                    for s, v in waits:
                        if s in fg:
                            v = final[s]
                        e.wait_ge(sems[s], v)
                    ins = fn(e)
                    ins.then_inc(sems[isem], iv)
                if eng_name == "sync":
                    for s in out_sems:
                        e.wait_ge(sems[s], final[s])
            return body

        with nc.Block() as block:
            block.tensor(emit("tensor"))
            block.vector(emit("vector"))
            block.scalar(emit("scalar"))
            block.gpsimd(emit("gpsimd"))
            block.sync(emit("sync"))
        self.stack.close()


D = 2048
NT = 1024
DEPTH = 2
KT = 16
H = 8
C = 64
NCH = NT // C
IN_W = 11296
EPS = 1e-6
NLEV = 5
COL = dict(a_val=0, a_glu=1024, a_z=2048, q=3072, k=4096, v=5120, z_b=6144,
           beta=7168, alpha=7184, gate_a=7200, gate_b=9248)


def build_program(n_layers=DEPTH, debug=False):
    nc = bass.Bass("TRN2", target_bir_lowering=False)
    S = Sched(nc)

    def din(name, shape):
        return nc.dram_tensor(name, shape, F32, kind="ExternalInput").ap()

    xT_d = din("xT", [D, NT])
    cv_d = din("cv", [128, KT])
    s0_d = din("s0", [DEPTH, 2, 128, H * 128])
    flg_d = din("flg", [128, 4])
    cst_d = din("cst", [128, 6 * 128])
    w_mod_d = din("w_mod", [DEPTH, D, 3 * D])
    b_mod_d = din("b_mod", [DEPTH, 128, 48])
    ng_d = din("norm_g", [DEPTH, 128, KT])
    w_in_d = din("w_in", [DEPTH, D, IN_W])
    caw_d = din("caw", [DEPTH, 128, 8 * 31])
    cab_d = din("cab", [DEPTH, 128, 8])
    lng_d = din("lng", [DEPTH, 128, 8])
    lnb_d = din("lnb", [DEPTH, 128, 8])
    w_pa_d = din("w_pa", [DEPTH, 1024, D])
    cqw_d = din("cqw", [DEPTH, 128, 24 * 3])
    alog_d = din("alog", [DEPTH, 128, 8])
    dtb_d = din("dtb", [DEPTH, 128, 8])
    hng_d = din("hng", [DEPTH, 128, 1])
    w_pb_d = din("w_pb", [DEPTH, 1024, D])
    w_o_d = din("w_o", [DEPTH, D, D])
    fng_d = din("fng", [128, KT])
    yT_d = nc.dram_tensor("yT", [D, NT], F32, kind="ExternalOutput").ap()
    st_d = nc.dram_tensor("st", [DEPTH, 4, 2, H, 128, 128], F32, kind="ExternalOutput").ap()

    xT = S.sb("xTs", [128, KT * NT], F32)
    xTv = xT[:].rearrange("p (k t) -> p k t", k=KT)
    big = S.sb("big", [128, 24 * NT], BF16)
    bigv = big[:].rearrange("p (s t) -> p s t", s=24)
    of = S.sb("of", [128, 8 * NT], BF16)
    ofv = of[:].rearrange("p (s t) -> p s t", s=8)
    R1 = S.sb("R1", [128, 8192], F32)
    hnT = R1[:].bitcast(BF16).rearrange("p (k t) -> p k t", k=KT)
    R3 = S.sb("R3", [128, 3080], F32)
    wbs = [S.sb("wb%d" % i, [128, 16 * 256], BF16) for i in range(2)]
    wmf = R3[:, 0:1024]
    cst = S.sb("cst_s", [128, 6 * 128], F32)
    identb = S.sb("identb", [128, 128], BF16)
    onesb = S.sb("onesb", [128, 128], BF16)
    onesf = S.sb("onesf", [128, 128], F32)
    negones = S.sb("negones", [128, 64], F32)
    negtri = S.sb("negtri", [128, 64], F32)
    flg = S.sb("flg_s", [128, 4], F32)
    epsc = S.sb("epsc", [128, 3], F32)
    cvs = S.sb("cvs", [128, KT], F32)
    fng = S.sb("fng_s", [128, KT], F32)
    mods = [S.sb("mod%d" % i, [128, 48], F32) for i in range(DEPTH)]
    modAs = [S.sb("modA%d" % i, [128, KT], F32) for i in range(DEPTH)]
    bmods = [S.sb("bmod%d" % i, [128, 48], F32) for i in range(DEPTH)]
    ngs = [S.sb("ngs%d" % i, [128, KT], F32) for i in range(DEPTH)]
    scb = S.sb("scb", [128, KT], BF16)
    par = {}
    for nm, w in (("caw", 248), ("cab", 8), ("lng", 8), ("lnb", 8),
                  ("cqw", 72), ("hng", 1)):
        par[nm] = S.sb("p_" + nm, [128, w], F32)
    cawP = S.sb("cawP", [128, 248], F32)
    cawS = S.sb("cawS", [128, 248], F32)
    cawNS = S.sb("cawNS", [128, 248], F32)
    cqwL = S.sb("cqwL", [128, 24], F32)
    cqwR = S.sb("cqwR", [128, 24], F32)
    alog = S.sb("alog_s", [128, 8], F32)
    dtb = S.sb("dtb_s", [128, 8], F32)
    negA = S.sb("negA", [128, 8], F32)
    gTM = S.sb("gTM", [128, NCH * 8], F32)
    bTM = S.sb("bTM", [128, NCH * 8], F32)
    wg = S.sb("wg", [128, KT * 32], BF16)
    St = R3[:, 1032:2056]
    tmpA = R3[:, 1032:2056]
    tmpB = S.sb("tmpB", [128, 512], F32)
    rs = R3[:, 2056:3080]
    sqs = [S.sb("sq%d" % i, [128, 512], BF16) for i in range(2)]
    sgb = R3[:, 0:1024].bitcast(BF16)
    pre = R3[:, 0:NT + 2]

    P = [S.ps("P%d" % i, [128, 512], F32) for i in range(7)]
    PT = S.ps("PT", [128, 1024], BF16)

    def T(fn, r, w): S.op("tensor", fn, r, w)
    def V(fn, r, w): S.op("vector", fn, r, w)
    def A(fn, r, w): S.op("scalar", fn, r, w)
    def G(fn, r, w): S.op("gpsimd", fn, r, w)

    S.dma("sync", cst[:], cst_d, writes=["cst"], group="par0")
    S.dma("sync", flg[:], flg_d, writes=["flg"], group="par0")
    S.dma("sync", cvs[:], cv_d, writes=["cvs"], group="par0")
    S.dma("sync", fng[:], fng_d, writes=["fng"], group="par0")
    S.dma("gpsimd", identb[:], cst_d[:, 0:128], writes=["identb"], group="par0c")
    for kt in range(KT):
        S.dma("sync", xTv[:, kt, :], xT_d[kt * 128:(kt + 1) * 128, :], writes=["x%d" % kt], group="xin")
    for i in range(DEPTH):
        S.dma("sync", bmods[i][:], b_mod_d[i], writes=["bmod%d" % i], group="par0")
        S.dma("sync", ngs[i][:], ng_d[i], writes=["ngs%d" % i], group="par0")
    A(lambda e: e.activation(scb[:], cvs[:], AF.Silu), ["cvs"], ["scb"])
    V(lambda e: e.memset(onesb[:], 1.0), [], ["onesb"])
    V(lambda e: e.memset(onesf[:], 1.0), [], ["onesf"])
    V(lambda e: e.memset(negones[:], -1.0), [], ["negones"])
    V(lambda e: e.memset(epsc[:, 0:1], EPS), [], ["epsc"])
    V(lambda e: e.memset(epsc[:, 1:2], 128.0 * EPS), ["epsc"], ["epsc"])
    V(lambda e: e.memset(epsc[:, 2:3], 1.0), ["epsc"], ["epsc"])
    tri2 = cst[:, 128:192]
    maskS2 = cst[:, 192:256]
    ident2 = cst[:, 256:320]
    V(lambda e: e.tensor_scalar(negtri[:], tri2, -1.0, None, ALU.mult), ["cst"], ["negtri"])

    def bc(ap, shape, axis):
        return ap.unsqueeze(axis).to_broadcast(shape)

    wctr = [0]

    wpool = [None]

    def load_w(src3, ktn, ncols):
        extra = wpool[0] or []
        i = wctr[0] % (2 + len(extra))
        wctr[0] += 1
        if i >= 2:
            bi = extra[i - 2]
            dst = big[:, bi * 4096:bi * 4096 + ktn * ncols].rearrange("p (k c) -> p k c", k=ktn)
            keys = ["big%d" % (4 * bi + j) for j in range(4)]
            S.dma("gpsimd", dst, src3, writes=keys, semkey="d_bigw%d" % bi)
            return dst, keys
        key = "wb%d" % i
        dst = wbs[i][:, 0:ktn * ncols].rearrange("p (k c) -> p k c", k=ktn)
        S.dma("gpsimd", dst, src3, writes=[key])
        return dst, [key]

    pctr = [0]

    def proj(w2d, c0, ncols_total, ktn, rhs_fn, rhs_keys_fn, consumer, gcols=256, after_group=None):
        wv = w2d.rearrange("(k p) c -> p k c", p=128)
        for g0 in range(0, ncols_total, gcols):
            gc_ = min(gcols, ncols_total - g0)
            wt, wkey = load_w(wv[:, :, c0 + g0:c0 + g0 + gc_], ktn, gc_)
            for f0 in range(0, gc_, 128):
                fw = min(128, gc_ - f0)
                ft = (g0 + f0) // 128
                for half in range(2):
                    pi = pctr[0] % 4
                    pctr[0] += 1
                    pk = "P%d" % pi
                    for kt in range(ktn):
                        T(lambda e, kt=kt, pi=pi, f0=f0, fw=fw, half=half, wt=wt: e.matmul(
                            P[pi][0:fw, :], wt[:, kt, f0:f0 + fw], rhs_fn(kt, half),
                            start=(kt == 0), stop=(kt == ktn - 1)),
                          wkey + rhs_keys_fn(kt), [pk])
                    consumer(ft, half, P[pi][0:fw, :], pk)
            if after_group is not None:
                after_group(g0 // gcols)

    def hs(half):
        return slice(half * 512, (half + 1) * 512)


    PTf = PT[:].bitcast(F32)

    def mod_group(l, g):
        wv = w_mod_d[l].rearrange("(k p) c -> p k c", p=128)
        wt, wkey = load_w(wv[:, :, g * 256:(g + 1) * 256], KT, 256)
        for f in range(2):
            n = g * 2 + f
            for kt in range(KT):
                T(lambda e, kt=kt, n=n, f=f, wt=wt: e.matmul(PTf[:, n:n + 1], wt[:, kt, f * 128:(f + 1) * 128], scb[:, kt:kt + 1],
                                                            start=(kt == 0), stop=(kt == KT - 1)), wkey + ["scb"], ["PT"])

    def mod_group0(g, l=0, bi=None):
        wv = w_mod_d[l].rearrange("(k p) c -> p k c", p=128)
        if bi is None:
            bi = g % 6
        wt = big[:, bi * 4096:(bi + 1) * 4096].rearrange("p (k c) -> p k c", k=KT)
        bkeys = ["big%d" % (4 * bi + j) for j in range(4)]
        S.dma("gpsimd", wt, wv[:, :, g * 256:(g + 1) * 256], writes=bkeys, semkey="d_bigw%d" % bi)
        for f in range(2):
            n = g * 2 + f
            for kt in range(KT):
                T(lambda e, kt=kt, n=n, f=f, wt=wt: e.matmul(PTf[:, n:n + 1], wt[:, kt, f * 128:(f + 1) * 128], scb[:, kt:kt + 1],
                                                            start=(kt == 0), stop=(kt == KT - 1)), bkeys + ["scb"], ["PT"])

    def mod_finish(l, part):
        if part == 0:
            V(lambda e: e.tensor_tensor(mods[l][:, 0:32], PTf[:, 0:32], bmods[l][:, 0:32], ALU.add), ["PT", "bmod%d" % l], ["mod%d" % l])
            S.op("vector", lambda e: e.scalar_tensor_tensor(modAs[l][:], mods[l][:, 16:32], 1.0, ngs[l][:], ALU.add, ALU.mult),
                 ["mod%d" % l, "ngs%d" % l], ["modA%d" % l], strict=True)
        else:
            V(lambda e: e.tensor_tensor(mods[l][:, 32:48], PTf[:, 32:48], bmods[l][:, 32:48], ALU.add), ["PT", "bmod%d" % l], ["modg%d" % l])

    def f32v(R, off, n):
        return R[:, off:off + n]

    def b16v(R, off, n):
        return R[:, off:off + n // 2].bitcast(BF16)

    def h3(ap, h=H):
        return ap.rearrange("p (h t) -> p h t", h=h)

    W0 = wbs[0][:].bitcast(F32)
    W1 = wbs[1][:].bitcast(F32)
    gB = f32v(R1, 0, 512); gTri = f32v(R1, 512, 512); Em = f32v(R1, 1024, 512); ETm = f32v(R1, 1536, 512)
    Pm = f32v(R1, 2048, 512); EGs = [f32v(R1, 2560, 512), f32v(R1, 3072, 512)]
    u_ = f32v(R1, 3584, 1024); tmin = f32v(R1, 3584, 512); osum = f32v(R1, 4608, 512); rstd = f32v(R1, 5120, 512)
    MT0 = b16v(R1, 5632, 512)
    AA = [b16v(R1, 5888, 512), b16v(R1, 6144, 512)]
    AT = [b16v(R1, 6400, 512), b16v(R1, 6656, 512)]
    Pb = b16v(R1, 6912, 512); Plo = b16v(R1, 7168, 512); Rb = b16v(R1, 7424, 512); PbT = b16v(R1, 7680, 512)
    TTb = b16v(R1, 7936, 512)
    TTbg = b16v(W0, 0, 512); qkT = b16v(W0, 256, 512)
    wTs = [b16v(W0, 512, 512), b16v(W0, 768, 512)]
    qdTs = [b16v(W0, 1024, 512), b16v(W0, 1280, 512)]
    sqo = b16v(W0, 1536, 512)
    gcsL = [f32v(W0, 1792 + 64 * i, 8) for i in range(2)]; egcL = [f32v(W0, 1800 + 64 * i, 8) for i in range(2)]
    ejL = [f32v(W0, 1808 + 64 * i, 8) for i in range(2)]; bgL = [f32v(W0, 1816 + 64 * i, 8) for i in range(2)]
    edecL = [f32v(W0, 1824 + 64 * i, 16) for i in range(2)]
    nbm = gB
    kTM = b16v(W1, 0, 1024); vTM = b16v(W1, 512, 1024); vn = b16v(W1, 1024, 1024); vns = b16v(W1, 1536, 1024)
    Sts = [R3[:, 0:1024], R3[:, 1032:2056]]
    Sbs = [R3[:, 2056:2568].bitcast(BF16), R3[:, 2568:3080].bitcast(BF16)]
    HR = [slice(0, 64), slice(64, 128)]

    def deltanet(l):
        for f in range(16):
            for half in range(2):
                sq = sqs[half]
                pb = 5 + half
                V(lambda e, f=f, half=half, sq=sq: e.tensor_tensor(sq[:], bigv[:, f, hs(half)], bigv[:, f, hs(half)], ALU.mult),
                  ["big%d" % f], ["sq%d" % half])
                T(lambda e, sq=sq, pb=pb: e.matmul(P[pb][:], onesb[:], sq[:], start=True, stop=True),
                  ["sq%d" % half, "onesb"], ["P%d" % pb])
                if f < 8:
                    A(lambda e, half=half, pb=pb: e.activation(rs[:, hs(half)], P[pb][:], AF.Ln, bias=epsc[:, 1:2], scale=128.0),
                      ["P%d" % pb, "epsc"], ["rs"])
                else:
                    A(lambda e, half=half, pb=pb: e.activation(rs[:, hs(half)], P[pb][:], AF.Ln, bias=epsc[:, 0:1], scale=1.0),
                      ["P%d" % pb, "epsc"], ["rs"])
                A(lambda e, half=half: e.activation(rs[:, hs(half)], rs[:, hs(half)], AF.Exp, scale=-0.5), ["rs"], ["rs"])
            V(lambda e, f=f: e.tensor_tensor(bigv[:, f, :], bigv[:, f, :], rs, ALU.mult), ["big%d" % f, "rs"], ["big%d" % f])

        gTMv = gTM[:].rearrange("p (c g) -> p c g", g=8)
        bTMv = bTM[:].rearrange("p (c g) -> p c g", g=8)
        qk_keys = ["big%d" % i for i in range(16)]
        k_keys = ["big%d" % i for i in range(8, 16)]
        v_keys = ["big%d" % i for i in range(16, 24)]
        q_keys = ["big%d" % i for i in range(0, 8)]
        for d_ in range(2):
            S.dma("sync", Sts[d_], s0_d[l, d_], writes=["St%d" % d_])
            A(lambda e, d_=d_: e.activation(Sbs[d_], Sts[d_], AF.Copy), ["St%d" % d_, "rs"], ["Sb%d" % d_, "rs"])

        def dn_gates(s_):
            pr = s_ % 2
            gcs, egc, ej, bg, edec = gcsL[pr], egcL[pr], ejL[pr], bgL[pr], edecL[pr]
            kp = "_%d" % pr
            g8 = gTMv[:, s_, :]
            b8 = bTMv[:, s_, :]
            gTri3 = h3(gTri)
            V(lambda e: e.tensor_tensor(gTri3, bc(g8, [128, 8, 64], 2), bc(tri2, [128, 8, 64], 1), ALU.mult), ["gTM", "cst"], ["gTri"])
            for hf in range(2):
                r = HR[hf]
                tpd = (64 * hf, 64 * hf)
                T(lambda e, r=r, tpd=tpd: e.matmul(P[3][r, 0:8], tri2[r, :], g8[r, :], start=True, stop=True, tile_position=tpd),
                  ["cst", "gTM"], ["P3"])
                T(lambda e, r=r, hf=hf: e.matmul(P[3][:, 8 + 8 * hf:16 + 8 * hf], onesf[r, :], g8[r, :], start=True, stop=True,
                                                 tile_position=(64 * hf, 0)), ["onesf", "gTM"], ["P3"])
                T(lambda e, r=r, tpd=tpd: e.matmul(P[3][r, 24:32], onesf[r, 0:64], g8[r, :], start=True, stop=True, tile_position=tpd),
                  ["onesf", "gTM"], ["P3"])
            for hf in range(2):
                r = HR[hf]
                pg = 2 if hf == 0 else 6
                T(lambda e, r=r, hf=hf, pg=pg: e.matmul(P[pg][:, :], onesf[r, :], gTri[r, :], start=True, stop=True,
                                                        tile_position=(64 * hf, 0)), ["onesf", "gTri"], ["P%d" % pg])
            G(lambda e: e.tensor_tensor(h3(nbm), bc(maskS2, [128, 8, 64], 1), bc(b8, [128, 8, 64], 2), ALU.mult),
              ["cst", "bTM"], ["nbm"])
            A(lambda e: e.activation(gcs, P[3][:, 0:8], AF.Copy), ["P3"], ["gcs" + kp])
            for hf in range(2):
                r = HR[hf]
                pg = 2 if hf == 0 else 6
                V(lambda e, r=r, pg=pg: e.tensor_tensor(h3(tmin)[r], bc(gcs[r, :], [64, 8, 64], 2), h3(P[pg][r, :]), ALU.subtract),
                  ["gcs" + kp, "P%d" % pg], ["tmin"])
                V(lambda e, r=r, pg=pg: e.tensor_tensor(h3(osum)[r], h3(P[pg][r, :]), bc(gcs[r, :], [64, 8, 64], 2), ALU.subtract),
                  ["gcs" + kp, "P%d" % pg], ["osum"])
            V(lambda e: e.tensor_scalar(tmin, tmin, 0.0, None, ALU.min), ["tmin"], ["tmin"])
            A(lambda e: e.activation(Em, tmin, AF.Exp), ["tmin"], ["Em"])
            V(lambda e: e.tensor_scalar(osum, osum, 0.0, None, ALU.min), ["osum"], ["osum"])
            A(lambda e: e.activation(ETm, osum, AF.Exp), ["osum"], ["ETm"])
            A(lambda e: e.activation(EGs[0], P[2][:, :], AF.Exp), ["P2"], ["EG0"])
            A(lambda e: e.activation(EGs[1], P[6][:, :], AF.Exp), ["P6"], ["EG1"])
            A(lambda e: e.activation(egc, P[3][:, 0:8], AF.Exp), ["P3"], ["egc" + kp])
            A(lambda e: e.activation(edec, P[3][:, 8:24], AF.Exp), ["P3"], ["edec" + kp])
            if s_ % 4 == 0 and s_ > 0:
                V(lambda e: e.tensor_scalar(edec, edec, flg[:, 1:2], None, ALU.mult), ["edec" + kp, "flg"], ["edec" + kp])
            V(lambda e: e.tensor_tensor(ej, P[3][:, 24:32], gcs, ALU.subtract), ["P3", "gcs" + kp], ["ej" + kp])
            A(lambda e: e.activation(ej, ej, AF.Exp), ["ej" + kp], ["ej" + kp])
            V(lambda e: e.tensor_tensor(bg, b8, egc, ALU.mult), ["bTM", "egc" + kp], ["bg" + kp])

        def dn_step(s_):
            toks = [slice(s_ * C, (s_ + 1) * C), slice((NCH - 1 - s_) * C, (NCH - s_) * C)]
            g8 = gTMv[:, s_, :]
            b8 = bTMv[:, s_, :]
            pr = s_ % 2
            gcs, egc, ej, bg, edec = gcsL[pr], egcL[pr], ejL[pr], bgL[pr], edecL[pr]
            kp = "_%d" % pr
            Em3, ETm3, Pm3 = h3(Em), h3(ETm), h3(Pm)
            for hf in range(2):
                r = HR[hf]; tok = toks[hf]
                for h in range(H):
                    T(lambda e, h=h, r=r, tok=tok, hf=hf: e.matmul(P[4][r, h * 64:(h + 1) * 64], bigv[:, 8 + h, tok], bigv[:, 8 + h, tok],
                                                                  start=True, stop=True, tile_position=(0, 64 * hf)), k_keys, ["P4"])
            V(lambda e: e.scalar_tensor_tensor(Em, Em, 1.0, nbm, ALU.min, ALU.mult), ["Em", "nbm"], ["Em"])
            V(lambda e: e.scalar_tensor_tensor(MT0, P[4][:, :], -1.0, Em, ALU.mult, ALU.mult), ["P4", "Em"], ["MT0"])

            def mm_hh(out_bank, lhs, rhs, rkeys, wkey, rhs2=None):
                for h in range(H):
                    for hf in range(2):
                        r = HR[hf]
                        T(lambda e, h=h, r=r, hf=hf: e.matmul(out_bank[r, h * 64:(h + 1) * 64], h3(lhs)[r, h, :], h3(rhs)[r, h, :],
                                                             start=True, stop=(rhs2 is None), tile_position=(64 * hf, 64 * hf)), rkeys, [wkey])
                        if rhs2 is not None:
                            T(lambda e, h=h, r=r, hf=hf: e.matmul(out_bank[r, h * 64:(h + 1) * 64], h3(lhs)[r, h, :], h3(rhs2)[r, h, :],
                                                                 start=False, stop=True, tile_position=(64 * hf, 64 * hf)), rkeys, [wkey])

            def tr_hh(src, skey):
                for h in range(H):
                    for hf in range(2):
                        r = HR[hf]
                        T(lambda e, h=h, r=r, hf=hf: e.transpose(PT[r, h * 64:(h + 1) * 64], h3(src)[r, h, :], identb[r, r],
                                                                tile_position=(64 * hf, 64 * hf)), [skey, "identb"], ["PT"])

            tr_hh(MT0, "MT0")
            A(lambda e: e.activation(AA[0], PT[:, 0:512], AF.Copy), ["PT"], ["AA0"])
            V(lambda e: e.tensor_tensor(h3(Pb), h3(AA[0]), bc(ident2, [128, 8, 64], 1), ALU.add), ["AA0", "cst"], ["Pb"])
            def kv_tr(base, keys, dst, dkey, eng):
                for hf in range(2):
                    r = HR[hf]; tok = toks[hf]
                    for h in range(H):
                        T(lambda e, h=h, r=r, tok=tok, hf=hf: e.transpose(PT[r, h * 128:(h + 1) * 128], bigv[:, base + h, tok],
                                                                         identb[:, :], tile_position=(0, 64 * hf)),
                          keys + ["identb"], ["PT"])

            def kv_cp(dst, dkey, eng):
                if eng == "scalar":
                    A(lambda e: e.activation(dst, PT[:, :], AF.Copy), ["PT"], [dkey])
                else:
                    V(lambda e: e.tensor_copy(dst, PT[:, :]), ["PT"], [dkey])

            def qk_all():
                for hf in range(2):
                    r = HR[hf]; tok = toks[hf]
                    for h in range(H):
                        T(lambda e, h=h, r=r, tok=tok, hf=hf: e.matmul(P[2][r, h * 64:(h + 1) * 64], bigv[:, 8 + h, tok], bigv[:, h, tok],
                                                                      start=True, stop=True, tile_position=(0, 64 * hf)), qk_keys, ["P2"])
                V(lambda e: e.scalar_tensor_tensor(ETm3, ETm3, 1.0, bc(tri2, [128, 8, 64], 1), ALU.min, ALU.mult), ["ETm", "cst"], ["ETm"])
                V(lambda e: e.tensor_tensor(qkT, P[2][:, :], ETm, ALU.mult), ["P2", "ETm"], ["qkT"])
                V(lambda e: e.tensor_tensor(h3(qdTs[0]), bigv[:, 0:8, toks[0]], h3(EGs[0]), ALU.mult), q_keys + ["EG0"], ["qdT0"])
                G(lambda e: e.tensor_tensor(h3(qdTs[1]), bigv[:, 0:8, toks[1]], h3(EGs[1]), ALU.mult), q_keys + ["EG1"], ["qdT1"])

            def prod(atn, atnk):
                mm_hh(P[4], atn, Pb, [atnk, "Pb"], "P4")
                V(lambda e: e.tensor_tensor(Pb, Pb, P[4][:, :], ALU.add), ["Pb", "P4"], ["Pb"])

            cur = 0
            pending = None
            for k in range(2, NLEV + 1):
                nxt = 1 - cur
                atc, atk = (MT0, "MT0") if k == 2 else (AT[cur], "AT%d" % cur)
                mm_hh(P[5], AA[cur], atc, ["AA%d" % cur, atk], "P5")
                if k < NLEV:
                    mm_hh(P[6], atc, AA[cur], ["AA%d" % cur, atk], "P6")
                if k == 2:
                    kv_tr(8, k_keys, kTM, "kTM", "scalar")
                elif k == 3:
                    kv_tr(16, v_keys, vTM, "vTM", "vector")
                A(lambda e, nxt=nxt: e.activation(AT[nxt], P[5][:, :], AF.Copy), ["P5"], ["AT%d" % nxt])
                if k < NLEV:
                    V(lambda e, nxt=nxt: e.tensor_copy(AA[nxt], P[6][:, :]), ["P6"], ["AA%d" % nxt])
                if k == 2:
                    kv_cp(kTM, "kTM", "scalar")
                elif k == 3:
                    kv_cp(vTM, "vTM", "vector")
                if pending is not None:
                    prod(*pending)
                if k == 4:
                    qk_all()
                if k == NLEV and s_ + 1 < NCH:
                    dn_gates(s_ + 1)
                pending = (AT[nxt], "AT%d" % nxt)
                cur = nxt
            prod(*pending)
            mm_hh(P[4], MT0, Pb, ["MT0", "Pb"], "P4")
            V(lambda e: e.tensor_tensor(tmin, P[4][:, :], Pb, ALU.subtract), ["P4", "Pb"], ["tmin"])
            V(lambda e: e.tensor_tensor(h3(Rb), h3(tmin), bc(ident2, [128, 8, 64], 1), ALU.add), ["tmin", "cst"], ["Rb"])
            tr_hh(Pb, "Pb")
            A(lambda e: e.activation(PbT, PT[:, 0:512], AF.Copy), ["PT"], ["PbT"])
            mm_hh(P[5], PbT, Rb, ["PbT", "Rb"], "P5")
            V(lambda e: e.tensor_tensor(Pm, Pb, P[5][:, :], ALU.add), ["Pb", "P5"], ["Pm"])
            V(lambda e: e.tensor_tensor(h3(TTb), Pm3, bc(b8, [128, 8, 64], 2), ALU.mult), ["Pm", "bTM"], ["TTb"])
            V(lambda e: e.tensor_tensor(h3(TTbg), Pm3, bc(bg, [128, 8, 64], 2), ALU.mult), ["Pm", "bg" + kp], ["TTbg"])
            for h in range(H):
                for hf in range(2):
                    r = HR[hf]
                    pp = 5 if h < 4 else 6
                    T(lambda e, h=h, pp=pp, r=r, hf=hf: e.matmul(P[pp][r, (h % 4) * 128:(h % 4 + 1) * 128], h3(TTb)[r, h, :],
                                                                h3(vTM)[r, h, :], start=True, stop=True,
                                                                tile_position=(64 * hf, 64 * hf)), ["TTb", "vTM"], ["P%d" % pp])
            A(lambda e: e.activation(u_[:, 0:512], P[5][:, :], AF.Copy), ["P5", "tmin"], ["u0", "tmin"])
            V(lambda e: e.tensor_copy(u_[:, 512:1024], P[6][:, :]), ["P6"], ["u1"])
            for h in range(H):
                for hf in range(2):
                    r = HR[hf]
                    T(lambda e, h=h, r=r, hf=hf: e.matmul(P[hf][:, h * 64:(h + 1) * 64], h3(kTM)[r, h, :], h3(TTbg)[r, h, :],
                                                         start=True, stop=True, tile_position=(64 * hf, 0)),
                      ["kTM", "TTbg"], ["P%d" % hf])
            A(lambda e: e.activation(wTs[0], P[0][:, :], AF.Copy), ["P0"], ["wT0"])
            V(lambda e: e.tensor_copy(wTs[1], P[1][:, :]), ["P1"], ["wT1"])
            for hf in range(2):
                r = HR[hf]
                for h in range(H):
                    pp = 5 if h < 4 else 6
                    T(lambda e, h=h, pp=pp, r=r, hf=hf: e.matmul(P[pp][r, (h % 4) * 128:(h % 4 + 1) * 128], h3(wTs[hf])[:, h, :],
                                                                Sbs[hf][:, h * 128:(h + 1) * 128], start=True, stop=True,
                                                                tile_position=(0, 64 * hf)), ["wT%d" % hf, "Sb%d" % hf], ["P%d" % pp])
            V(lambda e: e.tensor_tensor(vn[:, 0:512], u_[:, 0:512], P[5][:, :], ALU.subtract), ["u0", "P5"], ["vn0"])
            V(lambda e: e.tensor_tensor(vn[:, 512:1024], u_[:, 512:1024], P[6][:, :], ALU.subtract), ["u1", "P6"], ["vn1"])
            V(lambda e: e.tensor_tensor(h3(vns), h3(vn), bc(ej, [128, 8, 128], 2), ALU.mult), ["vn0", "vn1", "ej" + kp], ["vns"])
            for hf in range(2):
                r = HR[hf]
                po = 2 + hf
                for h in range(H):
                    T(lambda e, h=h, hf=hf, po=po: e.matmul(P[po][:, h * 64:(h + 1) * 64], Sbs[hf][:, h * 128:(h + 1) * 128],
                                                           h3(qdTs[hf])[:, h, :], start=True, stop=False),
                      ["Sb%d" % hf, "qdT%d" % hf], ["P%d" % po])
                    T(lambda e, h=h, hf=hf, po=po, r=r: e.matmul(P[po][:, h * 64:(h + 1) * 64], h3(vn)[r, h, :], h3(qkT)[r, h, :],
                                                                start=False, stop=True, tile_position=(64 * hf, 0)),
                      ["vn0", "vn1", "qkT"], ["P%d" % po])
            for hf in range(2):
                po = 2 + hf; tok = toks[hf]
                if s_ < NCH // 2:
                    if hf == 0:
                        A(lambda e, po=po, tok=tok: e.activation(ofv[:, :, tok], h3(P[po][:, :]), AF.Copy), ["P%d" % po], ["of"])
                    else:
                        V(lambda e, po=po, tok=tok: e.tensor_copy(ofv[:, :, tok], h3(P[po][:, :])), ["P%d" % po], ["of"])
                else:
                    V(lambda e, po=po, tok=tok: e.tensor_tensor(h3(osum), h3(P[po][:, :]), ofv[:, :, tok], ALU.add),
                      ["P%d" % po, "of"], ["osum"])
                    A(lambda e: e.activation(sqo, osum, AF.Square), ["osum"], ["sqo"])
                    T(lambda e: e.matmul(P[4][:, :], onesb[:], sqo, start=True, stop=True), ["sqo", "onesb"], ["P4"])
                    A(lambda e: e.activation(rstd, P[4][:, :], AF.Ln, bias=epsc[:, 0:1], scale=1.0 / 128), ["P4", "epsc"], ["rstd"])
                    A(lambda e: e.activation(rstd, rstd, AF.Exp, scale=-0.5), ["rstd"], ["rstd"])
                    V(lambda e, tok=tok: e.scalar_tensor_tensor(ofv[:, :, tok], h3(osum), par["hng"][:, 0:1], h3(rstd),
                                                                ALU.mult, ALU.mult), ["osum", "rstd", "p_hng", "of"], ["of"])
            for hf in range(2):
                r = HR[hf]
                banks = (5, 6) if hf == 0 else (0, 1)
                for h in range(H):
                    pp = banks[0] if h < 4 else banks[1]
                    T(lambda e, h=h, pp=pp, r=r, hf=hf: e.matmul(P[pp][:, (h % 4) * 128:(h % 4 + 1) * 128], h3(kTM)[r, h, :],
                                                                h3(vns)[r, h, :], start=True, stop=True, tile_position=(64 * hf, 0)),
                      ["kTM", "vns"], ["P%d" % pp])
                St = Sts[hf]; sk = "St%d" % hf
                V(lambda e, St=St, hf=hf: e.tensor_tensor(h3(St), h3(St), bc(edec[:, 8 * hf:8 * hf + 8], [128, 8, 128], 2), ALU.mult),
                  [sk, "edec" + kp], [sk])
                V(lambda e, St=St, b0=banks[0]: e.tensor_tensor(St[:, 0:512], St[:, 0:512], P[b0][:, :], ALU.add), [sk, "P%d" % banks[0]], [sk])
                V(lambda e, St=St, b1=banks[1]: e.tensor_tensor(St[:, 512:1024], St[:, 512:1024], P[b1][:, :], ALU.add), [sk, "P%d" % banks[1]], [sk])
                if s_ % 4 == 3:
                    seq = (s_ // 4) if hf == 0 else ((NCH - 1 - s_) // 4)
                    S.dma("sync", st_d[l, seq, hf].rearrange("h k v -> k h v"), h3(St), reads=[sk],
                          writes=["st_out"], semkey="d_Sout%d" % hf, out_final=True)
                    V(lambda e, St=St, hf=hf: e.tensor_scalar(Sbs[hf], St, flg[:, 1:2], None, ALU.mult), [sk, "flg"], ["Sb%d" % hf])
                else:
                    A(lambda e, St=St, hf=hf: e.activation(Sbs[hf], St, AF.Copy), [sk], ["Sb%d" % hf])

        dn_gates(0)
        for s_ in range(NCH):
            dn_step(s_)

    accs = [big[:, 0:2048].bitcast(F32), big[:, 2048:4096].bitcast(F32)]
    acc_keys = [["big0", "big1"], ["big2", "big3"]]
    mean = big[:, 4096:6144].bitcast(F32); mean_keys = ["big4", "big5"]
    lrs = big[:, 6144:8192].bitcast(F32); lrs_keys = ["big6", "big7"]
    cttmp = big[:, 8192:10240].bitcast(F32); cttmp_keys = ["big8", "big9"]

    NDG = 6
    dgs = [R3[:, 64 * i:64 * (i + 1)].bitcast(BF16) for i in range(NDG)]
    dgctr = [0]

    def conv_tile(l, ct):
        acc = accs[ct % 2]; ak = acc_keys[ct % 2]
        ua = bigv[:, 16 + ct, :]; uk = "big%d" % (16 + ct)
        caw = par["caw"]
        banks = [0, 1] if ct % 2 == 0 else [2, 0]
        taps = []
        wmain, wk = (caw, "p_caw") if ct < 4 else (cawP, "cawP")
        order = [0] + [d for d in range(-15, 16) if d != 0]
        for d in order:
            j = ct * 31 + 15 + d
            if d == 0:
                taps.append((caw, "p_caw", j, "seg", d))
            else:
                taps.append((wmain, wk, j, "seg", d))
        if ct >= 4:
            for d in range(-15, 16):
                if d != 0:
                    taps.append((cawS, "cawS", ct * 31 + 15 + d, "vert", d))
        specs = {0: [], 1: []}
        for ti, (wt_, wk_, j, kind, d) in enumerate(taps):
            for hf in range(2):
                base = 512 * hf
                if kind == "seg":
                    e_ = abs(d)
                    if d >= 0:
                        o = (0, 256 - d); i_ = (d, 256)
                    else:
                        o = (e_, 256); i_ = (0, 256 - e_)
                    specs[hf].append((ti, "seg", o, i_))
                else:
                    if d > 0:
                        lo, hi = base, min(base + 512, NT - 64 * d); sh = 64 * d
                    else:
                        lo, hi = max(base, -64 * d), base + 512; sh = 64 * d
                    if hi > lo:
                        specs[hf].append((ti, "vert", (lo - base, hi - base), (lo + sh, hi + sh)))
        last = {hf: specs[hf][-1][0] for hf in range(2)}
        by_tap = {}
        for hf in range(2):
            for sp in specs[hf]:
                by_tap.setdefault(sp[0], []).append((hf, sp))
        for ti, (wt_, wk_, j, kind, d) in enumerate(taps):
            di = dgctr[0] % NDG
            dgctr[0] += 1
            dg = dgs[di]; dk = "dg%d" % di
            A(lambda e, dg=dg, wt_=wt_, j=j: e.activation(dg, identb[:, :], AF.Identity, bias=0.0, scale=wt_[:, j:j + 1]),
              ["identb", wk_], [dk])
            for hf, sp in by_tap.get(ti, []):
                pb = banks[hf]
                pv = P[pb][:, :].rearrange("p (s t) -> p s t", s=2)
                xv = ua[:, hs(hf)].rearrange("p (s t) -> p s t", s=2)
                first = (ti == 0)
                lastf = (ti == last[hf]) and (sp is [x for x in specs[hf] if x[0] == ti][-1])
                if sp[1] == "seg":
                    (o0, o1), (i0_, i1_) = sp[2], sp[3]
                    T(lambda e, dg=dg, pv=pv, xv=xv, o0=o0, o1=o1, i0_=i0_, i1_=i1_, first=first, lastf=lastf: e.matmul(
                        pv[:, :, o0:o1], dg, xv[:, :, i0_:i1_], start=first, stop=lastf), [dk, uk], ["P%d" % pb])
                else:
                    (o0, o1), (i0_, i1_) = sp[2], sp[3]
                    T(lambda e, dg=dg, pb=pb, o0=o0, o1=o1, i0_=i0_, i1_=i1_, lastf=lastf: e.matmul(
                        P[pb][:, o0:o1], dg, ua[:, i0_:i1_], start=False, stop=lastf), [dk, uk], ["P%d" % pb])
        for hf in range(2):
            pb = banks[hf]
            A(lambda e, pb=pb, hf=hf: e.activation(acc[:, hs(hf)], P[pb][:, :], AF.Identity, bias=par["cab"][:, ct:ct + 1], scale=1.0),
              ["P%d" % pb, "p_cab"], ak)
        if ct < 4:
            a4 = acc.rearrange("p (b s t) -> p b s t", b=4, s=4); x4 = ua.rearrange("p (b s t) -> p b s t", b=4, s=4)

            def tap(o_ap, i_ap, w_ap, wkey):
                V(lambda e: e.scalar_tensor_tensor(o_ap, i_ap, w_ap, o_ap, ALU.mult, ALU.add), [uk, wkey] + ak, ak)

            for d in list(range(-15, 0)) + list(range(1, 16)):
                j = ct * 31 + 15 + d
                e_ = abs(d)
                for sg in range(3):
                    if d > 0:
                        tap(a4[:, :, sg, 64 - d:64], x4[:, :, sg + 1, 0:d], cawNS[:, j:j + 1], "cawNS")
                    else:
                        tap(a4[:, :, sg + 1, 0:e_], x4[:, :, sg, 64 - e_:64], cawNS[:, j:j + 1], "cawNS")
        A(lambda e: e.activation(ua, acc, AF.Copy), ak, [uk])
        for half in range(2):
            A(lambda e, half=half: e.activation(sqs[half][:], acc[:, hs(half)], AF.Square), ak, ["sq%d" % half])
            T(lambda e, half=half: e.matmul(P[4 + half][:], onesb[:], ua[:, hs(half)], start=(ct == 0), stop=(ct == 7)),
              [uk, "onesb"], ["P%d" % (4 + half)])
            pq = 6 if half == 0 else 3
            T(lambda e, half=half, pq=pq: e.matmul(P[pq][:], onesb[:], sqs[half][:], start=(ct == 0), stop=(ct == 7)),
              ["sq%d" % half, "onesb"], ["P%d" % pq])

    def rest_of_layer(l, compute_stats_rs, hn_rhs, hn_keys, mod):
        def zb_cons(ft, half, ps, pk):
            A(lambda e: e.activation(sqs[half][:], ps, AF.Silu), [pk], ["sq%d" % half])
            V(lambda e: e.tensor_tensor(ofv[:, ft, hs(half)], ofv[:, ft, hs(half)], sqs[half][:], ALU.mult),
              ["of", "sq%d" % half], ["of"])
        wpool[0] = [0, 1, 2, 3]
        proj(w_in_d[l], COL["z_b"], 1024, KT, hn_rhs, hn_keys, zb_cons)

        def glu_cons(ft, half, ps, pk):
            A(lambda e: e.activation(bigv[:, 16 + ft, hs(half)], ps, AF.Sigmoid), [pk], ["big%d" % (16 + ft)])
        proj(w_in_d[l], COL["a_glu"], 1024, KT, hn_rhs, hn_keys, glu_cons)

        def val_cons(ft, half, ps, pk):
            V(lambda e: e.tensor_tensor(bigv[:, 16 + ft, hs(half)], bigv[:, 16 + ft, hs(half)], ps, ALU.mult),
              [pk, "big%d" % (16 + ft)], ["big%d" % (16 + ft)])
        proj(w_in_d[l], COL["a_val"], 1024, KT, hn_rhs, hn_keys, val_cons)
        wpool[0] = None

        for ct in range(8):
            conv_tile(l, ct)
            if l + 1 < n_layers:
                for gg in range(3 * ct, 3 * ct + 3):
                    if gg % 5 < 2:
                        mod_group(l + 1, gg)
                    else:
                        mod_group0(gg, l + 1, bi=gg % 5 - 1)
        if l + 1 < n_layers:
            mod_finish(l + 1, 0)
            mod_finish(l + 1, 1)
        for half in range(2):
            pq = 6 if half == 0 else 3
            A(lambda e, half=half: e.activation(mean[:, hs(half)], P[4 + half][:], AF.Identity, bias=0.0, scale=1.0 / 1024),
              ["P%d" % (4 + half)], mean_keys)
            A(lambda e, half=half: e.activation(tmpB[:], P[4 + half][:], AF.Square, scale=1.0 / 1024),
              ["P%d" % (4 + half)], ["tmpB"])
            V(lambda e, pq=pq: e.scalar_tensor_tensor(tmpB[:], P[pq][:], 1.0 / 1024, tmpB[:], ALU.mult, ALU.subtract),
              ["P%d" % pq, "tmpB"], ["tmpB"])
            A(lambda e, half=half: e.activation(lrs[:, hs(half)], tmpB[:], AF.Ln, bias=epsc[:, 0:1], scale=1.0),
              ["tmpB", "epsc"], lrs_keys)
            A(lambda e, half=half: e.activation(lrs[:, hs(half)], lrs[:, hs(half)], AF.Exp, scale=-0.5), lrs_keys, lrs_keys)
        for ct in range(8):
            uk = "big%d" % (16 + ct)
            V(lambda e, ct=ct: e.tensor_tensor(cttmp, bigv[:, 16 + ct, :], mean, ALU.subtract), [uk] + mean_keys, cttmp_keys)
            V(lambda e: e.tensor_tensor(cttmp, cttmp, lrs, ALU.mult), cttmp_keys + lrs_keys, cttmp_keys)
            A(lambda e, ct=ct: e.activation(bigv[:, 16 + ct, :], cttmp, AF.Silu, bias=par["lnb"][:, ct:ct + 1],
                                            scale=par["lng"][:, ct:ct + 1]), cttmp_keys + ["p_lnb", "p_lng"], [uk])

        def az_cons(ft, half, ps, pk):
            A(lambda e: e.activation(sqs[half][:], ps, AF.Silu), [pk], ["sq%d" % half])
            V(lambda e: e.tensor_tensor(bigv[:, 16 + ft, hs(half)], bigv[:, 16 + ft, hs(half)], sqs[half][:], ALU.mult),
              ["big%d" % (16 + ft), "sq%d" % half], ["big%d" % (16 + ft)])
        proj(w_in_d[l], COL["a_z"], 1024, KT, hn_rhs, hn_keys, az_cons)

        sgbv = sgb.rearrange("p (f t) -> p f t", f=2)
        pa_rhs = lambda kt, half: bigv[:, 16 + kt, hs(half)]
        pa_keys = lambda kt: ["big%d" % (16 + kt)]
        pb_rhs = lambda kt, half: ofv[:, kt, hs(half)]
        pb_keys = lambda kt: ["of"]
        for g in range(8):
            def ga_cons(ft, half, ps, pk, g=g):
                A(lambda e: e.activation(bigv[:, 2 * g + ft, hs(half)], ps, AF.Sigmoid), [pk], ["big%d" % (2 * g + ft)])
            proj(w_in_d[l], COL["gate_a"] + g * 256, 256, KT, hn_rhs, hn_keys, ga_cons)

            def oa_cons(ft, half, ps, pk, g=g):
                V(lambda e: e.tensor_tensor(bigv[:, 2 * g + ft, hs(half)], bigv[:, 2 * g + ft, hs(half)], ps, ALU.mult),
                  [pk, "big%d" % (2 * g + ft)], ["big%d" % (2 * g + ft)])
            proj(w_pa_d[l], g * 256, 256, 8, pa_rhs, pa_keys, oa_cons)

            def gb_cons(ft, half, ps, pk, g=g):
                A(lambda e: e.activation(sgbv[:, ft, hs(half)], ps, AF.Sigmoid), [pk], ["R3a"])
            proj(w_in_d[l], COL["gate_b"] + g * 256, 256, KT, hn_rhs, hn_keys, gb_cons)

            def ob_cons(ft, half, ps, pk, g=g):
                V(lambda e: e.tensor_tensor(tmpB[:], sgbv[:, ft, hs(half)], ps, ALU.mult), [pk, "R3a"], ["tmpB"])
                V(lambda e: e.tensor_tensor(bigv[:, 2 * g + ft, hs(half)], bigv[:, 2 * g + ft, hs(half)], tmpB[:], ALU.add),
                  ["tmpB", "big%d" % (2 * g + ft)], ["big%d" % (2 * g + ft)])
            proj(w_pb_d[l], g * 256, 256, 8, pb_rhs, pb_keys, ob_cons)

        def wo_cons(ft, half, ps, pk):
            V(lambda e: e.scalar_tensor_tensor(xTv[:, ft, hs(half)], ps, mod[:, 32 + ft:33 + ft], xTv[:, ft, hs(half)],
                                               ALU.mult, ALU.add), [pk, "modg%d" % l, "x%d" % ft], ["x%d" % ft])
        proj(w_o_d[l], 0, D, KT, lambda kt, half: bigv[:, kt, hs(half)], lambda kt: ["big%d" % kt], wo_cons)

    def final_norm(compute_stats_rs):
        compute_stats_rs(1.0 / D, epsc[:, 0:1])
        bufs = [(tmpA, "tmpA"), (R3[:, 0:NT], "R3a")]
        for kt in range(KT):
            buf, bk = bufs[kt % 2]
            V(lambda e, kt=kt, buf=buf: e.scalar_tensor_tensor(buf, xTv[:, kt, :], fng[:, kt:kt + 1], rs, ALU.mult, ALU.mult),
              ["x%d" % kt, "fng", "rs"], [bk])
            S.dma("sync", yT_d[kt * 128:(kt + 1) * 128, :], buf, reads=[bk], writes=["yT%d" % kt], semkey="d_y" + bk,
                  out_final=True)

    for l in range(n_layers):
        grp = "par%d" % (l + 1)
        for nm, d_ap in (("caw", caw_d), ("cab", cab_d), ("lng", lng_d),
                         ("lnb", lnb_d), ("cqw", cqw_d), ("hng", hng_d)):
            S.dma("sync", par[nm][:], d_ap[l], writes=["p_" + nm], group=grp)
        S.dma("sync", alog[:], alog_d[l], writes=["alog"], group=grp)
        S.dma("sync", dtb[:], dtb_d[l], writes=["dtb"], group=grp)
        S.dma("gpsimd", wg[:].rearrange("p (k c) -> p k c", k=KT),
              w_in_d[l].rearrange("(k p) c -> p k c", p=128)[:, :, COL["beta"]:COL["beta"] + 32],
              writes=["wg"], semkey="d_wg")
        caw = par["caw"]
        V(lambda e: e.tensor_scalar(cawP[:], caw[:], flg[:, 0:1], None, ALU.mult), ["p_caw", "flg"], ["cawP"])
        V(lambda e: e.tensor_scalar(cawS[:], caw[:], flg[:, 1:2], None, ALU.mult), ["p_caw", "flg"], ["cawS"])
        V(lambda e: e.tensor_scalar(cawNS[:], caw[:], flg[:, 3:4], None, ALU.mult), ["p_caw", "flg"], ["cawNS"])
        cq3 = par["cqw"][:].rearrange("p (f j) -> p f j", j=3)
        V(lambda e: e.tensor_scalar(cqwL[:], cq3[:, :, 0], flg[:, 2:3], None, ALU.mult), ["p_cqw", "flg"], ["cqwL"])
        V(lambda e: e.tensor_scalar(cqwR[:], cq3[:, :, 2], flg[:, 2:3], None, ALU.mult), ["p_cqw", "flg"], ["cqwR"])
        A(lambda e: e.activation(negA[:], alog[:], AF.Exp), ["alog"], ["negA"])
        V(lambda e: e.tensor_scalar(negA[:], negA[:], -1.0, None, ALU.mult), ["negA"], ["negA"])

        mod = mods[l]; modA = modAs[l]
        if l == 0:
            for g in range(24):
                mod_group0(g)
            mod_finish(0, 0)
            mod_finish(0, 1)

        def compute_stats_rs(scale, bias_ap):
            for half in range(2):
                for kt in range(KT):
                    sq = sqs[kt % 2]
                    A(lambda e, sq=sq, kt=kt, half=half: e.activation(sq[:], xTv[:, kt, hs(half)], AF.Square),
                      ["x%d" % kt], ["sq%d" % (kt % 2)])
                    T(lambda e, sq=sq, kt=kt: e.matmul(P[5][:], onesb[:], sq[:], start=(kt == 0), stop=(kt == KT - 1)),
                      ["sq%d" % (kt % 2), "onesb"], ["P5"])
                A(lambda e, half=half: e.activation(rs[:, hs(half)], P[5][:], AF.Ln, bias=bias_ap, scale=scale),
                  ["P5", "epsc"], ["rs"])
                A(lambda e, half=half: e.activation(rs[:, hs(half)], rs[:, hs(half)], AF.Exp, scale=-0.5), ["rs"], ["rs"])

        def compute_hn(mod=mod, modA=modA, l=l):
            compute_stats_rs(1.0 / D, epsc[:, 0:1])
            for kt in range(KT):
                V(lambda e, kt=kt: e.scalar_tensor_tensor(tmpA, xTv[:, kt, :], modA[:, kt:kt + 1], rs,
                                                          ALU.mult, ALU.mult),
                  ["x%d" % kt, "modA%d" % l, "rs"], ["tmpA"])
                A(lambda e, kt=kt: e.activation(hnT[:, kt, :], tmpA, AF.Identity, bias=mod[:, kt:kt + 1], scale=1.0),
                  ["tmpA", "mod%d" % l], ["hn%d" % kt])

        hn_rhs = lambda kt, half: hnT[:, kt, hs(half)]
        hn_keys = lambda kt: ["hn%d" % kt]
        compute_hn()

        def qkv_consumer(base_ft):
            def cons(ft, half, ps, pk):
                f = base_ft + ft
                A(lambda e: e.activation(pre[:, 1 + half * 512:1 + (half + 1) * 512], ps, AF.Copy), [pk], ["R3a"])
                if half == 1:
                    cw = par["cqw"]
                    V(lambda e: e.tensor_scalar(tmpA, pre[:, 1:NT + 1], cw[:, 3 * f + 1:3 * f + 2], None, ALU.mult),
                      ["R3a", "p_cqw"], ["tmpA"])
                    V(lambda e: e.scalar_tensor_tensor(tmpA, pre[:, 0:NT], cw[:, 3 * f:3 * f + 1], tmpA,
                                                       ALU.mult, ALU.add), ["R3a", "p_cqw", "tmpA"], ["tmpA"])
                    V(lambda e: e.scalar_tensor_tensor(tmpA, pre[:, 2:NT + 2], cw[:, 3 * f + 2:3 * f + 3], tmpA,
                                                       ALU.mult, ALU.add), ["R3a", "p_cqw", "tmpA"], ["tmpA"])
                    tv = tmpA.rearrange("p (s t) -> p s t", s=4)
                    pv = pre[:, 1:NT + 1].rearrange("p (s t) -> p s t", s=4)
                    V(lambda e: e.scalar_tensor_tensor(tv[:, 1:4, 0:1], pv[:, 0:3, 255:256], cqwL[:, f:f + 1],
                                                       tv[:, 1:4, 0:1], ALU.mult, ALU.add),
                      ["R3a", "cqwL", "tmpA"], ["tmpA"])
                    V(lambda e: e.scalar_tensor_tensor(tv[:, 0:3, 255:256], pv[:, 1:4, 0:1], cqwR[:, f:f + 1],
                                                       tv[:, 0:3, 255:256], ALU.mult, ALU.add),
                      ["R3a", "cqwR", "tmpA"], ["tmpA"])
                    A(lambda e: e.activation(bigv[:, f, :], tmpA, AF.Silu), ["tmpA"], ["big%d" % f])
            return cons

        V(lambda e: e.memset(pre[:, 0:1], 0.0), [], ["R3a"])
        V(lambda e: e.memset(pre[:, NT + 1:NT + 2], 0.0), ["R3a"], ["R3a"])
        proj(w_in_d[l], COL["q"], 3072, KT, hn_rhs, hn_keys, qkv_consumer(0))

        wg5 = wg[:].rearrange("p (k a d e) -> p k a d e", k=KT, a=2, d=2, e=8)
        for s_ in range(NCH):
            for hf in range(2):
                c = s_ if hf == 0 else NCH - 1 - s_
                for kt in range(KT):
                    T(lambda e, c=c, kt=kt, hf=hf, s_=s_: e.matmul(
                        P[4][hf * 64:hf * 64 + 64, s_ * 16:(s_ + 1) * 16].rearrange("p (a e) -> p a e", a=2),
                        hnT[:, kt, c * C:(c + 1) * C], wg5[:, kt, :, hf, :], start=(kt == 0), stop=(kt == KT - 1),
                        tile_position=(0, 64 * hf)), ["hn%d" % kt, "wg"], ["P4"])
        p4v = P[4][:, 0:NCH * 16].rearrange("p (s a e) -> p s a e", s=NCH, a=2)
        gTMv = gTM[:].rearrange("p (c g) -> p c g", g=8)
        bTMv = bTM[:].rearrange("p (c g) -> p c g", g=8)
        A(lambda e: e.activation(bTMv, p4v[:, :, 0, :], AF.Sigmoid), ["P4"], ["bTM"])
        V(lambda e: e.tensor_tensor(gTMv, p4v[:, :, 1, :], bc(dtb[:], [128, NCH, 8], 1), ALU.add), ["P4", "dtb"], ["gTM"])
        A(lambda e: e.activation(gTM[:], gTM[:], AF.Exp), ["gTM"], ["gTM"])
        S.op("scalar", lambda e: e.activation(gTM[:], gTM[:], AF.Ln, bias=epsc[:, 2:3], scale=1.0), ["gTM", "epsc"], ["gTM"], strict=True)
        V(lambda e: e.tensor_tensor(gTMv, gTMv, bc(negA[:], [128, NCH, 8], 1), ALU.mult), ["gTM", "negA"], ["gTM"])

        S.barrier()
        deltanet(l)
        S.barrier()
        compute_hn()
        rest_of_layer(l, compute_stats_rs, hn_rhs, hn_keys, mod)
    final_norm(compute_stats_rs)
    S.finish()
    return nc


_PROG = {}
_DEBUG = False
_LAST = {}


def _consts():
    c = np.zeros((128, 6 * 128), np.float32)
    c[:, 0:128] = np.eye(128, dtype=np.float32)
    r = np.arange(64)[:, None]; q = np.arange(64)[None, :]
    c[0:64, 128:192] = (r <= q); c[64:128, 128:192] = (r >= q)
    c[0:64, 192:256] = (r > q); c[64:128, 192:256] = (r < q)
    c[0:64, 256:320] = np.eye(64); c[64:128, 256:320] = np.eye(64)
    return c


def _fm(v, nt):
    return np.ascontiguousarray(np.asarray(v, np.float32).reshape(nt, 128).T)


def kernel(x_prompt, x_sample, state_delta, c, c_ctx, w_mod, b_mod, norm_g, w_in, conv_a_w, conv_a_b,
           ln_a_g, ln_a_b, w_pa, conv_qkv_w, a_log, dt_bias, head_norm_g, w_pb, w_o, final_norm_g):
    f32 = np.float32
    if "nc" not in _PROG:
        _PROG["nc"] = build_program(debug=_DEBUG)
    nc = _PROG["nc"]
    x_prompt = np.asarray(x_prompt, f32); x_sample = np.asarray(x_sample, f32)
    state_delta = np.asarray(state_delta, f32)
    shared = {
        "cst": _consts(),
        "w_mod": np.ascontiguousarray(np.asarray(w_mod, f32)),
        "b_mod": np.stack([_fm(b_mod[l], 48) for l in range(DEPTH)]),
        "norm_g": np.stack([_fm(norm_g[l], KT) for l in range(DEPTH)]),
        "w_in": np.ascontiguousarray(np.asarray(w_in, f32)),
        "caw": np.stack([np.ascontiguousarray(np.asarray(conv_a_w[l], f32).T.reshape(8, 128, 31).transpose(1, 0, 2)).reshape(128, 248)
                         for l in range(DEPTH)]),
        "cab": np.stack([_fm(conv_a_b[l], 8) for l in range(DEPTH)]),
        "lng": np.stack([_fm(ln_a_g[l], 8) for l in range(DEPTH)]),
        "lnb": np.stack([_fm(ln_a_b[l], 8) for l in range(DEPTH)]),
        "w_pa": np.ascontiguousarray(np.asarray(w_pa, f32)),
        "cqw": np.stack([np.ascontiguousarray(np.asarray(conv_qkv_w[l], f32).T.reshape(24, 128, 3).transpose(1, 0, 2)).reshape(128, 72)
                         for l in range(DEPTH)]),
        "alog": np.stack([np.repeat(np.asarray(a_log[l], f32).reshape(2, 8), 64, axis=0) for l in range(DEPTH)]),
        "dtb": np.stack([np.repeat(np.asarray(dt_bias[l], f32).reshape(2, 8), 64, axis=0) for l in range(DEPTH)]),
        "hng": np.asarray(head_norm_g, f32).reshape(DEPTH, 128, 1).copy(),
        "w_pb": np.ascontiguousarray(np.asarray(w_pb, f32)),
        "w_o": np.ascontiguousarray(np.asarray(w_o, f32)),
        "fng": _fm(final_norm_g, KT),
    }
    in_maps = []
    for core in range(8):
        m = dict(shared)
        if core < 4:
            xt = x_prompt[4 * core:4 * core + 4].reshape(NT, D)
            m["cv"] = _fm(c_ctx, KT)
            m["s0"] = np.zeros((DEPTH, 2, 128, H * 128), f32)
            m["flg"] = np.ascontiguousarray(np.broadcast_to(np.array([1, 0, -1, 0], f32), (128, 4)))
        else:
            b = core - 4
            xt = x_sample[b]
            m["cv"] = _fm(np.asarray(c, f32)[b], KT)
            m["s0"] = np.ascontiguousarray(state_delta[b].transpose(0, 1, 3, 2, 4)).reshape(DEPTH, 2, 128, H * 128)
            m["flg"] = np.ascontiguousarray(np.broadcast_to(np.array([0, 1, 0, -1], f32), (128, 4)))
        m["xT"] = np.ascontiguousarray(xt.T)
        in_maps.append(m)
    res = run_bass_kernel_spmd(nc, in_maps, core_ids=list(range(8)))
    r = res.results
    _LAST['r'] = r
    y_prompt = np.stack([r[i]["yT"].T.reshape(4, 256, D) for i in range(4)]).reshape(16, 256, D)
    y_sample = np.stack([r[4 + b]["yT"].T for b in range(4)])
    st = np.concatenate([r[i]["st"].transpose(1, 0, 2, 3, 4, 5) for i in range(4)], axis=0)
    return (np.ascontiguousarray(y_prompt, dtype=f32), np.ascontiguousarray(y_sample, dtype=f32),
            np.ascontiguousarray(st, dtype=f32))
```

```python
from contextlib import ExitStack
import numpy as np
import concourse.bass as bass
import concourse.mybir as mybir
from concourse.bass_utils import run_bass_kernel_spmd

F32 = mybir.dt.float32
BF16 = mybir.dt.bfloat16
AF = mybir.ActivationFunctionType
ALU = mybir.AluOpType

SAME_ENGINE_SYNC = False


class Sched:
    ENGS = ["tensor", "vector", "scalar", "gpsimd", "sync"]

    def __init__(self, nc):
        self.nc = nc
        self.stack = ExitStack()
        self.ops = {e: [] for e in self.ENGS}
        self.count = {}
        self.seen = {e: {} for e in self.ENGS}
        self.last_w = {}
        self.readers = {}
        self.sem_names = set(self.ENGS)
        self.final_group = set()
        self.out_sems = set()
        self.barrier_toks = []

    def barrier(self):
        self.barrier_toks = [(k, v) for k, v in self.count.items()]

    def sb(self, name, shape, dt):
        return self.stack.enter_context(self.nc.sbuf_tensor(name, shape, dt))

    def ps(self, name, shape, dt):
        return self.stack.enter_context(self.nc.psum_tensor(name, shape, dt))

    def _deps(self, eng, reads, writes, strict=False):
        toks = list(self.barrier_toks)
        for k in list(reads) + list(writes):
            t = self.last_w.get(k)
            if t is not None:
                toks.append(t)
        for k in writes:
            toks.extend(self.readers.get(k, []))
        waits = {}
        for (s, v) in toks:
            if s == eng and (eng in ("tensor", "sync") or not (SAME_ENGINE_SYNC or strict)):
                continue
            if self.seen[eng].get(s, 0) >= v:
                continue
            waits[s] = max(waits.get(s, 0), v)
        for s, v in waits.items():
            self.seen[eng][s] = v
        return sorted(waits.items())

    def _commit(self, tok, reads, writes):
        for k in writes:
            self.last_w[k] = tok
            self.readers[k] = []
        for k in reads:
            if k not in writes:
                self.readers.setdefault(k, []).append(tok)

    def op(self, eng, fn, reads=(), writes=(), strict=False):
        waits = self._deps(eng, reads, writes, strict)
        self.count[eng] = self.count.get(eng, 0) + 1
        tok = (eng, self.count[eng])
        self.ops[eng].append((waits, fn, (eng, 1)))
        self._commit(tok, reads, writes)

    def dma(self, eng, out, in_, reads=(), writes=(), semkey=None, out_final=False, group=None):
        if group is not None:
            semkey = "g_" + group
            self.final_group.add(semkey)
        if semkey is None:
            semkey = "d_" + str((list(writes) + list(reads))[0])
        self.sem_names.add(semkey)
        if out_final:
            self.out_sems.add(semkey)
        waits = self._deps(eng, reads, writes)
        self.count[semkey] = self.count.get(semkey, 0) + 16
        tok = (semkey, self.count[semkey])
        self.ops[eng].append((waits, lambda e: e.dma_start(out=out, in_=in_), (semkey, 16)))
        self._commit(tok, reads, writes)

    def finish(self):
        nc = self.nc
        sems = {}
        for i, s in enumerate(sorted(self.sem_names)):
            sems[s] = self.stack.enter_context(nc.semaphore("s%d" % i))
        final = dict(self.count)
        ops = self.ops
        fg = self.final_group
        out_sems = sorted(self.out_sems)

        def emit(eng_name):
            def body(e):
                for waits, fn, (isem, iv) in ops[eng_name]:
                    for s, v in waits:
                        if s in fg:
                            v = final[s]
                        e.wait_ge(sems[s], v)
                    ins = fn(e)
                    ins.then_inc(sems[isem], iv)
                if eng_name == "sync":
                    for s in out_sems:
                        e.wait_ge(sems[s], final[s])
            return body

        with nc.Block() as block:
            block.tensor(emit("tensor"))
            block.vector(emit("vector"))
            block.scalar(emit("scalar"))
            block.gpsimd(emit("gpsimd"))
            block.sync(emit("sync"))
        self.stack.close()


D = 2048
NT = 1024
DEPTH = 2
KT = 16
H = 8
C = 64
NCH = NT // C
IN_W = 11296
EPS = 1e-6
NLEV = 5
COL = dict(a_val=0, a_glu=1024, a_z=2048, q=3072, k=4096, v=5120, z_b=6144,
           beta=7168, alpha=7184, gate_a=7200, gate_b=9248)


def build_program(n_layers=DEPTH, debug=False):
    nc = bass.Bass("TRN2", target_bir_lowering=False)
    S = Sched(nc)

    def din(name, shape):
        return nc.dram_tensor(name, shape, F32, kind="ExternalInput").ap()

    xT_d = din("xT", [D, NT])
    cv_d = din("cv", [128, KT])
    s0_d = din("s0", [DEPTH, 2, 128, H * 128])
    flg_d = din("flg", [128, 4])
    cst_d = din("cst", [128, 6 * 128])
    w_mod_d = din("w_mod", [DEPTH, D, 3 * D])
    b_mod_d = din("b_mod", [DEPTH, 128, 48])
    ng_d = din("norm_g", [DEPTH, 128, KT])
    w_in_d = din("w_in", [DEPTH, D, IN_W])
    caw_d = din("caw", [DEPTH, 128, 8 * 31])
    cab_d = din("cab", [DEPTH, 128, 8])
    lng_d = din("lng", [DEPTH, 128, 8])
    lnb_d = din("lnb", [DEPTH, 128, 8])
    w_pa_d = din("w_pa", [DEPTH, 1024, D])
    cqw_d = din("cqw", [DEPTH, 128, 24 * 3])
    alog_d = din("alog", [DEPTH, 128, 8])
    dtb_d = din("dtb", [DEPTH, 128, 8])
    hng_d = din("hng", [DEPTH, 128, 1])
    w_pb_d = din("w_pb", [DEPTH, 1024, D])
    w_o_d = din("w_o", [DEPTH, D, D])
    fng_d = din("fng", [128, KT])
    yT_d = nc.dram_tensor("yT", [D, NT], F32, kind="ExternalOutput").ap()
    st_d = nc.dram_tensor("st", [DEPTH, 4, 2, H, 128, 128], F32, kind="ExternalOutput").ap()

    xT = S.sb("xTs", [128, KT * NT], F32)
    xTv = xT[:].rearrange("p (k t) -> p k t", k=KT)
    big = S.sb("big", [128, 24 * NT], BF16)
    bigv = big[:].rearrange("p (s t) -> p s t", s=24)
    of = S.sb("of", [128, 8 * NT], BF16)
    ofv = of[:].rearrange("p (s t) -> p s t", s=8)
    R1 = S.sb("R1", [128, 8192], F32)
    hnT = R1[:].bitcast(BF16).rearrange("p (k t) -> p k t", k=KT)
    R3 = S.sb("R3", [128, 3080], F32)
    wbs = [S.sb("wb%d" % i, [128, 16 * 256], BF16) for i in range(2)]
    wmf = R3[:, 0:1024]
    cst = S.sb("cst_s", [128, 6 * 128], F32)
    identb = S.sb("identb", [128, 128], BF16)
    onesb = S.sb("onesb", [128, 128], BF16)
    onesf = S.sb("onesf", [128, 128], F32)
    negones = S.sb("negones", [128, 64], F32)
    negtri = S.sb("negtri", [128, 64], F32)
    flg = S.sb("flg_s", [128, 4], F32)
    epsc = S.sb("epsc", [128, 3], F32)
    cvs = S.sb("cvs", [128, KT], F32)
    fng = S.sb("fng_s", [128, KT], F32)
    mods = [S.sb("mod%d" % i, [128, 48], F32) for i in range(DEPTH)]
    modAs = [S.sb("modA%d" % i, [128, KT], F32) for i in range(DEPTH)]
    bmods = [S.sb("bmod%d" % i, [128, 48], F32) for i in range(DEPTH)]
    ngs = [S.sb("ngs%d" % i, [128, KT], F32) for i in range(DEPTH)]
    scb = S.sb("scb", [128, KT], BF16)
    par = {}
    for nm, w in (("caw", 248), ("cab", 8), ("lng", 8), ("lnb", 8),
                  ("cqw", 72), ("hng", 1)):
        par[nm] = S.sb("p_" + nm, [128, w], F32)
    cawP = S.sb("cawP", [128, 248], F32)
    cawS = S.sb("cawS", [128, 248], F32)
    cawNS = S.sb("cawNS", [128, 248], F32)
    cqwL = S.sb("cqwL", [128, 24], F32)
    cqwR = S.sb("cqwR", [128, 24], F32)
    alog = S.sb("alog_s", [128, 8], F32)
    dtb = S.sb("dtb_s", [128, 8], F32)
    negA = S.sb("negA", [128, 8], F32)
    gTM = S.sb("gTM", [128, NCH * 8], F32)
    bTM = S.sb("bTM", [128, NCH * 8], F32)
    wg = S.sb("wg", [128, KT * 32], BF16)
    St = R3[:, 1032:2056]
    tmpA = R3[:, 1032:2056]
    tmpB = S.sb("tmpB", [128, 512], F32)
    rs = R3[:, 2056:3080]
    sqs = [S.sb("sq%d" % i, [128, 512], BF16) for i in range(2)]
    sgb = R3[:, 0:1024].bitcast(BF16)
    pre = R3[:, 0:NT + 2]

    P = [S.ps("P%d" % i, [128, 512], F32) for i in range(7)]
    PT = S.ps("PT", [128, 1024], BF16)

    def T(fn, r, w): S.op("tensor", fn, r, w)
    def V(fn, r, w): S.op("vector", fn, r, w)
    def A(fn, r, w): S.op("scalar", fn, r, w)
    def G(fn, r, w): S.op("gpsimd", fn, r, w)

    S.dma("sync", cst[:], cst_d, writes=["cst"], group="par0")
    S.dma("sync", flg[:], flg_d, writes=["flg"], group="par0")
    S.dma("sync", cvs[:], cv_d, writes=["cvs"], group="par0")
    S.dma("sync", fng[:], fng_d, writes=["fng"], group="par0")
    S.dma("gpsimd", identb[:], cst_d[:, 0:128], writes=["identb"], group="par0c")
    for kt in range(KT):
        S.dma("sync", xTv[:, kt, :], xT_d[kt * 128:(kt + 1) * 128, :], writes=["x%d" % kt], group="xin")
    for i in range(DEPTH):
        S.dma("sync", bmods[i][:], b_mod_d[i], writes=["bmod%d" % i], group="par0")
        S.dma("sync", ngs[i][:], ng_d[i], writes=["ngs%d" % i], group="par0")
    A(lambda e: e.activation(scb[:], cvs[:], AF.Silu), ["cvs"], ["scb"])
    V(lambda e: e.memset(onesb[:], 1.0), [], ["onesb"])
    V(lambda e: e.memset(onesf[:], 1.0), [], ["onesf"])
    V(lambda e: e.memset(negones[:], -1.0), [], ["negones"])
    V(lambda e: e.memset(epsc[:, 0:1], EPS), [], ["epsc"])
    V(lambda e: e.memset(epsc[:, 1:2], 128.0 * EPS), ["epsc"], ["epsc"])
    V(lambda e: e.memset(epsc[:, 2:3], 1.0), ["epsc"], ["epsc"])
    tri2 = cst[:, 128:192]
    maskS2 = cst[:, 192:256]
    ident2 = cst[:, 256:320]
    V(lambda e: e.tensor_scalar(negtri[:], tri2, -1.0, None, ALU.mult), ["cst"], ["negtri"])

    def bc(ap, shape, axis):
        return ap.unsqueeze(axis).to_broadcast(shape)

    wctr = [0]

    wpool = [None]

    def load_w(src3, ktn, ncols):
        extra = wpool[0] or []
        i = wctr[0] % (2 + len(extra))
        wctr[0] += 1
        if i >= 2:
            bi = extra[i - 2]
            dst = big[:, bi * 4096:bi * 4096 + ktn * ncols].rearrange("p (k c) -> p k c", k=ktn)
            keys = ["big%d" % (4 * bi + j) for j in range(4)]
            S.dma("gpsimd", dst, src3, writes=keys, semkey="d_bigw%d" % bi)
            return dst, keys
        key = "wb%d" % i
        dst = wbs[i][:, 0:ktn * ncols].rearrange("p (k c) -> p k c", k=ktn)
        S.dma("gpsimd", dst, src3, writes=[key])
        return dst, [key]

    pctr = [0]

    def proj(w2d, c0, ncols_total, ktn, rhs_fn, rhs_keys_fn, consumer, gcols=256, after_group=None):
        wv = w2d.rearrange("(k p) c -> p k c", p=128)
        for g0 in range(0, ncols_total, gcols):
            gc_ = min(gcols, ncols_total - g0)
            wt, wkey = load_w(wv[:, :, c0 + g0:c0 + g0 + gc_], ktn, gc_)
            for f0 in range(0, gc_, 128):
                fw = min(128, gc_ - f0)
                ft = (g0 + f0) // 128
                for half in range(2):
                    pi = pctr[0] % 4
                    pctr[0] += 1
                    pk = "P%d" % pi
                    for kt in range(ktn):
                        T(lambda e, kt=kt, pi=pi, f0=f0, fw=fw, half=half, wt=wt: e.matmul(
                            P[pi][0:fw, :], wt[:, kt, f0:f0 + fw], rhs_fn(kt, half),
                            start=(kt == 0), stop=(kt == ktn - 1)),
                          wkey + rhs_keys_fn(kt), [pk])
                    consumer(ft, half, P[pi][0:fw, :], pk)
            if after_group is not None:
                after_group(g0 // gcols)

    def hs(half):
        return slice(half * 512, (half + 1) * 512)


    PTf = PT[:].bitcast(F32)

    def mod_group(l, g):
        wv = w_mod_d[l].rearrange("(k p) c -> p k c", p=128)
        wt, wkey = load_w(wv[:, :, g * 256:(g + 1) * 256], KT, 256)
        for f in range(2):
            n = g * 2 + f
            for kt in range(KT):
                T(lambda e, kt=kt, n=n, f=f, wt=wt: e.matmul(PTf[:, n:n + 1], wt[:, kt, f * 128:(f + 1) * 128], scb[:, kt:kt + 1],
                                                            start=(kt == 0), stop=(kt == KT - 1)), wkey + ["scb"], ["PT"])

    def mod_group0(g, l=0, bi=None):
        wv = w_mod_d[l].rearrange("(k p) c -> p k c", p=128)
        if bi is None:
            bi = g % 6
        wt = big[:, bi * 4096:(bi + 1) * 4096].rearrange("p (k c) -> p k c", k=KT)
        bkeys = ["big%d" % (4 * bi + j) for j in range(4)]
        S.dma("gpsimd", wt, wv[:, :, g * 256:(g + 1) * 256], writes=bkeys, semkey="d_bigw%d" % bi)
        for f in range(2):
            n = g * 2 + f
            for kt in range(KT):
                T(lambda e, kt=kt, n=n, f=f, wt=wt: e.matmul(PTf[:, n:n + 1], wt[:, kt, f * 128:(f + 1) * 128], scb[:, kt:kt + 1],
                                                            start=(kt == 0), stop=(kt == KT - 1)), bkeys + ["scb"], ["PT"])

    def mod_finish(l, part):
        if part == 0:
            V(lambda e: e.tensor_tensor(mods[l][:, 0:32], PTf[:, 0:32], bmods[l][:, 0:32], ALU.add), ["PT", "bmod%d" % l], ["mod%d" % l])
            S.op("vector", lambda e: e.scalar_tensor_tensor(modAs[l][:], mods[l][:, 16:32], 1.0, ngs[l][:], ALU.add, ALU.mult),
                 ["mod%d" % l, "ngs%d" % l], ["modA%d" % l], strict=True)
        else:
            V(lambda e: e.tensor_tensor(mods[l][:, 32:48], PTf[:, 32:48], bmods[l][:, 32:48], ALU.add), ["PT", "bmod%d" % l], ["modg%d" % l])

    def f32v(R, off, n):
        return R[:, off:off + n]

    def b16v(R, off, n):
        return R[:, off:off + n // 2].bitcast(BF16)

    def h3(ap, h=H):
        return ap.rearrange("p (h t) -> p h t", h=h)

    W0 = wbs[0][:].bitcast(F32)
    W1 = wbs[1][:].bitcast(F32)
    gB = f32v(R1, 0, 512); gTri = f32v(R1, 512, 512); Em = f32v(R1, 1024, 512); ETm = f32v(R1, 1536, 512)
    Pm = f32v(R1, 2048, 512); EGs = [f32v(R1, 2560, 512), f32v(R1, 3072, 512)]
    u_ = f32v(R1, 3584, 1024); tmin = f32v(R1, 3584, 512); osum = f32v(R1, 4608, 512); rstd = f32v(R1, 5120, 512)
    MT0 = b16v(R1, 5632, 512)
    AA = [b16v(R1, 5888, 512), b16v(R1, 6144, 512)]
    AT = [b16v(R1, 6400, 512), b16v(R1, 6656, 512)]
    Pb = b16v(R1, 6912, 512); Plo = b16v(R1, 7168, 512); Rb = b16v(R1, 7424, 512); PbT = b16v(R1, 7680, 512)
    TTb = b16v(R1, 7936, 512)
    TTbg = b16v(W0, 0, 512); qkT = b16v(W0, 256, 512)
    wTs = [b16v(W0, 512, 512), b16v(W0, 768, 512)]
    qdTs = [b16v(W0, 1024, 512), b16v(W0, 1280, 512)]
    sqo = b16v(W0, 1536, 512)
    gcsL = [f32v(W0, 1792 + 64 * i, 8) for i in range(2)]; egcL = [f32v(W0, 1800 + 64 * i, 8) for i in range(2)]
    ejL = [f32v(W0, 1808 + 64 * i, 8) for i in range(2)]; bgL = [f32v(W0, 1816 + 64 * i, 8) for i in range(2)]
    edecL = [f32v(W0, 1824 + 64 * i, 16) for i in range(2)]
    nbm = gB
    kTM = b16v(W1, 0, 1024); vTM = b16v(W1, 512, 1024); vn = b16v(W1, 1024, 1024); vns = b16v(W1, 1536, 1024)
    Sts = [R3[:, 0:1024], R3[:, 1032:2056]]
    Sbs = [R3[:, 2056:2568].bitcast(BF16), R3[:, 2568:3080].bitcast(BF16)]
    HR = [slice(0, 64), slice(64, 128)]

    def deltanet(l):
        for f in range(16):
            for half in range(2):
                sq = sqs[half]
                pb = 5 + half
                V(lambda e, f=f, half=half, sq=sq: e.tensor_tensor(sq[:], bigv[:, f, hs(half)], bigv[:, f, hs(half)], ALU.mult),
                  ["big%d" % f], ["sq%d" % half])
                T(lambda e, sq=sq, pb=pb: e.matmul(P[pb][:], onesb[:], sq[:], start=True, stop=True),
                  ["sq%d" % half, "onesb"], ["P%d" % pb])
                if f < 8:
                    A(lambda e, half=half, pb=pb: e.activation(rs[:, hs(half)], P[pb][:], AF.Ln, bias=epsc[:, 1:2], scale=128.0),
                      ["P%d" % pb, "epsc"], ["rs"])
                else:
                    A(lambda e, half=half, pb=pb: e.activation(rs[:, hs(half)], P[pb][:], AF.Ln, bias=epsc[:, 0:1], scale=1.0),
                      ["P%d" % pb, "epsc"], ["rs"])
                A(lambda e, half=half: e.activation(rs[:, hs(half)], rs[:, hs(half)], AF.Exp, scale=-0.5), ["rs"], ["rs"])
            V(lambda e, f=f: e.tensor_tensor(bigv[:, f, :], bigv[:, f, :], rs, ALU.mult), ["big%d" % f, "rs"], ["big%d" % f])

        gTMv = gTM[:].rearrange("p (c g) -> p c g", g=8)
        bTMv = bTM[:].rearrange("p (c g) -> p c g", g=8)
        qk_keys = ["big%d" % i for i in range(16)]
        k_keys = ["big%d" % i for i in range(8, 16)]
        v_keys = ["big%d" % i for i in range(16, 24)]
        q_keys = ["big%d" % i for i in range(0, 8)]
        for d_ in range(2):
            S.dma("sync", Sts[d_], s0_d[l, d_], writes=["St%d" % d_])
            A(lambda e, d_=d_: e.activation(Sbs[d_], Sts[d_], AF.Copy), ["St%d" % d_, "rs"], ["Sb%d" % d_, "rs"])

        def dn_gates(s_):
            pr = s_ % 2
            gcs, egc, ej, bg, edec = gcsL[pr], egcL[pr], ejL[pr], bgL[pr], edecL[pr]
            kp = "_%d" % pr
            g8 = gTMv[:, s_, :]
            b8 = bTMv[:, s_, :]
            gTri3 = h3(gTri)
            V(lambda e: e.tensor_tensor(gTri3, bc(g8, [128, 8, 64], 2), bc(tri2, [128, 8, 64], 1), ALU.mult), ["gTM", "cst"], ["gTri"])
            for hf in range(2):
                r = HR[hf]
                tpd = (64 * hf, 64 * hf)
                T(lambda e, r=r, tpd=tpd: e.matmul(P[3][r, 0:8], tri2[r, :], g8[r, :], start=True, stop=True, tile_position=tpd),
                  ["cst", "gTM"], ["P3"])
                T(lambda e, r=r, hf=hf: e.matmul(P[3][:, 8 + 8 * hf:16 + 8 * hf], onesf[r, :], g8[r, :], start=True, stop=True,
                                                 tile_position=(64 * hf, 0)), ["onesf", "gTM"], ["P3"])
                T(lambda e, r=r, tpd=tpd: e.matmul(P[3][r, 24:32], onesf[r, 0:64], g8[r, :], start=True, stop=True, tile_position=tpd),
                  ["onesf", "gTM"], ["P3"])
            for hf in range(2):
                r = HR[hf]
                pg = 2 if hf == 0 else 6
                T(lambda e, r=r, hf=hf, pg=pg: e.matmul(P[pg][:, :], onesf[r, :], gTri[r, :], start=True, stop=True,
                                                        tile_position=(64 * hf, 0)), ["onesf", "gTri"], ["P%d" % pg])
            G(lambda e: e.tensor_tensor(h3(nbm), bc(maskS2, [128, 8, 64], 1), bc(b8, [128, 8, 64], 2), ALU.mult),
              ["cst", "bTM"], ["nbm"])
            A(lambda e: e.activation(gcs, P[3][:, 0:8], AF.Copy), ["P3"], ["gcs" + kp])
            for hf in range(2):
                r = HR[hf]
                pg = 2 if hf == 0 else 6
                V(lambda e, r=r, pg=pg: e.tensor_tensor(h3(tmin)[r], bc(gcs[r, :], [64, 8, 64], 2), h3(P[pg][r, :]), ALU.subtract),
                  ["gcs" + kp, "P%d" % pg], ["tmin"])
                V(lambda e, r=r, pg=pg: e.tensor_tensor(h3(osum)[r], h3(P[pg][r, :]), bc(gcs[r, :], [64, 8, 64], 2), ALU.subtract),
                  ["gcs" + kp, "P%d" % pg], ["osum"])
            V(lambda e: e.tensor_scalar(tmin, tmin, 0.0, None, ALU.min), ["tmin"], ["tmin"])
            A(lambda e: e.activation(Em, tmin, AF.Exp), ["tmin"], ["Em"])
            V(lambda e: e.tensor_scalar(osum, osum, 0.0, None, ALU.min), ["osum"], ["osum"])
            A(lambda e: e.activation(ETm, osum, AF.Exp), ["osum"], ["ETm"])
            A(lambda e: e.activation(EGs[0], P[2][:, :], AF.Exp), ["P2"], ["EG0"])
            A(lambda e: e.activation(EGs[1], P[6][:, :], AF.Exp), ["P6"], ["EG1"])
            A(lambda e: e.activation(egc, P[3][:, 0:8], AF.Exp), ["P3"], ["egc" + kp])
            A(lambda e: e.activation(edec, P[3][:, 8:24], AF.Exp), ["P3"], ["edec" + kp])
            if s_ % 4 == 0 and s_ > 0:
                V(lambda e: e.tensor_scalar(edec, edec, flg[:, 1:2], None, ALU.mult), ["edec" + kp, "flg"], ["edec" + kp])
            V(lambda e: e.tensor_tensor(ej, P[3][:, 24:32], gcs, ALU.subtract), ["P3", "gcs" + kp], ["ej" + kp])
            A(lambda e: e.activation(ej, ej, AF.Exp), ["ej" + kp], ["ej" + kp])
            V(lambda e: e.tensor_tensor(bg, b8, egc, ALU.mult), ["bTM", "egc" + kp], ["bg" + kp])

        def dn_step(s_):
            toks = [slice(s_ * C, (s_ + 1) * C), slice((NCH - 1 - s_) * C, (NCH - s_) * C)]
            g8 = gTMv[:, s_, :]
            b8 = bTMv[:, s_, :]
            pr = s_ % 2
            gcs, egc, ej, bg, edec = gcsL[pr], egcL[pr], ejL[pr], bgL[pr], edecL[pr]
            kp = "_%d" % pr
            Em3, ETm3, Pm3 = h3(Em), h3(ETm), h3(Pm)
            for hf in range(2):
                r = HR[hf]; tok = toks[hf]
                for h in range(H):
                    T(lambda e, h=h, r=r, tok=tok, hf=hf: e.matmul(P[4][r, h * 64:(h + 1) * 64], bigv[:, 8 + h, tok], bigv[:, 8 + h, tok],
                                                                  start=True, stop=True, tile_position=(0, 64 * hf)), k_keys, ["P4"])
            V(lambda e: e.scalar_tensor_tensor(Em, Em, 1.0, nbm, ALU.min, ALU.mult), ["Em", "nbm"], ["Em"])
            V(lambda e: e.scalar_tensor_tensor(MT0, P[4][:, :], -1.0, Em, ALU.mult, ALU.mult), ["P4", "Em"], ["MT0"])

            def mm_hh(out_bank, lhs, rhs, rkeys, wkey, rhs2=None):
                for h in range(H):
                    for hf in range(2):
                        r = HR[hf]
                        T(lambda e, h=h, r=r, hf=hf: e.matmul(out_bank[r, h * 64:(h + 1) * 64], h3(lhs)[r, h, :], h3(rhs)[r, h, :],
                                                             start=True, stop=(rhs2 is None), tile_position=(64 * hf, 64 * hf)), rkeys, [wkey])
                        if rhs2 is not None:
                            T(lambda e, h=h, r=r, hf=hf: e.matmul(out_bank[r, h * 64:(h + 1) * 64], h3(lhs)[r, h, :], h3(rhs2)[r, h, :],
                                                                 start=False, stop=True, tile_position=(64 * hf, 64 * hf)), rkeys, [wkey])

            def tr_hh(src, skey):
                for h in range(H):
                    for hf in range(2):
                        r = HR[hf]
                        T(lambda e, h=h, r=r, hf=hf: e.transpose(PT[r, h * 64:(h + 1) * 64], h3(src)[r, h, :], identb[r, r],
                                                                tile_position=(64 * hf, 64 * hf)), [skey, "identb"], ["PT"])

            tr_hh(MT0, "MT0")
            A(lambda e: e.activation(AA[0], PT[:, 0:512], AF.Copy), ["PT"], ["AA0"])
            V(lambda e: e.tensor_tensor(h3(Pb), h3(AA[0]), bc(ident2, [128, 8, 64], 1), ALU.add), ["AA0", "cst"], ["Pb"])
            def kv_tr(base, keys, dst, dkey, eng):
                for hf in range(2):
                    r = HR[hf]; tok = toks[hf]
                    for h in range(H):
                        T(lambda e, h=h, r=r, tok=tok, hf=hf: e.transpose(PT[r, h * 128:(h + 1) * 128], bigv[:, base + h, tok],
                                                                         identb[:, :], tile_position=(0, 64 * hf)),
                          keys + ["identb"], ["PT"])

            def kv_cp(dst, dkey, eng):
                if eng == "scalar":
                    A(lambda e: e.activation(dst, PT[:, :], AF.Copy), ["PT"], [dkey])
                else:
                    V(lambda e: e.tensor_copy(dst, PT[:, :]), ["PT"], [dkey])

            def qk_all():
                for hf in range(2):
                    r = HR[hf]; tok = toks[hf]
                    for h in range(H):
                        T(lambda e, h=h, r=r, tok=tok, hf=hf: e.matmul(P[2][r, h * 64:(h + 1) * 64], bigv[:, 8 + h, tok], bigv[:, h, tok],
                                                                      start=True, stop=True, tile_position=(0, 64 * hf)), qk_keys, ["P2"])
                V(lambda e: e.scalar_tensor_tensor(ETm3, ETm3, 1.0, bc(tri2, [128, 8, 64], 1), ALU.min, ALU.mult), ["ETm", "cst"], ["ETm"])
                V(lambda e: e.tensor_tensor(qkT, P[2][:, :], ETm, ALU.mult), ["P2", "ETm"], ["qkT"])
                V(lambda e: e.tensor_tensor(h3(qdTs[0]), bigv[:, 0:8, toks[0]], h3(EGs[0]), ALU.mult), q_keys + ["EG0"], ["qdT0"])
                G(lambda e: e.tensor_tensor(h3(qdTs[1]), bigv[:, 0:8, toks[1]], h3(EGs[1]), ALU.mult), q_keys + ["EG1"], ["qdT1"])

            def prod(atn, atnk):
                mm_hh(P[4], atn, Pb, [atnk, "Pb"], "P4")
                V(lambda e: e.tensor_tensor(Pb, Pb, P[4][:, :], ALU.add), ["Pb", "P4"], ["Pb"])

            cur = 0
            pending = None
            for k in range(2, NLEV + 1):
                nxt = 1 - cur
                atc, atk = (MT0, "MT0") if k == 2 else (AT[cur], "AT%d" % cur)
                mm_hh(P[5], AA[cur], atc, ["AA%d" % cur, atk], "P5")
                if k < NLEV:
                    mm_hh(P[6], atc, AA[cur], ["AA%d" % cur, atk], "P6")
                if k == 2:
                    kv_tr(8, k_keys, kTM, "kTM", "scalar")
                elif k == 3:
                    kv_tr(16, v_keys, vTM, "vTM", "vector")
                A(lambda e, nxt=nxt: e.activation(AT[nxt], P[5][:, :], AF.Copy), ["P5"], ["AT%d" % nxt])
                if k < NLEV:
                    V(lambda e, nxt=nxt: e.tensor_copy(AA[nxt], P[6][:, :]), ["P6"], ["AA%d" % nxt])
                if k == 2:
                    kv_cp(kTM, "kTM", "scalar")
                elif k == 3:
                    kv_cp(vTM, "vTM", "vector")
                if pending is not None:
                    prod(*pending)
                if k == 4:
                    qk_all()
                if k == NLEV and s_ + 1 < NCH:
                    dn_gates(s_ + 1)
                pending = (AT[nxt], "AT%d" % nxt)
                cur = nxt
            prod(*pending)
            mm_hh(P[4], MT0, Pb, ["MT0", "Pb"], "P4")
            V(lambda e: e.tensor_tensor(tmin, P[4][:, :], Pb, ALU.subtract), ["P4", "Pb"], ["tmin"])
            V(lambda e: e.tensor_tensor(h3(Rb), h3(tmin), bc(ident2, [128, 8, 64], 1), ALU.add), ["tmin", "cst"], ["Rb"])
            tr_hh(Pb, "Pb")
            A(lambda e: e.activation(PbT, PT[:, 0:512], AF.Copy), ["PT"], ["PbT"])
            mm_hh(P[5], PbT, Rb, ["PbT", "Rb"], "P5")
            V(lambda e: e.tensor_tensor(Pm, Pb, P[5][:, :], ALU.add), ["Pb", "P5"], ["Pm"])
            V(lambda e: e.tensor_tensor(h3(TTb), Pm3, bc(b8, [128, 8, 64], 2), ALU.mult), ["Pm", "bTM"], ["TTb"])
            V(lambda e: e.tensor_tensor(h3(TTbg), Pm3, bc(bg, [128, 8, 64], 2), ALU.mult), ["Pm", "bg" + kp], ["TTbg"])
            for h in range(H):
                for hf in range(2):
                    r = HR[hf]
                    pp = 5 if h < 4 else 6
                    T(lambda e, h=h, pp=pp, r=r, hf=hf: e.matmul(P[pp][r, (h % 4) * 128:(h % 4 + 1) * 128], h3(TTb)[r, h, :],
                                                                h3(vTM)[r, h, :], start=True, stop=True,
                                                                tile_position=(64 * hf, 64 * hf)), ["TTb", "vTM"], ["P%d" % pp])
            A(lambda e: e.activation(u_[:, 0:512], P[5][:, :], AF.Copy), ["P5", "tmin"], ["u0", "tmin"])
            V(lambda e: e.tensor_copy(u_[:, 512:1024], P[6][:, :]), ["P6"], ["u1"])
            for h in range(H):
                for hf in range(2):
                    r = HR[hf]
                    T(lambda e, h=h, r=r, hf=hf: e.matmul(P[hf][:, h * 64:(h + 1) * 64], h3(kTM)[r, h, :], h3(TTbg)[r, h, :],
                                                         start=True, stop=True, tile_position=(64 * hf, 0)),
                      ["kTM", "TTbg"], ["P%d" % hf])
            A(lambda e: e.activation(wTs[0], P[0][:, :], AF.Copy), ["P0"], ["wT0"])
            V(lambda e: e.tensor_copy(wTs[1], P[1][:, :]), ["P1"], ["wT1"])
            for hf in range(2):
                r = HR[hf]
                for h in range(H):
                    pp = 5 if h < 4 else 6
                    T(lambda e, h=h, pp=pp, r=r, hf=hf: e.matmul(P[pp][r, (h % 4) * 128:(h % 4 + 1) * 128], h3(wTs[hf])[:, h, :],
                                                                Sbs[hf][:, h * 128:(h + 1) * 128], start=True, stop=True,
                                                                tile_position=(0, 64 * hf)), ["wT%d" % hf, "Sb%d" % hf], ["P%d" % pp])
            V(lambda e: e.tensor_tensor(vn[:, 0:512], u_[:, 0:512], P[5][:, :], ALU.subtract), ["u0", "P5"], ["vn0"])
            V(lambda e: e.tensor_tensor(vn[:, 512:1024], u_[:, 512:1024], P[6][:, :], ALU.subtract), ["u1", "P6"], ["vn1"])
            V(lambda e: e.tensor_tensor(h3(vns), h3(vn), bc(ej, [128, 8, 128], 2), ALU.mult), ["vn0", "vn1", "ej" + kp], ["vns"])
            for hf in range(2):
                r = HR[hf]
                po = 2 + hf
                for h in range(H):
                    T(lambda e, h=h, hf=hf, po=po: e.matmul(P[po][:, h * 64:(h + 1) * 64], Sbs[hf][:, h * 128:(h + 1) * 128],
                                                           h3(qdTs[hf])[:, h, :], start=True, stop=False),
                      ["Sb%d" % hf, "qdT%d" % hf], ["P%d" % po])
                    T(lambda e, h=h, hf=hf, po=po, r=r: e.matmul(P[po][:, h * 64:(h + 1) * 64], h3(vn)[r, h, :], h3(qkT)[r, h, :],
                                                                start=False, stop=True, tile_position=(64 * hf, 0)),
                      ["vn0", "vn1", "qkT"], ["P%d" % po])
            for hf in range(2):
                po = 2 + hf; tok = toks[hf]
                if s_ < NCH // 2:
                    if hf == 0:
                        A(lambda e, po=po, tok=tok: e.activation(ofv[:, :, tok], h3(P[po][:, :]), AF.Copy), ["P%d" % po], ["of"])
                    else:
                        V(lambda e, po=po, tok=tok: e.tensor_copy(ofv[:, :, tok], h3(P[po][:, :])), ["P%d" % po], ["of"])
                else:
                    V(lambda e, po=po, tok=tok: e.tensor_tensor(h3(osum), h3(P[po][:, :]), ofv[:, :, tok], ALU.add),
                      ["P%d" % po, "of"], ["osum"])
                    A(lambda e: e.activation(sqo, osum, AF.Square), ["osum"], ["sqo"])
                    T(lambda e: e.matmul(P[4][:, :], onesb[:], sqo, start=True, stop=True), ["sqo", "onesb"], ["P4"])
                    A(lambda e: e.activation(rstd, P[4][:, :], AF.Ln, bias=epsc[:, 0:1], scale=1.0 / 128), ["P4", "epsc"], ["rstd"])
                    A(lambda e: e.activation(rstd, rstd, AF.Exp, scale=-0.5), ["rstd"], ["rstd"])
                    V(lambda e, tok=tok: e.scalar_tensor_tensor(ofv[:, :, tok], h3(osum), par["hng"][:, 0:1], h3(rstd),
                                                                ALU.mult, ALU.mult), ["osum", "rstd", "p_hng", "of"], ["of"])
            for hf in range(2):
                r = HR[hf]
                banks = (5, 6) if hf == 0 else (0, 1)
                for h in range(H):
                    pp = banks[0] if h < 4 else banks[1]
                    T(lambda e, h=h, pp=pp, r=r, hf=hf: e.matmul(P[pp][:, (h % 4) * 128:(h % 4 + 1) * 128], h3(kTM)[r, h, :],
                                                                h3(vns)[r, h, :], start=True, stop=True, tile_position=(64 * hf, 0)),
                      ["kTM", "vns"], ["P%d" % pp])
                St = Sts[hf]; sk = "St%d" % hf
                V(lambda e, St=St, hf=hf: e.tensor_tensor(h3(St), h3(St), bc(edec[:, 8 * hf:8 * hf + 8], [128, 8, 128], 2), ALU.mult),
                  [sk, "edec" + kp], [sk])
                V(lambda e, St=St, b0=banks[0]: e.tensor_tensor(St[:, 0:512], St[:, 0:512], P[b0][:, :], ALU.add), [sk, "P%d" % banks[0]], [sk])
                V(lambda e, St=St, b1=banks[1]: e.tensor_tensor(St[:, 512:1024], St[:, 512:1024], P[b1][:, :], ALU.add), [sk, "P%d" % banks[1]], [sk])
                if s_ % 4 == 3:
                    seq = (s_ // 4) if hf == 0 else ((NCH - 1 - s_) // 4)
                    S.dma("sync", st_d[l, seq, hf].rearrange("h k v -> k h v"), h3(St), reads=[sk],
                          writes=["st_out"], semkey="d_Sout%d" % hf, out_final=True)
                    V(lambda e, St=St, hf=hf: e.tensor_scalar(Sbs[hf], St, flg[:, 1:2], None, ALU.mult), [sk, "flg"], ["Sb%d" % hf])
                else:
                    A(lambda e, St=St, hf=hf: e.activation(Sbs[hf], St, AF.Copy), [sk], ["Sb%d" % hf])

        dn_gates(0)
        for s_ in range(NCH):
            dn_step(s_)

    accs = [big[:, 0:2048].bitcast(F32), big[:, 2048:4096].bitcast(F32)]
    acc_keys = [["big0", "big1"], ["big2", "big3"]]
    mean = big[:, 4096:6144].bitcast(F32); mean_keys = ["big4", "big5"]
    lrs = big[:, 6144:8192].bitcast(F32); lrs_keys = ["big6", "big7"]
    cttmp = big[:, 8192:10240].bitcast(F32); cttmp_keys = ["big8", "big9"]

    NDG = 6
    dgs = [R3[:, 64 * i:64 * (i + 1)].bitcast(BF16) for i in range(NDG)]
    dgctr = [0]

    def conv_tile(l, ct):
        acc = accs[ct % 2]; ak = acc_keys[ct % 2]
        ua = bigv[:, 16 + ct, :]; uk = "big%d" % (16 + ct)
        caw = par["caw"]
        banks = [0, 1] if ct % 2 == 0 else [2, 0]
        taps = []
        wmain, wk = (caw, "p_caw") if ct < 4 else (cawP, "cawP")
        order = [0] + [d for d in range(-15, 16) if d != 0]
        for d in order:
            j = ct * 31 + 15 + d
            if d == 0:
                taps.append((caw, "p_caw", j, "seg", d))
            else:
                taps.append((wmain, wk, j, "seg", d))
        if ct >= 4:
            for d in range(-15, 16):
                if d != 0:
                    taps.append((cawS, "cawS", ct * 31 + 15 + d, "vert", d))
        specs = {0: [], 1: []}
        for ti, (wt_, wk_, j, kind, d) in enumerate(taps):
            for hf in range(2):
                base = 512 * hf
                if kind == "seg":
                    e_ = abs(d)
                    if d >= 0:
                        o = (0, 256 - d); i_ = (d, 256)
                    else:
                        o = (e_, 256); i_ = (0, 256 - e_)
                    specs[hf].append((ti, "seg", o, i_))
                else:
                    if d > 0:
                        lo, hi = base, min(base + 512, NT - 64 * d); sh = 64 * d
                    else:
                        lo, hi = max(base, -64 * d), base + 512; sh = 64 * d
                    if hi > lo:
                        specs[hf].append((ti, "vert", (lo - base, hi - base), (lo + sh, hi + sh)))
        last = {hf: specs[hf][-1][0] for hf in range(2)}
        by_tap = {}
        for hf in range(2):
            for sp in specs[hf]:
                by_tap.setdefault(sp[0], []).append((hf, sp))
        for ti, (wt_, wk_, j, kind, d) in enumerate(taps):
            di = dgctr[0] % NDG
            dgctr[0] += 1
            dg = dgs[di]; dk = "dg%d" % di
            A(lambda e, dg=dg, wt_=wt_, j=j: e.activation(dg, identb[:, :], AF.Identity, bias=0.0, scale=wt_[:, j:j + 1]),
              ["identb", wk_], [dk])
            for hf, sp in by_tap.get(ti, []):
                pb = banks[hf]
                pv = P[pb][:, :].rearrange("p (s t) -> p s t", s=2)
                xv = ua[:, hs(hf)].rearrange("p (s t) -> p s t", s=2)
                first = (ti == 0)
                lastf = (ti == last[hf]) and (sp is [x for x in specs[hf] if x[0] == ti][-1])
                if sp[1] == "seg":
                    (o0, o1), (i0_, i1_) = sp[2], sp[3]
                    T(lambda e, dg=dg, pv=pv, xv=xv, o0=o0, o1=o1, i0_=i0_, i1_=i1_, first=first, lastf=lastf: e.matmul(
                        pv[:, :, o0:o1], dg, xv[:, :, i0_:i1_], start=first, stop=lastf), [dk, uk], ["P%d" % pb])
                else:
                    (o0, o1), (i0_, i1_) = sp[2], sp[3]
                    T(lambda e, dg=dg, pb=pb, o0=o0, o1=o1, i0_=i0_, i1_=i1_, lastf=lastf: e.matmul(
                        P[pb][:, o0:o1], dg, ua[:, i0_:i1_], start=False, stop=lastf), [dk, uk], ["P%d" % pb])
        for hf in range(2):
            pb = banks[hf]
            A(lambda e, pb=pb, hf=hf: e.activation(acc[:, hs(hf)], P[pb][:, :], AF.Identity, bias=par["cab"][:, ct:ct + 1], scale=1.0),
              ["P%d" % pb, "p_cab"], ak)
        if ct < 4:
            a4 = acc.rearrange("p (b s t) -> p b s t", b=4, s=4); x4 = ua.rearrange("p (b s t) -> p b s t", b=4, s=4)

            def tap(o_ap, i_ap, w_ap, wkey):
                V(lambda e: e.scalar_tensor_tensor(o_ap, i_ap, w_ap, o_ap, ALU.mult, ALU.add), [uk, wkey] + ak, ak)

            for d in list(range(-15, 0)) + list(range(1, 16)):
                j = ct * 31 + 15 + d
                e_ = abs(d)
                for sg in range(3):
                    if d > 0:
                        tap(a4[:, :, sg, 64 - d:64], x4[:, :, sg + 1, 0:d], cawNS[:, j:j + 1], "cawNS")
                    else:
                        tap(a4[:, :, sg + 1, 0:e_], x4[:, :, sg, 64 - e_:64], cawNS[:, j:j + 1], "cawNS")
        A(lambda e: e.activation(ua, acc, AF.Copy), ak, [uk])
        for half in range(2):
            A(lambda e, half=half: e.activation(sqs[half][:], acc[:, hs(half)], AF.Square), ak, ["sq%d" % half])
            T(lambda e, half=half: e.matmul(P[4 + half][:], onesb[:], ua[:, hs(half)], start=(ct == 0), stop=(ct == 7)),
              [uk, "onesb"], ["P%d" % (4 + half)])
            pq = 6 if half == 0 else 3
            T(lambda e, half=half, pq=pq: e.matmul(P[pq][:], onesb[:], sqs[half][:], start=(ct == 0), stop=(ct == 7)),
              ["sq%d" % half, "onesb"], ["P%d" % pq])

    def rest_of_layer(l, compute_stats_rs, hn_rhs, hn_keys, mod):
        def zb_cons(ft, half, ps, pk):
            A(lambda e: e.activation(sqs[half][:], ps, AF.Silu), [pk], ["sq%d" % half])
            V(lambda e: e.tensor_tensor(ofv[:, ft, hs(half)], ofv[:, ft, hs(half)], sqs[half][:], ALU.mult),
              ["of", "sq%d" % half], ["of"])
        wpool[0] = [0, 1, 2, 3]
        proj(w_in_d[l], COL["z_b"], 1024, KT, hn_rhs, hn_keys, zb_cons)

        def glu_cons(ft, half, ps, pk):
            A(lambda e: e.activation(bigv[:, 16 + ft, hs(half)], ps, AF.Sigmoid), [pk], ["big%d" % (16 + ft)])
        proj(w_in_d[l], COL["a_glu"], 1024, KT, hn_rhs, hn_keys, glu_cons)

        def val_cons(ft, half, ps, pk):
            V(lambda e: e.tensor_tensor(bigv[:, 16 + ft, hs(half)], bigv[:, 16 + ft, hs(half)], ps, ALU.mult),
              [pk, "big%d" % (16 + ft)], ["big%d" % (16 + ft)])
        proj(w_in_d[l], COL["a_val"], 1024, KT, hn_rhs, hn_keys, val_cons)
        wpool[0] = None

        for ct in range(8):
            conv_tile(l, ct)
            if l + 1 < n_layers:
                for gg in range(3 * ct, 3 * ct + 3):
                    if gg % 5 < 2:
                        mod_group(l + 1, gg)
                    else:
                        mod_group0(gg, l + 1, bi=gg % 5 - 1)
        if l + 1 < n_layers:
            mod_finish(l + 1, 0)
            mod_finish(l + 1, 1)
        for half in range(2):
            pq = 6 if half == 0 else 3
            A(lambda e, half=half: e.activation(mean[:, hs(half)], P[4 + half][:], AF.Identity, bias=0.0, scale=1.0 / 1024),
              ["P%d" % (4 + half)], mean_keys)
            A(lambda e, half=half: e.activation(tmpB[:], P[4 + half][:], AF.Square, scale=1.0 / 1024),
              ["P%d" % (4 + half)], ["tmpB"])
            V(lambda e, pq=pq: e.scalar_tensor_tensor(tmpB[:], P[pq][:], 1.0 / 1024, tmpB[:], ALU.mult, ALU.subtract),
              ["P%d" % pq, "tmpB"], ["tmpB"])
            A(lambda e, half=half: e.activation(lrs[:, hs(half)], tmpB[:], AF.Ln, bias=epsc[:, 0:1], scale=1.0),
              ["tmpB", "epsc"], lrs_keys)
            A(lambda e, half=half: e.activation(lrs[:, hs(half)], lrs[:, hs(half)], AF.Exp, scale=-0.5), lrs_keys, lrs_keys)
        for ct in range(8):
            uk = "big%d" % (16 + ct)
            V(lambda e, ct=ct: e.tensor_tensor(cttmp, bigv[:, 16 + ct, :], mean, ALU.subtract), [uk] + mean_keys, cttmp_keys)
            V(lambda e: e.tensor_tensor(cttmp, cttmp, lrs, ALU.mult), cttmp_keys + lrs_keys, cttmp_keys)
            A(lambda e, ct=ct: e.activation(bigv[:, 16 + ct, :], cttmp, AF.Silu, bias=par["lnb"][:, ct:ct + 1],
                                            scale=par["lng"][:, ct:ct + 1]), cttmp_keys + ["p_lnb", "p_lng"], [uk])

        def az_cons(ft, half, ps, pk):
            A(lambda e: e.activation(sqs[half][:], ps, AF.Silu), [pk], ["sq%d" % half])
            V(lambda e: e.tensor_tensor(bigv[:, 16 + ft, hs(half)], bigv[:, 16 + ft, hs(half)], sqs[half][:], ALU.mult),
              ["big%d" % (16 + ft), "sq%d" % half], ["big%d" % (16 + ft)])
        proj(w_in_d[l], COL["a_z"], 1024, KT, hn_rhs, hn_keys, az_cons)

        sgbv = sgb.rearrange("p (f t) -> p f t", f=2)
        pa_rhs = lambda kt, half: bigv[:, 16 + kt, hs(half)]
        pa_keys = lambda kt: ["big%d" % (16 + kt)]
        pb_rhs = lambda kt, half: ofv[:, kt, hs(half)]
        pb_keys = lambda kt: ["of"]
        for g in range(8):
            def ga_cons(ft, half, ps, pk, g=g):
                A(lambda e: e.activation(bigv[:, 2 * g + ft, hs(half)], ps, AF.Sigmoid), [pk], ["big%d" % (2 * g + ft)])
            proj(w_in_d[l], COL["gate_a"] + g * 256, 256, KT, hn_rhs, hn_keys, ga_cons)

            def oa_cons(ft, half, ps, pk, g=g):
                V(lambda e: e.tensor_tensor(bigv[:, 2 * g + ft, hs(half)], bigv[:, 2 * g + ft, hs(half)], ps, ALU.mult),
                  [pk, "big%d" % (2 * g + ft)], ["big%d" % (2 * g + ft)])
            proj(w_pa_d[l], g * 256, 256, 8, pa_rhs, pa_keys, oa_cons)

            def gb_cons(ft, half, ps, pk, g=g):
                A(lambda e: e.activation(sgbv[:, ft, hs(half)], ps, AF.Sigmoid), [pk], ["R3a"])
            proj(w_in_d[l], COL["gate_b"] + g * 256, 256, KT, hn_rhs, hn_keys, gb_cons)

            def ob_cons(ft, half, ps, pk, g=g):
                V(lambda e: e.tensor_tensor(tmpB[:], sgbv[:, ft, hs(half)], ps, ALU.mult), [pk, "R3a"], ["tmpB"])
                V(lambda e: e.tensor_tensor(bigv[:, 2 * g + ft, hs(half)], bigv[:, 2 * g + ft, hs(half)], tmpB[:], ALU.add),
                  ["tmpB", "big%d" % (2 * g + ft)], ["big%d" % (2 * g + ft)])
            proj(w_pb_d[l], g * 256, 256, 8, pb_rhs, pb_keys, ob_cons)

        def wo_cons(ft, half, ps, pk):
            V(lambda e: e.scalar_tensor_tensor(xTv[:, ft, hs(half)], ps, mod[:, 32 + ft:33 + ft], xTv[:, ft, hs(half)],
                                               ALU.mult, ALU.add), [pk, "modg%d" % l, "x%d" % ft], ["x%d" % ft])
        wpool[0] = [4, 5]
        proj(w_o_d[l], 0, D, KT, lambda kt, half: bigv[:, kt, hs(half)], lambda kt: ["big%d" % kt], wo_cons)
        wpool[0] = None

    def final_norm(compute_stats_rs):
        compute_stats_rs(1.0 / D, epsc[:, 0:1])
        bufs = [(tmpA, "tmpA"), (R3[:, 0:NT], "R3a")]
        for kt in range(KT):
            buf, bk = bufs[kt % 2]
            V(lambda e, kt=kt, buf=buf: e.scalar_tensor_tensor(buf, xTv[:, kt, :], fng[:, kt:kt + 1], rs, ALU.mult, ALU.mult),
              ["x%d" % kt, "fng", "rs"], [bk])
            S.dma("sync", yT_d[kt * 128:(kt + 1) * 128, :], buf, reads=[bk], writes=["yT%d" % kt], semkey="d_y" + bk,
                  out_final=True)

    for l in range(n_layers):
        grp = "par%d" % (l + 1)
        for nm, d_ap in (("caw", caw_d), ("cab", cab_d), ("lng", lng_d),
                         ("lnb", lnb_d), ("cqw", cqw_d), ("hng", hng_d)):
            S.dma("sync", par[nm][:], d_ap[l], writes=["p_" + nm], group=grp)
        S.dma("sync", alog[:], alog_d[l], writes=["alog"], group=grp)
        S.dma("sync", dtb[:], dtb_d[l], writes=["dtb"], group=grp)
        S.dma("gpsimd", wg[:].rearrange("p (k c) -> p k c", k=KT),
              w_in_d[l].rearrange("(k p) c -> p k c", p=128)[:, :, COL["beta"]:COL["beta"] + 32],
              writes=["wg"], semkey="d_wg")
        caw = par["caw"]
        V(lambda e: e.tensor_scalar(cawP[:], caw[:], flg[:, 0:1], None, ALU.mult), ["p_caw", "flg"], ["cawP"])
        V(lambda e: e.tensor_scalar(cawS[:], caw[:], flg[:, 1:2], None, ALU.mult), ["p_caw", "flg"], ["cawS"])
        V(lambda e: e.tensor_scalar(cawNS[:], caw[:], flg[:, 3:4], None, ALU.mult), ["p_caw", "flg"], ["cawNS"])
        cq3 = par["cqw"][:].rearrange("p (f j) -> p f j", j=3)
        V(lambda e: e.tensor_scalar(cqwL[:], cq3[:, :, 0], flg[:, 2:3], None, ALU.mult), ["p_cqw", "flg"], ["cqwL"])
        V(lambda e: e.tensor_scalar(cqwR[:], cq3[:, :, 2], flg[:, 2:3], None, ALU.mult), ["p_cqw", "flg"], ["cqwR"])
        A(lambda e: e.activation(negA[:], alog[:], AF.Exp), ["alog"], ["negA"])
        V(lambda e: e.tensor_scalar(negA[:], negA[:], -1.0, None, ALU.mult), ["negA"], ["negA"])

        mod = mods[l]; modA = modAs[l]
        if l == 0:
            for g in range(24):
                mod_group0(g)
            mod_finish(0, 0)
            mod_finish(0, 1)

        def compute_stats_rs(scale, bias_ap):
            for half in range(2):
                for kt in range(KT):
                    sq = sqs[kt % 2]
                    A(lambda e, sq=sq, kt=kt, half=half: e.activation(sq[:], xTv[:, kt, hs(half)], AF.Square),
                      ["x%d" % kt], ["sq%d" % (kt % 2)])
                    T(lambda e, sq=sq, kt=kt: e.matmul(P[5][:], onesb[:], sq[:], start=(kt == 0), stop=(kt == KT - 1)),
                      ["sq%d" % (kt % 2), "onesb"], ["P5"])
                A(lambda e, half=half: e.activation(rs[:, hs(half)], P[5][:], AF.Ln, bias=bias_ap, scale=scale),
                  ["P5", "epsc"], ["rs"])
                A(lambda e, half=half: e.activation(rs[:, hs(half)], rs[:, hs(half)], AF.Exp, scale=-0.5), ["rs"], ["rs"])

        def compute_hn(mod=mod, modA=modA, l=l):
            compute_stats_rs(1.0 / D, epsc[:, 0:1])
            for kt in range(KT):
                V(lambda e, kt=kt: e.scalar_tensor_tensor(tmpA, xTv[:, kt, :], modA[:, kt:kt + 1], rs,
                                                          ALU.mult, ALU.mult),
                  ["x%d" % kt, "modA%d" % l, "rs"], ["tmpA"])
                A(lambda e, kt=kt: e.activation(hnT[:, kt, :], tmpA, AF.Identity, bias=mod[:, kt:kt + 1], scale=1.0),
                  ["tmpA", "mod%d" % l], ["hn%d" % kt])

        hn_rhs = lambda kt, half: hnT[:, kt, hs(half)]
        hn_keys = lambda kt: ["hn%d" % kt]
        compute_hn()

        def qkv_consumer(base_ft):
            def cons(ft, half, ps, pk):
                f = base_ft + ft
                A(lambda e: e.activation(pre[:, 1 + half * 512:1 + (half + 1) * 512], ps, AF.Copy), [pk], ["R3a"])
                if half == 1:
                    cw = par["cqw"]
                    V(lambda e: e.tensor_scalar(tmpA, pre[:, 1:NT + 1], cw[:, 3 * f + 1:3 * f + 2], None, ALU.mult),
                      ["R3a", "p_cqw"], ["tmpA"])
                    V(lambda e: e.scalar_tensor_tensor(tmpA, pre[:, 0:NT], cw[:, 3 * f:3 * f + 1], tmpA,
                                                       ALU.mult, ALU.add), ["R3a", "p_cqw", "tmpA"], ["tmpA"])
                    V(lambda e: e.scalar_tensor_tensor(tmpA, pre[:, 2:NT + 2], cw[:, 3 * f + 2:3 * f + 3], tmpA,
                                                       ALU.mult, ALU.add), ["R3a", "p_cqw", "tmpA"], ["tmpA"])
                    tv = tmpA.rearrange("p (s t) -> p s t", s=4)
                    pv = pre[:, 1:NT + 1].rearrange("p (s t) -> p s t", s=4)
                    V(lambda e: e.scalar_tensor_tensor(tv[:, 1:4, 0:1], pv[:, 0:3, 255:256], cqwL[:, f:f + 1],
                                                       tv[:, 1:4, 0:1], ALU.mult, ALU.add),
                      ["R3a", "cqwL", "tmpA"], ["tmpA"])
                    V(lambda e: e.scalar_tensor_tensor(tv[:, 0:3, 255:256], pv[:, 1:4, 0:1], cqwR[:, f:f + 1],
                                                       tv[:, 0:3, 255:256], ALU.mult, ALU.add),
                      ["R3a", "cqwR", "tmpA"], ["tmpA"])
                    A(lambda e: e.activation(bigv[:, f, :], tmpA, AF.Silu), ["tmpA"], ["big%d" % f])
            return cons

        V(lambda e: e.memset(pre[:, 0:1], 0.0), [], ["R3a"])
        V(lambda e: e.memset(pre[:, NT + 1:NT + 2], 0.0), ["R3a"], ["R3a"])
        proj(w_in_d[l], COL["q"], 3072, KT, hn_rhs, hn_keys, qkv_consumer(0))

        wg5 = wg[:].rearrange("p (k a d e) -> p k a d e", k=KT, a=2, d=2, e=8)
        for s_ in range(NCH):
            for hf in range(2):
                c = s_ if hf == 0 else NCH - 1 - s_
                for kt in range(KT):
                    T(lambda e, c=c, kt=kt, hf=hf, s_=s_: e.matmul(
                        P[4][hf * 64:hf * 64 + 64, s_ * 16:(s_ + 1) * 16].rearrange("p (a e) -> p a e", a=2),
                        hnT[:, kt, c * C:(c + 1) * C], wg5[:, kt, :, hf, :], start=(kt == 0), stop=(kt == KT - 1),
                        tile_position=(0, 64 * hf)), ["hn%d" % kt, "wg"], ["P4"])
        p4v = P[4][:, 0:NCH * 16].rearrange("p (s a e) -> p s a e", s=NCH, a=2)
        gTMv = gTM[:].rearrange("p (c g) -> p c g", g=8)
        bTMv = bTM[:].rearrange("p (c g) -> p c g", g=8)
        A(lambda e: e.activation(bTMv, p4v[:, :, 0, :], AF.Sigmoid), ["P4"], ["bTM"])
        V(lambda e: e.tensor_tensor(gTMv, p4v[:, :, 1, :], bc(dtb[:], [128, NCH, 8], 1), ALU.add), ["P4", "dtb"], ["gTM"])
        A(lambda e: e.activation(gTM[:], gTM[:], AF.Exp), ["gTM"], ["gTM"])
        S.op("scalar", lambda e: e.activation(gTM[:], gTM[:], AF.Ln, bias=epsc[:, 2:3], scale=1.0), ["gTM", "epsc"], ["gTM"], strict=True)
        V(lambda e: e.tensor_tensor(gTMv, gTMv, bc(negA[:], [128, NCH, 8], 1), ALU.mult), ["gTM", "negA"], ["gTM"])

        S.barrier()
        deltanet(l)
        S.barrier()
        compute_hn()
        rest_of_layer(l, compute_stats_rs, hn_rhs, hn_keys, mod)
    final_norm(compute_stats_rs)
    S.finish()
    return nc


_PROG = {}
_DEBUG = False
_LAST = {}


def _consts():
    c = np.zeros((128, 6 * 128), np.float32)
    c[:, 0:128] = np.eye(128, dtype=np.float32)
    r = np.arange(64)[:, None]; q = np.arange(64)[None, :]
    c[0:64, 128:192] = (r <= q); c[64:128, 128:192] = (r >= q)
    c[0:64, 192:256] = (r > q); c[64:128, 192:256] = (r < q)
    c[0:64, 256:320] = np.eye(64); c[64:128, 256:320] = np.eye(64)
    return c


def _fm(v, nt):
    return np.ascontiguousarray(np.asarray(v, np.float32).reshape(nt, 128).T)


def kernel(x_prompt, x_sample, state_delta, c, c_ctx, w_mod, b_mod, norm_g, w_in, conv_a_w, conv_a_b,
           ln_a_g, ln_a_b, w_pa, conv_qkv_w, a_log, dt_bias, head_norm_g, w_pb, w_o, final_norm_g):
    f32 = np.float32
    if "nc" not in _PROG:
        _PROG["nc"] = build_program(debug=_DEBUG)
    nc = _PROG["nc"]
    x_prompt = np.asarray(x_prompt, f32); x_sample = np.asarray(x_sample, f32)
    state_delta = np.asarray(state_delta, f32)
    shared = {
        "cst": _consts(),
        "w_mod": np.ascontiguousarray(np.asarray(w_mod, f32)),
        "b_mod": np.stack([_fm(b_mod[l], 48) for l in range(DEPTH)]),
        "norm_g": np.stack([_fm(norm_g[l], KT) for l in range(DEPTH)]),
        "w_in": np.ascontiguousarray(np.asarray(w_in, f32)),
        "caw": np.stack([np.ascontiguousarray(np.asarray(conv_a_w[l], f32).T.reshape(8, 128, 31).transpose(1, 0, 2)).reshape(128, 248)
                         for l in range(DEPTH)]),
        "cab": np.stack([_fm(conv_a_b[l], 8) for l in range(DEPTH)]),
        "lng": np.stack([_fm(ln_a_g[l], 8) for l in range(DEPTH)]),
        "lnb": np.stack([_fm(ln_a_b[l], 8) for l in range(DEPTH)]),
        "w_pa": np.ascontiguousarray(np.asarray(w_pa, f32)),
        "cqw": np.stack([np.ascontiguousarray(np.asarray(conv_qkv_w[l], f32).T.reshape(24, 128, 3).transpose(1, 0, 2)).reshape(128, 72)
                         for l in range(DEPTH)]),
        "alog": np.stack([np.repeat(np.asarray(a_log[l], f32).reshape(2, 8), 64, axis=0) for l in range(DEPTH)]),
        "dtb": np.stack([np.repeat(np.asarray(dt_bias[l], f32).reshape(2, 8), 64, axis=0) for l in range(DEPTH)]),
        "hng": np.asarray(head_norm_g, f32).reshape(DEPTH, 128, 1).copy(),
        "w_pb": np.ascontiguousarray(np.asarray(w_pb, f32)),
        "w_o": np.ascontiguousarray(np.asarray(w_o, f32)),
        "fng": _fm(final_norm_g, KT),
    }
    in_maps = []
    for core in range(8):
        m = dict(shared)
        if core < 4:
            xt = x_prompt[4 * core:4 * core + 4].reshape(NT, D)
            m["cv"] = _fm(c_ctx, KT)
            m["s0"] = np.zeros((DEPTH, 2, 128, H * 128), f32)
            m["flg"] = np.ascontiguousarray(np.broadcast_to(np.array([1, 0, -1, 0], f32), (128, 4)))
        else:
            b = core - 4
            xt = x_sample[b]
            m["cv"] = _fm(np.asarray(c, f32)[b], KT)
            m["s0"] = np.ascontiguousarray(state_delta[b].transpose(0, 1, 3, 2, 4)).reshape(DEPTH, 2, 128, H * 128)
            m["flg"] = np.ascontiguousarray(np.broadcast_to(np.array([0, 1, 0, -1], f32), (128, 4)))
        m["xT"] = np.ascontiguousarray(xt.T)
        in_maps.append(m)
    res = run_bass_kernel_spmd(nc, in_maps, core_ids=list(range(8)))
    r = res.results
    _LAST['r'] = r
    y_prompt = np.stack([r[i]["yT"].T.reshape(4, 256, D) for i in range(4)]).reshape(16, 256, D)
    y_sample = np.stack([r[4 + b]["yT"].T for b in range(4)])
    st = np.concatenate([r[i]["st"].transpose(1, 0, 2, 3, 4, 5) for i in range(4)], axis=0)
    return (np.ascontiguousarray(y_prompt, dtype=f32), np.ascontiguousarray(y_sample, dtype=f32),
            np.ascontiguousarray(st, dtype=f32))
```
